# Optimizing a Trainium2 kernel written in Bass

```python
import math
import jax, jax.numpy as jnp
from jax import lax
import numpy as np

D_MODEL = 1024
BATCH = 4
SEQ = 8192
DEPTH = 2

N_MIXERS = 2
N_S5_LAYERS = (DEPTH + 1) // 2
N_ATTN_LAYERS = DEPTH // 2

S5_GROUP = 16
S5_GROUPS = D_MODEL // S5_GROUP
S5_STATE = 64
DT_MIN = 1e-3
DT_MAX = 1e-1

HEAD_DIM = 64
N_Q_HEADS = D_MODEL // HEAD_DIM
N_KV_HEADS = 4
Q_PER_KV = N_Q_HEADS // N_KV_HEADS
WINDOW = 128
BLOCK = 128
ROPE_DIM = HEAD_DIM // 4
ROPE_THETA = 500000.0
QKV_DIM = (N_Q_HEADS + 2 * N_KV_HEADS) * HEAD_DIM
NEG_INF = -1e30

D_FF = 2816
CONV_WIDTH = 3

EPS = 1e-5

kernel_name = "hybrid_s5_swa_sink_convffn"


def rmsnorm(x, g):
    xf = x.astype(jnp.float32)
    y = xf * lax.rsqrt(jnp.mean(xf * xf, axis=-1, keepdims=True) + EPS)
    return (y * g.astype(jnp.float32)).astype(x.dtype)


def s5_mixer(u, lam_re, lam_im, log_dt, b_re, b_im, c_re, c_im, d_skip, w_glu, b_glu):
    f32 = jnp.float32
    bsz, L, d = u.shape
    lr, li = lam_re.astype(f32), lam_im.astype(f32)
    dt = jnp.exp(log_dt.astype(f32))[:, None]
    mag = jnp.exp(lr * dt)
    a_re = mag * jnp.cos(li * dt)
    a_im = mag * jnp.sin(li * dt)
    den = lr * lr + li * li
    z_re = ((a_re - 1.0) * lr + a_im * li) / den
    z_im = (a_im * lr - (a_re - 1.0) * li) / den
    br, bi = b_re.astype(f32), b_im.astype(f32)
    bb_re = z_re[..., None] * br - z_im[..., None] * bi
    bb_im = z_re[..., None] * bi + z_im[..., None] * br
    ug = u.astype(f32).reshape(bsz, L, S5_GROUPS, S5_GROUP)
    bu_re = jnp.einsum('blgc,gpc->lbgp', ug, bb_re)
    bu_im = jnp.einsum('blgc,gpc->lbgp', ug, bb_im)
    a_re_t = jnp.broadcast_to(a_re[None, None], (L, 1, S5_GROUPS, S5_STATE))
    a_im_t = jnp.broadcast_to(a_im[None, None], (L, 1, S5_GROUPS, S5_STATE))

    def combine(e1, e2):
        a1r, a1i, b1r, b1i = e1
        a2r, a2i, b2r, b2i = e2
        return (a2r * a1r - a2i * a1i,
                a2r * a1i + a2i * a1r,
                a2r * b1r - a2i * b1i + b2r,
                a2r * b1i + a2i * b1r + b2i)

    _, _, h_re, h_im = lax.associative_scan(combine, (a_re_t, a_im_t, bu_re, bu_im), axis=0)
    y = (jnp.einsum('lbgp,gcp->blgc', h_re, c_re.astype(f32))
         - jnp.einsum('lbgp,gcp->blgc', h_im, c_im.astype(f32)))
    y = y.reshape(bsz, L, d) + d_skip.astype(f32) * u.astype(f32)
    g = jax.nn.gelu(y).astype(u.dtype)
    return g * jax.nn.sigmoid(g @ w_glu + b_glu)


def rope_partial(t, pos):
    f32 = jnp.float32
    half = ROPE_DIM // 2
    inv_freq = 1.0 / jnp.power(ROPE_THETA, jnp.arange(0, ROPE_DIM, 2, dtype=f32) / ROPE_DIM)
    ang = pos.astype(f32)[..., None] * inv_freq
    cos = jnp.cos(ang)[:, :, None, :]
    sin = jnp.sin(ang)[:, :, None, :]
    tr = t[..., :ROPE_DIM].astype(f32)
    t1, t2 = tr[..., :half], tr[..., half:]
    rot = jnp.concatenate([t1 * cos - t2 * sin, t2 * cos + t1 * sin], axis=-1).astype(t.dtype)
    return jnp.concatenate([rot, t[..., ROPE_DIM:]], axis=-1)


def swa_mixer(h, pos, w_qkv, b_qkv, sinks, w_o, b_o):
    f32 = jnp.float32
    bsz, L, d = h.shape
    nb = L // BLOCK
    qkv = h @ w_qkv + b_qkv
    q, k, v = jnp.split(qkv, [N_Q_HEADS * HEAD_DIM, (N_Q_HEADS + N_KV_HEADS) * HEAD_DIM], axis=-1)
    q = rope_partial(q.reshape(bsz, L, N_Q_HEADS, HEAD_DIM), pos)
    k = rope_partial(k.reshape(bsz, L, N_KV_HEADS, HEAD_DIM), pos)
    v = v.reshape(bsz, L, N_KV_HEADS, HEAD_DIM)
    q = q.reshape(bsz, nb, BLOCK, N_KV_HEADS, Q_PER_KV, HEAD_DIM)
    k = k.reshape(bsz, nb, BLOCK, N_KV_HEADS, HEAD_DIM)
    v = v.reshape(bsz, nb, BLOCK, N_KV_HEADS, HEAD_DIM)

    def with_prev(t):
        prev = jnp.pad(t[:, :-1], ((0, 0), (1, 0), (0, 0), (0, 0), (0, 0)))
        return jnp.concatenate([prev, t], axis=2)

    kb, vb = with_prev(k), with_prev(v)
    s = jnp.einsum('bnqhgd,bnkhd->bhgnqk', q, kb).astype(f32) * (HEAD_DIM ** -0.5)
    qi = jnp.arange(BLOCK)[:, None] + BLOCK
    ki = jnp.arange(2 * BLOCK)[None, :]
    band = (ki <= qi) & (qi - ki < WINDOW)
    first = (jnp.arange(nb) == 0)[:, None, None]
    valid = band[None] & ~(first & (ki < BLOCK)[None])
    s = jnp.where(valid, s, NEG_INF)
    sink = sinks.astype(f32).reshape(N_KV_HEADS, Q_PER_KV)[None, :, :, None, None, None]
    m = jnp.maximum(jnp.max(s, axis=-1, keepdims=True), sink)
    p = jnp.exp(s - m)
    p = p / (jnp.sum(p, axis=-1, keepdims=True) + jnp.exp(sink - m))
    o = jnp.einsum('bhgnqk,bnkhd->bnqhgd', p.astype(vb.dtype), vb).reshape(bsz, L, d)
    return o @ w_o + b_o


def conv_ffn(h, w_up, w_conv, b_conv, w_down):
    u = h @ w_up
    ch = u.shape[-1]
    u = lax.conv_general_dilated(
        u, w_conv[:, None, :].astype(u.dtype), window_strides=(1,),
        padding=[(CONV_WIDTH - 1, 0)], dimension_numbers=('NWC', 'WIO', 'NWC'),
        feature_group_count=ch) + b_conv
    a, val = jnp.split(u, 2, axis=-1)
    return (jax.nn.silu(a) * val) @ w_down


def setup_inputs(seed: int = 0) -> dict:
    key = jax.random.key(seed)
    ks = iter(jax.random.split(key, 32))
    nrm = lambda shape, scale: jax.random.normal(next(ks), shape, jnp.float32) * scale
    G, P, C = S5_GROUPS, S5_STATE, S5_GROUP
    na, nbl = N_S5_LAYERS, N_ATTN_LAYERS
    x = jax.random.normal(next(ks), (BATCH, SEQ, D_MODEL), jnp.float32)
    offs = jax.random.randint(next(ks), (BATCH, 1), 0, 1024, dtype=jnp.int32)
    positions = offs + jnp.arange(SEQ, dtype=jnp.int32)[None, :]
    n_idx = jnp.arange(P, dtype=jnp.float32)
    return {
        "x": x,
        "positions": positions,
        "norm_mix": 1.0 + nrm((DEPTH, D_MODEL), 0.02),
        "norm_ffn": 1.0 + nrm((DEPTH, D_MODEL), 0.02),
        "norm_final": 1.0 + nrm((D_MODEL,), 0.02),
        "s5_lambda_re": -0.5 + nrm((na, G, P), 0.01),
        "s5_lambda_im": math.pi * n_idx + nrm((na, G, P), 0.01),
        "s5_log_dt": jax.random.uniform(next(ks), (na, G), jnp.float32, math.log(DT_MIN), math.log(DT_MAX)),
        "s5_b_re": nrm((na, G, P, C), (2 * C) ** -0.5),
        "s5_b_im": nrm((na, G, P, C), (2 * C) ** -0.5),
        "s5_c_re": nrm((na, G, C, P), (2 * P) ** -0.5),
        "s5_c_im": nrm((na, G, C, P), (2 * P) ** -0.5),
        "s5_d": nrm((na, D_MODEL), 1.0),
        "s5_w_glu": nrm((na, D_MODEL, D_MODEL), D_MODEL ** -0.5),
        "s5_b_glu": nrm((na, D_MODEL), 0.01),
        "attn_w_qkv": nrm((nbl, D_MODEL, QKV_DIM), D_MODEL ** -0.5),
        "attn_b_qkv": nrm((nbl, QKV_DIM), 0.01),
        "attn_sinks": nrm((nbl, N_Q_HEADS), 0.5),
        "attn_w_o": nrm((nbl, D_MODEL, D_MODEL), D_MODEL ** -0.5),
        "attn_b_o": nrm((nbl, D_MODEL), 0.01),
        "ffn_w_up": nrm((DEPTH, D_MODEL, 2 * D_FF), D_MODEL ** -0.5),
        "ffn_w_conv": nrm((DEPTH, CONV_WIDTH, 2 * D_FF), CONV_WIDTH ** -0.5),
        "ffn_b_conv": nrm((DEPTH, 2 * D_FF), 0.01),
        "ffn_w_down": nrm((DEPTH, D_FF, D_MODEL), D_FF ** -0.5),
    }


def reference(x, positions, norm_mix, norm_ffn, norm_final,
              s5_lambda_re, s5_lambda_im, s5_log_dt, s5_b_re, s5_b_im, s5_c_re, s5_c_im,
              s5_d, s5_w_glu, s5_b_glu,
              attn_w_qkv, attn_b_qkv, attn_sinks, attn_w_o, attn_b_o,
              ffn_w_up, ffn_w_conv, ffn_b_conv, ffn_w_down):
    for i in range(DEPTH):
        j = i // N_MIXERS
        h = rmsnorm(x, norm_mix[i])
        if i % N_MIXERS == 0:
            x = x + s5_mixer(h, s5_lambda_re[j], s5_lambda_im[j], s5_log_dt[j],
                             s5_b_re[j], s5_b_im[j], s5_c_re[j], s5_c_im[j],
                             s5_d[j], s5_w_glu[j], s5_b_glu[j])
        else:
            x = x + swa_mixer(h, positions, attn_w_qkv[j], attn_b_qkv[j],
                              attn_sinks[j], attn_w_o[j], attn_b_o[j])
        h = rmsnorm(x, norm_ffn[i])
        x = x + conv_ffn(h, ffn_w_up[i], ffn_w_conv[i], ffn_b_conv[i], ffn_w_down[i])
    return rmsnorm(x, norm_final)
```

```python
import numpy as np
import ml_dtypes
from contextlib import ExitStack
import concourse.bass as bass
import concourse.mybir as mybir
from concourse.bass_utils import run_bass_kernel_spmd

F32 = mybir.dt.float32
BF16 = mybir.dt.bfloat16
I32 = mybir.dt.int32
AF = mybir.ActivationFunctionType
ALU = mybir.AluOpType

D = 1024
DFF = 2816
NCH = 22
TT = 512
EPS = 1e-5
ENGS = ("pe", "act", "dve", "pool", "sp")
STRICT = False


class Op:
    __slots__ = ("eng", "emit", "deps", "idx", "eidx", "signal", "semval", "dma", "dma_cnt", "waits")


class Prog:
    def __init__(self, nc):
        self.nc = nc
        self.ops = []
        self.eng_ops = {e: [] for e in ENGS}
        self.state = {}
        self.by_head = {}
        self.dma_counts = {}
        self.pending_dma = []

    def _conf(self, key):
        for k in self.by_head.get(key[0], ()):
            n = min(len(k), len(key))
            if k[:n] == key[:n]:
                yield k

    def _get(self, key):
        st = self.state.get(key)
        if st is None:
            st = [None, {}, []]
            self.state[key] = st
            self.by_head.setdefault(key[0], []).append(key)
        return st

    def add(self, eng, emit, reads=(), writes=(), dma=None, extra_deps=None):
        op = Op()
        op.eng, op.emit, op.dma, op.signal, op.semval, op.waits = eng, emit, dma, False, 0, []
        op.idx = len(self.ops)
        op.eidx = len(self.eng_ops[eng])
        deps = {}
        reads = [r if isinstance(r, tuple) else (r,) for r in reads]
        writes = [w if isinstance(w, tuple) else (w,) for w in writes]
        for r in reads:
            for k in self._conf(r):
                w = self.state[k][0]
                if w is not None:
                    deps[w] = True
        for w_ in writes:
            for k in self._conf(w_):
                st = self.state[k]
                if st[0] is not None:
                    deps.setdefault(st[0], False)
                for ro in st[1].values():
                    deps.setdefault(ro, False)
                for ro in st[2]:
                    deps.setdefault(ro, False)
        for r in reads:
            st = self._get(r)
            if dma is not None:
                st[2].append(op)
            else:
                st[1][eng] = op
        for w_ in writes:
            for k in list(self._conf(w_)):
                if len(k) > len(w_):
                    del self.state[k]
                    self.by_head[k[0]].remove(k)
            st = self._get(w_)
            st[0], st[1], st[2] = op, {}, []
        if extra_deps:
            for p in extra_deps:
                deps.setdefault(p, False)
        deps.pop(op, None)
        op.deps = deps
        if dma is not None:
            c = self.dma_counts.get(dma, 0) + 1
            self.dma_counts[dma] = c
            op.dma_cnt = c
            self.pending_dma.append(op)
        self.ops.append(op)
        self.eng_ops[eng].append(op)
        return op

    def dma_batch(self, ops):
        c = max(o.dma_cnt for o in ops)
        for o in ops:
            o.dma_cnt = c

    def barrier(self):
        lasts = [self.eng_ops[e][-1] for e in ENGS if self.eng_ops[e]]
        pend = list(self.pending_dma)
        for e in ENGS:
            self.add(e, lambda eng: eng.nop(), extra_deps=lasts + pend)
        self.state = {}
        self.by_head = {}
        self.pending_dma = []

    def finalize_and_emit(self):
        nc = self.nc
        waited = {e: {} for e in ENGS}
        for op in self.ops:
            E = op.eng
            for p, is_raw in op.deps.items():
                if p.dma is not None:
                    k = ("dma", p.dma)
                    if p.dma[0] == "SETUP":
                        p.dma_cnt = self.dma_counts[p.dma]
                    if waited[E].get(k, 0) >= p.dma_cnt:
                        continue
                    waited[E][k] = p.dma_cnt
                    op.waits.append(p)
                elif p.eng == E:
                    if E == "sp":
                        continue
                    if STRICT:
                        if waited[E].get(E, -1) >= p.eidx:
                            continue
                        waited[E][E] = p.eidx
                        op.waits.append(p)
                        p.signal = True
                        continue
                    if E == "pe":
                        continue
                    if is_raw and (op.eidx - p.eidx) <= 1:
                        op.waits.append(p)
                        p.signal = True
                else:
                    if waited[E].get(p.eng, -1) >= p.eidx:
                        continue
                    waited[E][p.eng] = p.eidx
                    op.waits.append(p)
                    p.signal = True
        for e in ENGS:
            c = 0
            for op in self.eng_ops[e]:
                if op.signal:
                    c += 1
                    op.semval = c
        with ExitStack() as es:
            sems = {e: es.enter_context(nc.semaphore("sem_" + e)) for e in ENGS}
            dsems = {}
            for i, k in enumerate(self.dma_counts):
                dsems[k] = es.enter_context(nc.semaphore("dsem%d" % i))
            block = es.enter_context(nc.Block())

            def mk(e):
                def body(eng):
                    for op in self.eng_ops[e]:
                        for p in op.waits:
                            if p.dma is not None:
                                eng.wait_ge(dsems[p.dma], 16 * p.dma_cnt)
                            else:
                                eng.wait_ge(sems[p.eng], p.semval)
                        ins = op.emit(eng)
                        if op.dma is not None:
                            ins.then_inc(dsems[op.dma], 16)
                        elif op.signal:
                            ins.then_inc(sems[e], 1)
                return body

            block.tensor(mk("pe"))
            block.scalar(mk("act"))
            block.vector(mk("dve"))
            block.gpsimd(mk("pool"))
            block.sync(mk("sp"))


QHEADS = [(0, 4), (1, 5), (2, 6), (3, 7), (8, 12), (9, 13), (10, 14), (11, 15)]


def host_consts():
    c = {}
    c["identb"] = np.eye(128, dtype=np.float32).astype(ml_dtypes.bfloat16)
    c["identf"] = np.eye(128, dtype=np.float32)
    pm = np.zeros((128, 128), np.float32)
    for p in range(64):
        pm[p, p + 64] = 1.0
        pm[p + 64, p] = 1.0
    c["pswap"] = pm
    s = np.arange(128) // 16
    c["masks"] = (s[:, None] <= s[None, :]).astype(np.float32)
    q = np.arange(128)[:, None]
    k = np.arange(256)[None, :]
    valid = (k <= q + 128) & (q + 128 - k < 128)
    c["band"] = np.where(valid, 0.0, -1e30).astype(np.float32).astype(ml_dtypes.bfloat16)
    inv = 1.0 / np.power(np.float32(500000.0), np.arange(0, 16, 2, dtype=np.float32) / np.float32(16))
    col = np.zeros((128, 1), np.float32)
    jt = np.zeros((128, 128), np.float32)
    for p in range(128):
        d = p % 64
        if d < 16:
            col[p, 0] = inv[d % 8]
            if d < 8:
                jt[p + 8, p] = -1.0
            else:
                jt[p - 8, p] = 1.0
    c["invf"] = col
    c["jt"] = jt.astype(ml_dtypes.bfloat16)
    c["onesb"] = np.ones((1, 128), np.float32).astype(ml_dtypes.bfloat16)
    c["mhalf"] = np.full((128, 1), -0.5, np.float32)
    return c


CONST_SPECS = {
    "identb": ([128, 128], BF16), "identf": ([128, 128], F32), "pswap": ([128, 128], F32),
    "masks": ([128, 128], F32), "band": ([128, 256], BF16), "invf": ([128, 1], F32),
    "jt": ([128, 128], BF16), "onesb": ([1, 128], BF16), "mhalf": ([128, 1], F32),
}

WEIGHT_SPECS = {
    "norm_mix": [2, 1024], "norm_ffn": [2, 1024], "norm_final": [1024],
    "s5_lambda_re": [1, 64, 64], "s5_lambda_im": [1, 64, 64], "s5_log_dt": [1, 64],
    "s5_b_re": [1, 64, 64, 16], "s5_b_im": [1, 64, 64, 16], "s5_c_re": [1, 64, 16, 64], "s5_c_im": [1, 64, 16, 64],
    "s5_d": [1, 1024], "s5_w_glu": [1, 1024, 1024], "s5_b_glu": [1, 1024],
    "attn_w_qkv": [1, 1024, 1536], "attn_b_qkv": [1, 1536], "attn_sinks": [1, 16],
    "attn_w_o": [1, 1024, 1024], "attn_b_o": [1, 1024],
    "ffn_w_up": [2, 1024, 5632], "ffn_w_conv": [2, 3, 5632], "ffn_b_conv": [2, 5632], "ffn_w_down": [2, 2816, 1024],
}


AX = mybir.AxisListType
TWO_PI = 2.0 * np.pi
MAGIC = 12582912.0
CW1 = 6.28125
CW2 = float(np.float32(TWO_PI - 6.28125))
CW3 = float(TWO_PI - 6.28125 - np.float64(np.float32(TWO_PI - 6.28125)))


class Builder:
    def __init__(self, cfg):
        self.cfg = cfg
        self.nc = bass.Bass("TRN2", target_bir_lowering=False)
        self.P = Prog(self.nc)
        self.es = ExitStack()

    def din(self, name, shape, dt=F32):
        return self.nc.dram_tensor(name, list(shape), dt, kind="ExternalInput").ap()

    def dout(self, name, shape, dt=F32):
        return self.nc.dram_tensor(name, list(shape), dt, kind="ExternalOutput").ap()

    def dscr(self, name, shape, dt):
        return self.nc.dram_tensor(name, list(shape), dt).ap()

    def sb(self, stack, name, shape, dt):
        return stack.enter_context(self.nc.sbuf_tensor(name, list(shape), dt))

    def ps(self, stack, name, shape, dt):
        return stack.enter_context(self.nc.psum_tensor(name, list(shape), dt))

    def dma(self, q, out, in_, reads, writes, key, slow=False):
        if slow:
            f = lambda eng: eng.dma_start(out=out, in_=in_, allow_slow_non_contiguous=True)
        else:
            f = lambda eng: eng.dma_start(out=out, in_=in_)
        return self.P.add(q, f, reads=reads, writes=writes, dma=key)

    def act(self, out, in_, func, reads, writes, scale=None, bias=None, accum=None):
        kw = {}
        if scale is not None:
            kw["scale"] = scale
        if bias is not None:
            kw["bias"] = bias
        if accum is not None:
            kw["accum_out"] = accum
        return self.P.add("act", lambda e: e.activation(out=out, in_=in_, func=func, **kw), reads, writes)

    def ts(self, eng, out, in0, s1, s2, op0, op1, reads, writes):
        if s2 is None:
            f = lambda e: e.tensor_scalar(out=out, in0=in0, scalar1=s1, scalar2=None, op0=op0)
        else:
            f = lambda e: e.tensor_scalar(out=out, in0=in0, scalar1=s1, scalar2=s2, op0=op0, op1=op1)
        return self.P.add(eng, f, reads, writes)

    def stt(self, eng, out, in0, scalar, in1, op0, op1, reads, writes):
        return self.P.add(eng, lambda e: e.scalar_tensor_tensor(out=out, in0=in0, scalar=scalar, in1=in1, op0=op0, op1=op1), reads, writes)

    def tt(self, eng, out, in0, in1, op, reads, writes):
        return self.P.add(eng, lambda e: e.tensor_tensor(out=out, in0=in0, in1=in1, op=op), reads, writes)

    def cp(self, eng, out, in_, reads, writes):
        if eng == "act":
            return self.P.add("act", lambda e: e.copy(out=out, in_=in_), reads, writes)
        return self.P.add(eng, lambda e: e.tensor_copy(out=out, in_=in_), reads, writes)

    def memset(self, eng, ap, val, writes):
        return self.P.add(eng, lambda e: e.memset(ap, val), [], writes)

    def mm(self, items, reads, writes):
        def f(e):
            ins = None
            for (o, l, r, s0, s1) in items:
                ins = e.matmul(o, lhsT=l, rhs=r, start=s0, stop=s1)
            return ins
        return self.P.add("pe", f, reads, writes)

    def tr(self, items, reads, writes):
        def f(e):
            ins = None
            for (o, i_, idn) in items:
                ins = e.transpose(out=o, in_=i_, identity=idn)
            return ins
        return self.P.add("pe", f, reads, writes)


def psk(i):
    return ("ps", i)


def build(cfg):
    B = Builder(cfg)
    nc, P = B.nc, B.P
    NT = cfg.get("NT", 9)
    do_s5 = cfg.get("s5", True)
    layers = cfg.get("layers", ("ffn0", "att", "ffn1", "final"))

    W = {k: B.din(k, s) for k, s in WEIGHT_SPECS.items()}
    C = {k: B.din(k, s, dt) for k, (s, dt) in CONST_SPECS.items()}
    cflag_d = B.din("cflag", [128, 1])
    pos_d = B.din("posl", [9 * TT], I32)
    if do_s5:
        xl_d = B.din("xl", [8192, D])
        x1_d = B.dscr("x1s", [9 * TT, D], F32)
    else:
        xl_d = None
        x1_d = B.din("x1s", [9 * TT, D])
    out_d = B.dout("y", [8 * TT, D])
    dbg_d = B.dout("dbg", [9 * TT, D]) if cfg.get("dbg") else None
    wup_s = B.dscr("wup_s", [2, NCH, 128, 2 * 8 * 128], BF16)
    wdn_s = B.dscr("wdn_s", [2, 2, 11, 128, 2 * 512], BF16)
    wqk_s = B.dscr("wqk_s", [10, 128, 8 * 128], BF16)
    wv_s = B.dscr("wv_s", [128, 8 * 256], BF16)
    wo_s = B.dscr("wo_s", [2, 4, 128, 2 * 512], BF16)
    wglu_s = B.dscr("wglu_s", [8, 128, 1024], BF16)

    glob = B.es
    Cs = {}
    for k, (s, dt) in CONST_SPECS.items():
        Cs[k] = B.sb(glob, "c_" + k, s, dt)
        B.dma("sp", Cs[k][:], C[k], [], [("c", k)], ("SETUP", 0))
    cflag = B.sb(glob, "cflag_s", [128, 1], F32)
    B.dma("sp", cflag[:], cflag_d, [], [("c", "cflag")], ("SETUP", 0))
    PS = [B.ps(glob, "ps%d" % i, [128, 512], F32) for i in range(8)]

    with ExitStack() as st0:
        gcol = B.sb(st0, "gcol", [128, 4, 8], F32)
        for wi, src in enumerate((W["norm_ffn"][0], W["norm_ffn"][1], W["norm_mix"][1])):
            B.dma("sp", gcol[:, wi, :], src.rearrange("(k p) -> p k", p=128), [], [("gcol", wi)], ("SETUP", 0), slow=True)
        wraw = [B.sb(st0, "wraw%d" % i, [128, 5632], F32) for i in range(2)]
        wcv = [B.sb(st0, "wcv%d" % i, [128, 5632], BF16) for i in range(2)]
        cnt = [0]
        ei = [0]

        def nexti():
            i = cnt[0] % 2
            cnt[0] += 1
            return i

        def cast(i, n, gain_ap):
            e_ = ("act", "pool")[ei[0] % 2]
            ei[0] += 1
            if gain_ap is not None:
                if e_ == "act":
                    B.act(wcv[i][:, 0:n], wraw[i][:, 0:n], AF.Copy, [("wraw", i), ("gcol",)], [("wcv", i)], scale=gain_ap)
                else:
                    B.ts(e_, wcv[i][:, 0:n], wraw[i][:, 0:n], gain_ap, None, ALU.mult, None, [("wraw", i), ("gcol",)], [("wcv", i)])
            else:
                B.cp(e_, wcv[i][:, 0:n], wraw[i][:, 0:n], [("wraw", i)], [("wcv", i)])

        for l in range(2):
            if ("ffn%d" % l) not in layers:
                continue
            for k in range(8):
                i = nexti()
                B.dma("sp", wraw[i][:, :], W["ffn_w_up"][l, k * 128:(k + 1) * 128, :], [], [("wraw", i)], ("wraw", i))
                cast(i, 5632, gcol[:, l, k:k + 1])
                dst = wup_s[l].rearrange("m p (a k c) -> p a m k c", a=2, k=8, c=128)[:, :, :, k, :]
                B.dma("sp", dst, wcv[i][:, :].rearrange("p (a m c) -> p a m c", a=2, m=NCH, c=128), [("wcv", i)], [("wscr",)], ("wst", i))
            for mm in range(11):
                i = nexti()
                src = W["ffn_w_down"][l, mm * 256:(mm + 1) * 256, :].rearrange("(j p) n -> p j n", p=128)
                B.dma("sp", wraw[i][:, 0:2048].rearrange("p (j n) -> p j n", j=2), src, [], [("wraw", i)], ("wraw", i))
                cast(i, 2048, None)
                for half in range(2):
                    dst = wdn_s[l, half, mm].rearrange("p (j n) -> p j n", j=2)
                    srcv = wcv[i][:, 0:2048].rearrange("p (j n) -> p j n", j=2)[:, :, half * 512:(half + 1) * 512]
                    B.dma("sp", dst, srcv, [("wcv", i)], [("wscr",)], ("wst", i))
        if "att" in layers:
            for k in range(8):
                i = nexti()
                B.dma("sp", wraw[i][:, 0:1536], W["attn_w_qkv"][0, k * 128:(k + 1) * 128, :], [], [("wraw", i)], ("wraw", i))
                g_ap = gcol[:, 2, k:k + 1]
                for a in range(2):
                    srcv = wraw[i][:, a * 512:(a + 1) * 512].rearrange("p (j b d) -> p b j d", j=2, b=4, d=64)
                    dstv = wcv[i][:, a * 512:(a + 1) * 512].rearrange("p (b j d) -> p b j d", b=4, j=2, d=64)
                    B.ts("pool", dstv, srcv, g_ap, None, ALU.mult, None, [("wraw", i), ("gcol",)], [("wcv", i, "q%d" % a)])
                B.act(wcv[i][:, 1024:1536], wraw[i][:, 1024:1536], AF.Copy, [("wraw", i), ("gcol",)], [("wcv", i, "kv")], scale=g_ap)
                dstqk = wqk_s.rearrange("c p (k n) -> p c k n", k=8)[:, :, k, :]
                B.dma("sp", dstqk, wcv[i][:, 0:1280].rearrange("p (c n) -> p c n", n=128), [("wcv", i)], [("wscr",)], ("wst", i))
                dstv2 = wv_s.rearrange("p (k n) -> p k n", k=8)[:, k, :]
                B.dma("sp", dstv2, wcv[i][:, 1280:1536], [("wcv", i)], [("wscr",)], ("wst", i))
            for c in range(8):
                i = nexti()
                hA, hB = QHEADS[c]
                B.dma("sp", wraw[i][0:64, 0:1024], W["attn_w_o"][0, hA * 64:(hA + 1) * 64, :], [], [("wraw", i, "a")], ("wraw", i))
                B.dma("sp", wraw[i][64:128, 0:1024], W["attn_w_o"][0, hB * 64:(hB + 1) * 64, :], [], [("wraw", i, "b")], ("wrawb", i))
                cast(i, 1024, None)
                for half in range(2):
                    dst = wo_s[half, c // 2].rearrange("p (j n) -> p j n", j=2)[:, c % 2, :]
                    B.dma("sp", dst, wcv[i][:, half * 512:(half + 1) * 512], [("wcv", i)], [("wscr",)], ("wst", i))
        if do_s5:
            for k in range(8):
                i = nexti()
                B.dma("sp", wraw[i][:, 0:1024], W["s5_w_glu"][0, k * 128:(k + 1) * 128, :], [], [("wraw", i)], ("wraw", i))
                cast(i, 1024, None)
                B.dma("sp", wglu_s[k], wcv[i][:, 0:1024], [("wcv", i)], [("wscr",)], ("wst", i))
        P.barrier()

    if do_s5:
        emit_s5(B, W, Cs, PS, xl_d, x1_d, wglu_s, cflag)
        P.barrier()

    emit_phase2(B, W, Cs, PS, cflag, pos_d, x1_d, out_d, dbg_d, wup_s, wdn_s, wqk_s, wv_s, wo_s, NT, layers)
    P.barrier()
    P.finalize_and_emit()
    B.es.close()
    return nc


def emit_phase2(B, W, Cs, PS, cflag, pos_d, x1_d, out_d, dbg_d, wup_s, wdn_s, wqk_s, wv_s, wo_s, NT, layers):
    nc, P = B.nc, B.P
    st = ExitStack()
    sb = lambda name, shape, dt: B.sb(st, name, shape, dt)
    identb = Cs["identb"]

    xres = sb("xres", [128, 4, D], F32)
    hT = sb("hT", [128, 8, TT], BF16)
    gT = sb("gT", [128, NCH, TT], BF16)
    wup = [sb("wup%d" % i, [128, 2, 8, 128], BF16) for i in range(3)]
    wdn = [sb("wdn%d" % i, [128, 2, 512], BF16) for i in range(3)]
    uba = [sb("uba%d" % i, [128, TT + 2], F32) for i in range(2)]
    ubv = [sb("ubv%d" % i, [128, TT + 2], F32) for i in range(2)]
    ca = [sb("ca%d" % i, [128, TT], F32) for i in range(2)]
    cv = [sb("cv%d" % i, [128, TT], F32) for i in range(2)]
    sa = [sb("sa%d" % i, [128, TT], F32) for i in range(2)]
    halo = sb("halo", [128, 2, 2 * NCH, 2], F32)
    convw = sb("convw", [128, 2, 4, 2 * NCH], F32)
    cwnat = sb("cwnat", [2 * NCH, 2, 4, 128], F32)
    junk = sb("junk", [128, D], BF16)
    xn = [sb("xn%d" % i, [128, D], BF16) for i in range(2)]
    ss = sb("ss", [128, 4], F32)
    vv = sb("vv", [128, 4], F32)
    rstd = sb("rstd", [128, 4], F32)
    gfin = sb("gfin", [128, D], F32)
    otile = [sb("otile%d" % i, [128, D], F32) for i in range(2)]

    for l in range(2):
        for k in range(3):
            B.dma("sp", cwnat[:, l, k, :], W["ffn_w_conv"][l, k].rearrange("(c p) -> c p", p=128), [], [("cwnat", l, k)], ("SETUP", 2))
        B.dma("sp", cwnat[:, l, 3, :], W["ffn_b_conv"][l].rearrange("(c p) -> c p", p=128), [], [("cwnat", l, 3)], ("SETUP", 2))
    for l in range(2):
        B.tr([(PS[l][:, k * 64:k * 64 + 2 * NCH], cwnat[:, l, k, :], Cs["identf"][0:2 * NCH, 0:2 * NCH]) for k in range(4)],
             [("cwnat", l), ("c", "identf")], [psk(l)])
        B.cp("dve", convw[:, l, :, :], PS[l][:, 0:256].rearrange("p (k c) -> p k c", k=4)[:, :, 0:2 * NCH], [psk(l)], [("convw", l)])
    B.memset("pool", halo[:], 0.0, [("halo",)])
    B.dma("sp", gfin[:], W["norm_final"].partition_broadcast(128), [], [("gfin",)], ("SETUP", 2))

    def stats_rstd(warm):
        for sub in range(4):
            B.act(junk[:], xres[:, sub, :], AF.Square, [("xres", sub)], [("junk",), ("ss", sub)], accum=ss[:, sub:sub + 1])
        B.ts("dve", vv[:], ss[:], 1.0 / D, EPS, ALU.mult, ALU.add, [("ss",)], [("vv",)])
        B.tt("pool", rstd[:], vv[:], Cs["mhalf"][:, 0:1].broadcast_to([128, 4]), ALU.pow, [("vv",), ("c", "mhalf")], [("rstd",)])
        if warm:
            B.ts("pool", rstd[:], rstd[:], cflag[:, 0:1], None, ALU.mult, None, [("rstd",), ("c", "cflag")], [("rstd",)])

    def norm_to_hT(warm):
        stats_rstd(warm)
        for sub in range(4):
            i = sub % 2
            B.act(xn[i][:], xres[:, sub, :], AF.Copy, [("xres", sub), ("rstd",)], [("xn", i)], scale=rstd[:, sub:sub + 1])
            pb = 6 + i
            ptv = PS[pb][:].bitcast(BF16)
            B.tr([(ptv[:, k * 128:(k + 1) * 128], xn[i][:, k * 128:(k + 1) * 128], identb[:]) for k in range(8)],
                 [("xn", i), ("c", "identb")], [psk(pb)])
            B.cp("dve", hT[:, :, sub * 128:(sub + 1) * 128], ptv.rearrange("p (k t) -> p k t", k=8), [psk(pb)], [("hT", sub)])

    def ffn(l, down=True):
        for m in range(NCH):
            s3 = m % 3
            s2 = m % 2
            B.dma("sp", wup[s3][:].rearrange("p a k c -> p (a k c)"), wup_s[l, m], [("wscr",)], [("wup", s3)], ("wup", s3))
            for av, (ub, pbase, key) in enumerate(((uba, 0, "uba"), (ubv, 2, "ubv"))):
                pb = pbase + s2
                B.mm([(PS[pb][:], wup[s3][:, av, k, :], hT[:, k, :], k == 0, k == 7) for k in range(8)], [("wup", s3), ("hT",)], [psk(pb)])
                ch = av * NCH + m
                u = ub[s2]
                B.cp("pool", u[:, 0:2], halo[:, l, ch, :], [("halo", l, ch)], [(key, s2, "h")])
                B.cp("act", u[:, 2:TT + 2], PS[pb][:], [psk(pb)], [(key, s2, "m")])
                B.cp("pool", halo[:, l, ch, :], u[:, TT:TT + 2], [(key, s2, "m")], [("halo", l, ch)])
                dst = ca[s2] if av == 0 else cv[s2]
                dkey = ("ca", s2) if av == 0 else ("cv", s2)
                cw = [convw[:, l, k, ch:ch + 1] for k in range(4)]
                B.act(dst[:], PS[pb][:], AF.Identity, [psk(pb), ("convw", l)], [dkey], scale=cw[2], bias=cw[3])
                B.stt("dve", dst[:], u[:, 1:TT + 1], cw[1], dst[:], ALU.mult, ALU.add, [(key, s2), ("convw", l), dkey], [dkey])
                B.stt("dve", dst[:], u[:, 0:TT], cw[0], dst[:], ALU.mult, ALU.add, [(key, s2), ("convw", l), dkey], [dkey])
            B.act(sa[s2][:], ca[s2][:], AF.Silu, [("ca", s2)], [("sa", s2)])
            B.tt("dve", gT[:, m, :], sa[s2][:], cv[s2][:], ALU.mult, [("sa", s2), ("cv", s2)], [("gT", m)])
        if down:
            proj_tokmajor(gT, ("gT",), wdn_s[l], 11, None)

    wcount = [0]

    def proj_tokmajor(actT, actkey, wsrc, nmm, bias_row):
        for half in range(2):
            pbase = 4 if half == 0 else 0
            for mm in range(nmm):
                s3 = wcount[0] % 3
                wcount[0] += 1
                B.dma("sp", wdn[s3][:].rearrange("p j n -> p (j n)"), wsrc[half, mm], [("wscr",)], [("wdn", s3)], ("wdn", s3))
                items = []
                for sub in range(4):
                    for j in range(2):
                        first = (mm == 0 and j == 0)
                        last = (mm == nmm - 1 and j == 1 and bias_row is None)
                        items.append((PS[pbase + sub][:], actT[:, 2 * mm + j, sub * 128:(sub + 1) * 128], wdn[s3][:, j, :], first, last))
                B.mm(items, [actkey, ("wdn", s3)], [psk(pbase + sub) for sub in range(4)])
            if bias_row is not None:
                B.mm([(PS[pbase + sub][:], Cs["onesb"][0:1, :], bias_row[0:1, half * 512:(half + 1) * 512], False, True) for sub in range(4)],
                     [("c", "onesb"), ("bo",)], [psk(pbase + sub) for sub in range(4)])
            for sub in range(4):
                xs = xres[:, sub, half * 512:(half + 1) * 512]
                B.tt("dve", xs, xs, PS[pbase + sub][:], ALU.add, [psk(pbase + sub), ("xres", sub, half)], [("xres", sub, half)])

    def final_norm(t):
        stats_rstd(False)
        for sub in range(4):
            i = sub % 2
            B.stt("dve", otile[i][:], xres[:, sub, :], rstd[:, sub:sub + 1], gfin[:], ALU.mult, ALU.mult, [("xres", sub), ("rstd",), ("gfin",)], [("otile", i)])
            r0 = (t - 1) * TT + sub * 128
            B.dma("sp", out_d[r0:r0 + 128, :], otile[i][:], [("otile", i)], [("out", t, sub)], ("otile", i))

    attn = None
    if "att" in layers:
        attn = make_attention(B, W, Cs, PS, st, cflag, pos_d, wqk_s, wv_s, wo_s, xres, hT, norm_to_hT, proj_tokmajor)

    for t in range(NT):
        warm = (t == 0)
        B.dma("sp", xres[:], x1_d[t * TT:(t + 1) * TT, :].rearrange("(s p) f -> p s f", p=128), [("x1s",)], [("xres",)], ("xres",))
        if "ffn0" in layers:
            norm_to_hT(warm)
            ffn(0)
        if attn is not None:
            attn(t, warm)
        if "ffn1" in layers:
            norm_to_hT(warm)
            ffn(1, down=not warm)
        if dbg_d is not None:
            B.dma("sp", dbg_d[t * TT:(t + 1) * TT, :].rearrange("(s p) f -> p s f", p=128), xres[:], [("xres",)], [("dbg", t)], ("dbg",))
        if "final" in layers and not warm:
            final_norm(t)
    B.es.callback(st.close)


def make_attention(B, W, Cs, PS, st, cflag, pos_d, wqk_s, wv_s, wo_s, xres, hT, norm_to_hT, proj_tokmajor):
    sb = lambda name, shape, dt: B.sb(st, name, shape, dt)
    identb = Cs["identb"]
    qT = sb("qT", [128, 8, TT], BF16)
    kT = sb("kT", [128, 2, 128 + TT], BF16)
    Vb = sb("Vb", [128, 5, 256], BF16)
    oT = sb("oT", [128, 8, TT], BF16)
    qraw = [sb("qraw%d" % i, [128, TT], F32) for i in range(2)]
    qb16 = [sb("qb16%d" % i, [128, TT], BF16) for i in range(2)]
    rt1 = [sb("rt1%d" % i, [128, TT], F32) for i in range(2)]
    rt2 = [sb("rt2%d" % i, [128, TT], F32) for i in range(2)]
    cosT = sb("cosT", [128, TT], F32)
    sinT = sb("sinT", [128, TT], F32)
    posi = sb("posi", [128, TT], I32)
    angA = sb("angA", [128, TT], F32)
    angB = sb("angB", [128, TT], F32)
    angC = sb("angC", [128, TT], F32)
    wqk = [sb("wqk%d" % i, [128, 8, 128], BF16) for i in range(3)]
    wv = sb("wv", [128, 8, 256], BF16)
    bqk = sb("bqk", [128, 10], F32)
    bvb = sb("bvb", [128, 256], F32)
    bo32 = sb("bo32", [1, D], F32)
    bo16 = sb("bo16", [1, D], BF16)
    sinkb = sb("sinkb", [128, 16], F32)
    sinkp = sb("sinkp", [128, 8, 2], F32)
    nsinkp = sb("nsinkp", [128, 8, 2], F32)
    maskF = sb("maskF", [128, 256], BF16)
    mtmp = sb("mtmp", [128, 1], F32)
    Pexp = [sb("Pexp%d" % i, [128, 2, 256], BF16) for i in range(2)]
    sm = [sb("sm%d" % i, [128, 8, 2], F32) for i in range(2)]
    dgm = [sb("dgm%d" % i, [128, 2, 128], BF16) for i in range(2)]
    PTs = [sb("PTs%d" % i, [128, 4, 128], BF16) for i in range(2)]

    bq = W["attn_b_qkv"][0]
    for j in range(2):
        for a in range(2):
            B.dma("sp", bqk[64 * j:64 * j + 64, 4 * a:4 * a + 4],
                  bq[0:1024].rearrange("(a j b d) -> a j d b", a=2, j=2, b=4, d=64)[a, j], [], [("bqk", "q", j, a)], ("SETUP", 2), slow=True)
    B.dma("sp", bqk[:, 8:10], bq[1024:1280].rearrange("(c p) -> p c", p=128), [], [("bqk", "k")], ("SETUP", 2), slow=True)
    B.dma("sp", bvb[:], bq[1280:1536].partition_broadcast(128), [], [("bvb",)], ("SETUP", 2))
    B.dma("sp", bo32[:], W["attn_b_o"], [], [("bo32",)], ("SETUP", 2))
    B.cp("act", bo16[:], bo32[:], [("bo32",)], [("bo",)])
    B.dma("sp", sinkb[:], W["attn_sinks"][0].partition_broadcast(128), [], [("sinkb",)], ("SETUP", 2))
    B.cp("dve", sinkp[:].rearrange("p (a b) j -> p a b j", a=2), sinkb[:].rearrange("p (a j b) -> p a b j", a=2, j=2, b=4), [("sinkb",)], [("sinkp",)])
    B.ts("dve", nsinkp[:], sinkp[:], -1.0, None, ALU.mult, None, [("sinkp",)], [("nsinkp",)])
    B.ts("dve", mtmp[:], cflag[:], -1.0, 1e30, ALU.add, ALU.mult, [("c", "cflag")], [("mtmp",)])
    B.ts("dve", maskF[:, 0:128], Cs["band"][:, 0:128], mtmp[:, 0:1], None, ALU.add, None, [("c", "band"), ("mtmp",)], [("maskF", 0)])
    B.cp("dve", maskF[:, 128:256], Cs["band"][:, 128:256], [("c", "band")], [("maskF", 1)])
    B.memset("pool", kT[:], 0.0, [("kT",)])
    B.memset("pool", Vb[:], 0.0, [("Vb",)])
    cnt = [0]

    def attn(t, warm):
        norm_to_hT(warm)
        if B.cfg.get("att_stage", 9) < 0:
            return
        B.dma("sp", posi[:], pos_d[t * TT:(t + 1) * TT].partition_broadcast(128), [], [("posi",)], ("posi",))
        B.ts("dve", angA[:], posi[:], Cs["invf"][:, 0:1], None, ALU.mult, None, [("posi",), ("c", "invf")], [("angA",)])
        B.ts("dve", angB[:], angA[:], float(1.0 / TWO_PI), MAGIC, ALU.mult, ALU.add, [("angA",)], [("angB",)])
        B.ts("dve", angB[:], angB[:], -MAGIC, None, ALU.add, None, [("angB",)], [("angB",)])
        B.stt("dve", angC[:], angB[:], -CW1, angA[:], ALU.mult, ALU.add, [("angB",), ("angA",)], [("angC",)])
        B.stt("dve", angC[:], angB[:], -CW2, angC[:], ALU.mult, ALU.add, [("angB",), ("angC",)], [("angC",)])
        B.stt("dve", angC[:], angB[:], -CW3, angC[:], ALU.mult, ALU.add, [("angB",), ("angC",)], [("angC",)])
        B.ts("dve", angC[:], angC[:], -float(np.pi), float(np.pi), ALU.max, ALU.min, [("angC",)], [("angC",)])
        B.act(sinT[:], angC[:], AF.Sin, [("angC",)], [("sinT",)])
        B.ts("dve", angA[:], angC[:], float(np.pi / 2), None, ALU.add, None, [("angC",)], [("angA",)])
        B.ts("dve", angB[:], angA[:], float(np.pi), -float(TWO_PI), ALU.is_gt, ALU.mult, [("angA",)], [("angB",)])
        B.tt("dve", angA[:], angA[:], angB[:], ALU.add, [("angA",), ("angB",)], [("angA",)])
        B.ts("dve", angA[:], angA[:], -float(np.pi), float(np.pi), ALU.max, ALU.min, [("angA",)], [("angA",)])
        B.act(cosT[:], angA[:], AF.Sin, [("angA",)], [("cosT",)])
        if B.cfg.get("att_stage", 9) < 1:
            return
        for c in range(10):
            s3 = c % 3
            i = c % 2
            B.dma("sp", wqk[s3][:].rearrange("p k n -> p (k n)"), wqk_s[c], [("wscr",)], [("wqk", s3)], ("wqk", s3))
            B.mm([(PS[i][:], wqk[s3][:, k, :], hT[:, k, :], k == 0, k == 7) for k in range(8)], [("wqk", s3), ("hT",)], [psk(i)])
            B.act(qraw[i][:], PS[i][:], AF.Identity, [psk(i), ("bqk",)], [("qraw", i)], bias=bqk[:, c:c + 1])
            B.cp("dve", qb16[i][:], qraw[i][:], [("qraw", i)], [("qb16", i)])
            if B.cfg.get("qk_sub", 9) < 1:
                continue
            B.mm([(PS[2 + i][:], Cs["jt"][:], qb16[i][:], True, True)], [("c", "jt"), ("qb16", i)], [psk(2 + i)])
            B.tt("dve", rt1[i][:], PS[2 + i][:], sinT[:], ALU.mult, [psk(2 + i), ("sinT",)], [("rt1", i)])
            if B.cfg.get("qk_sub", 9) < 2:
                continue
            B.tt("dve", rt2[i][:], qraw[i][:], cosT[:], ALU.mult, [("qraw", i), ("cosT",)], [("rt2", i)])
            if c < 8:
                B.tt("dve", qT[:, c, :], rt1[i][:], rt2[i][:], ALU.add, [("rt1", i), ("rt2", i)], [("qT", c)])
            else:
                B.tt("dve", kT[:, c - 8, 128:128 + TT], rt1[i][:], rt2[i][:], ALU.add, [("rt1", i), ("rt2", i)], [("kT", c - 8, "cur")])
        if B.cfg.get("att_stage", 9) < 2:
            return
        B.dma("sp", wv[:].rearrange("p k n -> p (k n)"), wv_s, [("wscr",)], [("wv",)], ("wv",))
        for sub in range(4):
            pb = 4 + sub % 2
            B.mm([(PS[pb][:, 0:256], hT[:, k, sub * 128:(sub + 1) * 128], wv[:, k, :], k == 0, k == 7) for k in range(8)], [("wv",), ("hT",)], [psk(pb)])
            B.tt("dve", Vb[:, 1 + sub, :], PS[pb][:, 0:256], bvb[:], ALU.add, [psk(pb), ("bvb",)], [("Vb", 1 + sub)])
        if B.cfg.get("att_stage", 9) < 3:
            return
        for qb in range(4):
            msk = maskF if (t == 1 and qb == 0) else Cs["band"]
            mkey = ("maskF",) if (t == 1 and qb == 0) else ("c", "band")
            for c in range(8):
                a = c // 4
                u = cnt[0] % 2
                cnt[0] += 1
                sbk = u
                ptb = 2 + u
                ob = 4 + (c // 4)
                items = []
                for j in range(2):
                    r0 = 64 * j
                    o = PS[sbk][:, j * 256:(j + 1) * 256]
                    items.append((o, qT[r0:r0 + 64, c, qb * 128:(qb + 1) * 128], kT[r0:r0 + 64, a, qb * 128:qb * 128 + 256], True, False))
                    items.append((o, identb[:], msk[:], False, True))
                B.mm(items, [("qT", c), ("kT", a), ("c", "identb"), mkey], [psk(sbk)])
                S = sm[u]
                mx, nm, tsk, es, rs, den, rden = [S[:, r, :] for r in range(7)]
                B.P.add("dve", (lambda e, o_=mx, i_=PS[sbk][:].rearrange("p (j n) -> p j n", j=2): e.tensor_reduce(out=o_, in_=i_, axis=AX.X, op=ALU.max)),
                        [psk(sbk)], [("sm", u, 0)])
                B.stt("dve", nm, mx, -0.125, nsinkp[:, c, :], ALU.mult, ALU.min, [("sm", u, 0), ("nsinkp",)], [("sm", u, 1)])
                for j in range(2):
                    B.act(Pexp[u][:, j, :], PS[sbk][:, j * 256:(j + 1) * 256], AF.Exp, [psk(sbk), ("sm", u, 1)], [("Pexp", u, j), ("sm", u, 4, j)],
                          scale=0.125, bias=S[:, 1, j:j + 1], accum=S[:, 4, j:j + 1])
                B.tt("dve", tsk, nm, sinkp[:, c, :], ALU.add, [("sm", u, 1), ("sinkp",)], [("sm", u, 2)])
                B.act(es, tsk, AF.Exp, [("sm", u, 2)], [("sm", u, 3)])
                B.tt("dve", den, rs, es, ALU.add, [("sm", u, 4), ("sm", u, 3)], [("sm", u, 5)])
                B.P.add("dve", (lambda e, o_=rden, i_=den: e.reciprocal(out=o_, in_=i_)), [("sm", u, 5)], [("sm", u, 6)])
                for j in range(2):
                    B.act(dgm[u][:, j, :], identb[:], AF.Copy, [("c", "identb"), ("sm", u, 6)], [("dgm", u, j)], scale=S[:, 6, j:j + 1])
                items = []
                for j in range(2):
                    for kb in range(2):
                        items.append((PS[ptb][:, (2 * j + kb) * 128:(2 * j + kb + 1) * 128], Pexp[u][:, j, kb * 128:(kb + 1) * 128], dgm[u][:, j, :], True, True))
                B.mm(items, [("Pexp", u), ("dgm", u)], [psk(ptb)])
                B.cp("act", PTs[u][:].rearrange("p a q -> p (a q)"), PS[ptb][:], [psk(ptb)], [("PTs", u)])
                items = []
                for j in range(2):
                    kvh = 2 * a + j
                    for kb in range(2):
                        items.append((PS[ob][64 * j:64 * j + 64, (c % 4) * 128:(c % 4 + 1) * 128], Vb[:, qb + kb, kvh * 64:(kvh + 1) * 64],
                                      PTs[u][:, 2 * j + kb, :], kb == 0, kb == 1))
                B.mm(items, [("Vb", qb), ("Vb", qb + 1), ("PTs", u)], [("ps", ob, c % 4)])
                if c % 4 == 3:
                    B.cp("act", oT[:, 4 * a:4 * a + 4, qb * 128:(qb + 1) * 128], PS[ob][:].rearrange("p (c q) -> p c q", c=4), [psk(ob)], [("oT", a, qb)])
        if B.cfg.get("att_stage", 9) < 4:
            return
        proj_tokmajor(oT, ("oT",), wo_s, 4, bo16)
        B.cp("dve", kT[:, :, 0:128], kT[:, :, TT:TT + 128], [("kT",)], [("kT",)])
        B.cp("dve", Vb[:, 0, :], Vb[:, 4, :], [("Vb", 4)], [("Vb", 0)])

    return attn


S5_MS = list(range(-7, 9)) + [16 << k for k in range(9)]
S5_IDX = {m: i for i, m in enumerate(S5_MS)}
NPW = len(S5_MS)
BL0 = 3


def emit_s5(B, W, Cs, PS, xl_d, x1_d, wglu_s, cflag):
    P = B.P
    st = ExitStack()
    sb = lambda n, s, d: B.sb(st, n, s, d)
    SU = ("SETUP", 1)
    identb, identf = Cs["identb"], Cs["identf"]
    fpi = float(np.pi)

    Sp = sb("Sp", [128, 64, 128], BF16)
    Bz = sb("Bz", [128, 64, 128], BF16)
    Dm = sb("Dm", [128, 64, 128], BF16)
    ARHS = sb("ARHS", [128, 10, 64], F32)
    S2HS = sb("S2HS", [128, 10, 64], F32)
    RSTD = sb("RSTD", [128, 8, 8], F32)
    SSQ = sb("SSQ", [128, 8, 8], F32)
    wglu = sb("wglu", [128, 8, 1024], BF16)
    bglu32 = sb("bglu32", [1, D], F32)
    bglu16 = sb("bglu16", [1, D], BF16)
    dgcol = sb("dgcol", [128, 64], F32)
    ys_d = B.dscr("ys_s", [5 * 1024, D], F32)

    B.dma("sp", wglu[:], wglu_s.rearrange("k p n -> p k n"), [("wscr",)], [("wglu",)], SU)
    B.dma("sp", bglu32[:], W["s5_b_glu"], [], [("bglu32",)], SU)
    B.cp("act", bglu16[:], bglu32[:], [("bglu32",)], [("bglu",)])

    stA = ExitStack()
    sa_ = lambda n, s, d: B.sb(stA, n, s, d)
    natL = sa_("natL", [64, 2, 128], F32)
    LRr = sa_("LRr", [128, 64], F32)
    LIr = sa_("LIr", [128, 64], F32)
    dtb = sa_("dtb", [128, 64], F32)
    LR = sa_("LR", [128, 64], F32)
    TH = sa_("TH", [128, 64], F32)
    ANG = sa_("ANG", [128, NPW, 64], F32)
    MAG = sa_("MAG", [128, NPW, 64], F32)
    KK = sa_("KK", [128, NPW, 64], F32)
    RR = sa_("RR", [128, NPW, 64], F32)
    PWR = sa_("PWR", [128, NPW, 64], F32)
    PWI = sa_("PWI", [128, NPW, 64], F32)
    zr = sa_("zr", [128, 64], F32)
    zi = sa_("zi", [128, 64], F32)
    zt = [sa_("zt%d" % i, [128, 64], F32) for i in range(4)]
    dcolS = sa_("dcolS", [128, 64], F32)
    gcolS = sa_("gcolS", [128, 64], F32)

    for w, key in enumerate(("s5_lambda_re", "s5_lambda_im")):
        for h in range(2):
            B.dma("sp", natL[:, w, h * 64:(h + 1) * 64], W[key][0], [], [("natL", w, h)], SU)
    B.dma("sp", dtb[:], W["s5_log_dt"][0].partition_broadcast(128), [], [("dtb",)], SU)
    for s in range(8):
        B.dma("sp", dcolS[16 * s:16 * s + 16, :], W["s5_d"][0].rearrange("(g c) -> c g", c=16), [], [("dcolS", s)], SU, slow=True)
        B.dma("sp", gcolS[16 * s:16 * s + 16, :], W["norm_mix"][0].rearrange("(g c) -> c g", c=16), [], [("gcolS", s)], SU, slow=True)
    B.tt("dve", dgcol[:], dcolS[:], gcolS[:], ALU.mult, [("dcolS",), ("gcolS",)], [("dgcol",)])
    B.tr([(PS[0][:, w * 64:(w + 1) * 64], natL[:, w, :], identf[0:64, 0:64]) for w in range(2)], [("natL",), ("c", "identf")], [psk(0)])
    B.cp("dve", LRr[:], PS[0][:, 0:64], [psk(0)], [("LRr",)])
    B.cp("dve", LIr[:], PS[0][:, 64:128], [psk(0)], [("LIr",)])
    B.act(dtb[:], dtb[:], AF.Exp, [("dtb",)], [("dtb",)])
    B.tt("dve", LR[:], LRr[:], dtb[:], ALU.mult, [("LRr",), ("dtb",)], [("LR",)])
    B.tt("dve", TH[:], LIr[:], dtb[:], ALU.mult, [("LIr",), ("dtb",)], [("TH",)])
    for i, m in enumerate(S5_MS):
        B.ts("dve", ANG[:, i, :], TH[:], float(m), None, ALU.mult, None, [("TH",)], [("ANG", i)])
        B.ts("pool", MAG[:, i, :], LR[:], float(m), None, ALU.mult, None, [("LR",)], [("MAG", i)])
    B.act(MAG[:], MAG[:], AF.Exp, [("MAG",)], [("MAG",)])
    B.ts("dve", KK[:], ANG[:], float(1.0 / TWO_PI), MAGIC, ALU.mult, ALU.add, [("ANG",)], [("KK",)])
    B.ts("dve", KK[:], KK[:], -MAGIC, None, ALU.add, None, [("KK",)], [("KK",)])
    B.stt("dve", RR[:], KK[:], -CW1, ANG[:], ALU.mult, ALU.add, [("KK",), ("ANG",)], [("RR",)])
    B.stt("dve", RR[:], KK[:], -CW2, RR[:], ALU.mult, ALU.add, [("KK",), ("RR",)], [("RR",)])
    B.stt("dve", RR[:], KK[:], -CW3, RR[:], ALU.mult, ALU.add, [("KK",), ("RR",)], [("RR",)])
    B.ts("dve", RR[:], RR[:], -fpi, fpi, ALU.max, ALU.min, [("RR",)], [("RR",)])
    B.act(PWI[:], RR[:], AF.Sin, [("RR",)], [("PWI",)])
    B.ts("dve", RR[:], RR[:], fpi / 2, None, ALU.add, None, [("RR",)], [("RR",)])
    B.ts("dve", KK[:], RR[:], fpi, -float(TWO_PI), ALU.is_gt, ALU.mult, [("RR",)], [("KK",)])
    B.tt("dve", RR[:], RR[:], KK[:], ALU.add, [("RR",), ("KK",)], [("RR",)])
    B.ts("dve", RR[:], RR[:], -fpi, fpi, ALU.max, ALU.min, [("RR",)], [("RR",)])
    B.act(PWR[:], RR[:], AF.Sin, [("RR",)], [("PWR",)])
    B.tt("dve", PWR[:], PWR[:], MAG[:], ALU.mult, [("PWR",), ("MAG",)], [("PWR",)])
    B.tt("dve", PWI[:], PWI[:], MAG[:], ALU.mult, [("PWI",), ("MAG",)], [("PWI",)])
    i1 = S5_IDX[1]
    B.tt("dve", zt[0][:], LRr[:], LRr[:], ALU.mult, [("LRr",)], [("zt", 0)])
    B.tt("dve", zt[1][:], LIr[:], LIr[:], ALU.mult, [("LIr",)], [("zt", 1)])
    B.tt("dve", zt[0][:], zt[0][:], zt[1][:], ALU.add, [("zt", 0), ("zt", 1)], [("zt", 0)])
    B.P.add("dve", (lambda e: e.reciprocal(out=zt[0][:], in_=zt[0][:])), [("zt", 0)], [("zt", 0)])
    B.ts("dve", zt[1][:], PWR[:, i1, :], -1.0, None, ALU.add, None, [("PWR",)], [("zt", 1)])
    B.tt("dve", zt[2][:], zt[1][:], LRr[:], ALU.mult, [("zt", 1), ("LRr",)], [("zt", 2)])
    B.tt("dve", zt[3][:], PWI[:, i1, :], LIr[:], ALU.mult, [("PWI",), ("LIr",)], [("zt", 3)])
    B.tt("dve", zt[2][:], zt[2][:], zt[3][:], ALU.add, [("zt", 2), ("zt", 3)], [("zt", 2)])
    B.tt("dve", zr[:], zt[2][:], zt[0][:], ALU.mult, [("zt", 2), ("zt", 0)], [("zr",)])
    B.tt("dve", zt[2][:], PWI[:, i1, :], LRr[:], ALU.mult, [("PWI",), ("LRr",)], [("zt", 2)])
    B.tt("dve", zt[3][:], zt[1][:], LIr[:], ALU.mult, [("zt", 1), ("LIr",)], [("zt", 3)])
    B.tt("dve", zt[2][:], zt[2][:], zt[3][:], ALU.subtract, [("zt", 2), ("zt", 3)], [("zt", 2)])
    B.tt("dve", zi[:], zt[2][:], zt[0][:], ALU.mult, [("zt", 2), ("zt", 0)], [("zi",)])
    for k in range(10):
        ik = S5_IDX[8 << k]
        B.cp("dve", ARHS[:, k, :], PWR[:, ik, :], [("PWR",)], [("ARHS", k)])
        B.cp("dve", S2HS[0:64, k, :], PWI[0:64, ik, :], [("PWI",)], [("S2HS", k, 0)])
        B.ts("dve", S2HS[64:128, k, :], PWI[64:128, ik, :], -1.0, None, ALU.mult, None, [("PWI",)], [("S2HS", k, 1)])

    BR2q = [sa_("BR2_%d" % i, [128, 16, 16], F32) for i in range(4)]
    BI2q = [sa_("BI2_%d" % i, [128, 16, 16], F32) for i in range(4)]
    Gqq = [sa_("Gq_%d" % i, [128, 16, 16], F32) for i in range(4)]
    Cnq = [sa_("Cn_%d" % i, [128, 2, 2, 128], F32) for i in range(4)]
    ZB = sa_("ZB", [128, 16, 16], F32)
    ZBs = sa_("ZBs", [128, 16, 16], F32)
    tq = [sa_("tq%d" % i, [128, 16, 16], F32) for i in range(4)]
    Xt = sa_("Xt", [128, 16, 8, 16], F32)
    BzT = sa_("BzT", [128, 16, 8, 16], F32)
    YD = sa_("YD", [128, 16, 9, 16], F32)
    CA = sa_("CA", [128, 16, 16], F32)
    CB = sa_("CB", [128, 16, 16], F32)
    tmpS = sa_("tmpS", [128, 4, 128], F32)

    def bc(ap2):
        return ap2.unsqueeze(2).broadcast_to([128, 16, 16])

    for q in range(4):
        g0 = 16 * q
        BR2, BI2, Gq, Cn = BR2q[q], BI2q[q], Gqq[q], Cnq[q]
        bre = W["s5_b_re"][0, g0:g0 + 16].rearrange("g p c -> p g c")
        bim = W["s5_b_im"][0, g0:g0 + 16].rearrange("g p c -> p g c")
        B.dma("sp", BR2[0:64], bre, [], [("BR2", q, 0)], SU)
        B.dma("sp", BR2[64:128], bim, [], [("BR2", q, 1)], SU)
        B.dma("sp", BI2[0:64], bim, [], [("BI2", q, 0)], SU)
        B.dma("sp", BI2[64:128], bre, [], [("BI2", q, 1)], SU)
        B.dma("sp", Gq[:].rearrange("p g c -> p (g c)"), W["norm_mix"][0, g0 * 16:(g0 + 16) * 16].partition_broadcast(128), [], [("Gq", q)], SU)
        B.ts("dve", BI2[0:64], BI2[0:64], -1.0, None, ALU.mult, None, [("BI2", q, 0)], [("BI2", q, 0)])
        zrb, zib = bc(zr[:, g0:g0 + 16]), bc(zi[:, g0:g0 + 16])
        B.tt("dve", tq[0][:], BR2[:], zrb, ALU.mult, [("BR2", q), ("zr",)], [("tq", 0)])
        B.tt("dve", tq[1][:], BI2[:], zib, ALU.mult, [("BI2", q), ("zi",)], [("tq", 1)])
        B.tt("dve", tq[0][:], tq[0][:], tq[1][:], ALU.add, [("tq", 0), ("tq", 1)], [("tq", 0)])
        B.tt("dve", ZB[:], tq[0][:], Gq[:], ALU.mult, [("tq", 0), ("Gq", q)], [("ZB",)])
        B.tt("dve", tq[2][:], BI2[:], zrb, ALU.mult, [("BI2", q), ("zr",)], [("tq", 2)])
        B.tt("dve", tq[3][:], BR2[:], zib, ALU.mult, [("BR2", q), ("zi",)], [("tq", 3)])
        B.tt("dve", tq[2][:], tq[2][:], tq[3][:], ALU.subtract, [("tq", 2), ("tq", 3)], [("tq", 2)])
        B.tt("dve", ZBs[:], tq[2][:], Gq[:], ALU.mult, [("tq", 2), ("Gq", q)], [("ZBs",)])
        for s in range(8):
            for (dst, dkey, m) in ((Xt, "Xt", -s), (BzT, "BzT", 7 - s)):
                im = S5_IDX[m]
                B.tt("dve", tq[0][:], ZB[:], bc(PWR[:, im, g0:g0 + 16]), ALU.mult, [("ZB",), ("PWR",)], [("tq", 0)])
                B.tt("pool", tq[1][:], ZBs[:], bc(PWI[:, im, g0:g0 + 16]), ALU.mult, [("ZBs",), ("PWI",)], [("tq", 1)])
                B.tt("dve", dst[:, :, s, :], tq[0][:], tq[1][:], ALU.add, [("tq", 0), ("tq", 1)], [(dkey, s)])
        cre = W["s5_c_re"][0].rearrange("g c p -> (g c) p")
        cim = W["s5_c_im"][0].rearrange("g c p -> (g c) p")
        for t in range(2):
            r0 = (g0 + 8 * t) * 16
            B.dma("sp", Cn[:, t, 0, 0:64], cre[r0:r0 + 128, :], [], [("Cn", q, t, 0, 0)], SU)
            B.dma("sp", Cn[:, t, 0, 64:128], cim[r0:r0 + 128, :], [], [("Cn", q, t, 0, 1)], SU)
            B.dma("sp", Cn[:, t, 1, 0:64], cim[r0:r0 + 128, :], [], [("Cn", q, t, 1, 0)], SU)
            B.dma("sp", Cn[:, t, 1, 64:128], cre[r0:r0 + 128, :], [], [("Cn", q, t, 1, 1)], SU)
        B.tr([(PS[1][:, (2 * v + t) * 128:(2 * v + t + 1) * 128], Cn[:, t, v, :], identf[:]) for v in range(2) for t in range(2)],
             [("Cn", q), ("c", "identf")], [psk(1)])
        caf = CA[:].rearrange("p g c -> p (g c)")
        cbf = CB[:].rearrange("p g c -> p (g c)")
        B.cp("dve", caf[0:64, :], PS[1][0:64, 0:256], [psk(1)], [("CA", 0)])
        B.ts("dve", caf[64:128, :], PS[1][64:128, 0:256], -1.0, None, ALU.mult, None, [psk(1)], [("CA", 1)])
        B.ts("dve", cbf, PS[1][:, 256:512], -1.0, None, ALU.mult, None, [psk(1)], [("CB",)])
        for k in range(9):
            ik = S5_IDX[k]
            B.tt("dve", tq[2][:], CA[:], bc(PWR[:, ik, g0:g0 + 16]), ALU.mult, [("CA",), ("PWR",)], [("tq", 2)])
            B.tt("pool", tq[3][:], CB[:], bc(PWI[:, ik, g0:g0 + 16]), ALU.mult, [("CB",), ("PWI",)], [("tq", 3)])
            B.tt("dve", YD[:, :, k, :], tq[2][:], tq[3][:], ALU.add, [("tq", 2), ("tq", 3)], [("YD", k)])
        for g4 in range(4):
            gl0 = 4 * g4
            B.mm([(PS[2][:, j * 128:(j + 1) * 128], Xt[:, gl0 + j].rearrange("p s c -> p (s c)"),
                   YD[:, gl0 + j, 0:8, :].rearrange("p k c -> p (k c)"), True, True) for j in range(4)], [("Xt",), ("YD",)], [psk(2)])
            B.tt("dve", tmpS[:], PS[2][:].rearrange("p (j n) -> p j n", j=4), Cs["masks"][:].unsqueeze(1).broadcast_to([128, 4, 128]), ALU.mult,
                 [psk(2), ("c", "masks")], [("tmpS",)])
            for j in range(4):
                g = g0 + gl0 + j
                B.stt("dve", Sp[:, g, :], identf[:], dgcol[:, g:g + 1], tmpS[:, j, :], ALU.mult, ALU.add, [("c", "identf"), ("dgcol",), ("tmpS",)], [("Sp", g)])
            B.tr([(PS[3][:, j * 128:(j + 1) * 128], BzT[:, gl0 + j].rearrange("p s c -> p (s c)"), identf[:]) for j in range(4)],
                 [("BzT",), ("c", "identf")], [psk(3)])
            B.cp("act", Bz[:, g0 + gl0:g0 + gl0 + 4, :].rearrange("p g n -> p (g n)"), PS[3][:], [psk(3)], [("Bz", g0 + gl0)])
        B.cp("act", Dm[:, g0:g0 + 16, :].rearrange("p g (k c) -> p g k c", k=8), YD[:, :, 1:9, :], [("YD",)], [("Dm", q)])
    P.barrier()
    stA.close()

    stM = ExitStack()
    sm_ = lambda n, s, d: B.sb(stM, n, s, d)
    xo = sm_("xo", [128, 8, 8, 128], F32)
    uo = sm_("uo", [128, 8, 8, 128], BF16)
    Ug = [sm_("Ug%d" % i, [128, 1024], BF16) for i in range(2)]
    Hb = [[sm_("H%d%d" % (i, j), [128, 1024], BF16) for j in range(2)] for i in range(2)]
    Lt = [sm_("Lt%d" % i, [128, 128], BF16) for i in range(2)]
    La = [sm_("La%d" % i, [128, 128], F32) for i in range(2)]
    Lb = [sm_("Lb%d" % i, [128, 128], F32) for i in range(2)]
    y8o = sm_("y8o", [128, 5, 8, 128], F32)
    xi = [sm_("xi%d" % i, [128, D], F32) for i in range(3)]
    ysi = [sm_("ysi%d" % i, [128, D], F32) for i in range(3)]
    junk = sm_("junkS", [128, D], BF16)
    t1e = [sm_("t1e%d" % i, [128, D], F32) for i in range(2)]
    gq = [sm_("gq%d" % i, [128, D], BF16) for i in range(2)]
    gTi = [sm_("gTi%d" % i, [128, 8, 128], BF16) for i in range(2)]
    sg = [sm_("sg%d" % i, [128, 512], F32) for i in range(2)]
    xlv = xl_d.rearrange("(b j i) f -> b i j f", b=8, j=128, i=8)
    NB = B.cfg.get("s5_blocks", 8)

    n = 0
    for b in range(8):
        for i in range(8):
            s = n % 3
            n += 1
            B.dma("sp", xi[s][:], xlv[b, i], [], [("xi", s)], ("xi", s))
            B.act(junk[:], xi[s][:], AF.Square, [("xi", s)], [("junkS",), ("SSQ", b, i)], accum=SSQ[:, b, i:i + 1])
    B.ts("dve", SSQ[:], SSQ[:], 1.0 / D, EPS, ALU.mult, ALU.add, [("SSQ",)], [("SSQ",)])
    B.tt("pool", RSTD[:].rearrange("p b i -> p (b i)"), SSQ[:].rearrange("p b i -> p (b i)"), Cs["mhalf"][:, 0:1].broadcast_to([128, 64]), ALU.pow,
         [("SSQ",), ("c", "mhalf")], [("RSTD",)])

    xov = xl_d.rearrange("(b j i) f -> j b i f", b=8, j=128, i=8)
    ysv = ys_d.rearrange("(b j i) f -> j b i f", b=5, j=128, i=8)
    ev = [0]

    def evac_eng():
        ev[0] += 1
        return ("act", "dve")[ev[0] % 2]

    for o in range(8):
        P.dma_batch([B.dma("sp", xo[:, b], xov[:, b, :, o * 128:(o + 1) * 128], [], [("xo", b)], ("xo",)) for b in range(8)])
        for b in range(8):
            B.tt(("dve", "pool")[b % 2], uo[:, b].rearrange("p gl (s c) -> p s gl c", s=8, c=16),
                 xo[:, b].rearrange("p s (gl c) -> p s gl c", gl=8, c=16),
                 RSTD[:, b, :].unsqueeze(2).unsqueeze(3).broadcast_to([128, 8, 8, 16]), ALU.mult,
                 [("xo", b), ("RSTD",)], [("uo", b)])
        for gp in range(4):
            grp = [(2 * gp + z, 8 * o + 2 * gp + z, z) for z in range(2)]
            cur = {}
            for (gl, g, z) in grp:
                ptv = PS[6 + z][:].bitcast(BF16)
                B.tr([(ptv[:, b * 128:(b + 1) * 128], uo[:, b, gl, :], identb[:]) for b in range(8)], [("uo",), ("c", "identb")], [psk(6 + z)])
                B.cp(evac_eng(), Ug[z][:], ptv, [psk(6 + z)], [("Ug", z)])
                for h in range(2):
                    B.mm([(PS[2 * z + h][:], Bz[:, g, :], Ug[z][:, h * 512:(h + 1) * 512], True, True)], [("Bz",), ("Ug", z)], [psk(2 * z + h)])
                    B.cp(evac_eng(), Hb[z][0][:, h * 512:(h + 1) * 512], PS[2 * z + h][:], [psk(2 * z + h)], [("H", z, 0, h)])
                cur[z] = 0
            for k in range(10):
                sft = 1 << k
                for (gl, g, z) in grp:
                    B.ts("pool", La[z][:], Cs["pswap"][:], S2HS[:, k, g:g + 1], None, ALU.mult, None, [("c", "pswap"), ("S2HS",)], [("La", z)])
                    B.ts("pool", Lb[z][:], identf[:], ARHS[:, k, g:g + 1], None, ALU.mult, None, [("c", "identf"), ("ARHS",)], [("Lb", z)])
                    B.tt("pool", Lt[z][:], La[z][:], Lb[z][:], ALU.add, [("La", z), ("Lb", z)], [("Lt", z)])
                    src, dst = Hb[z][cur[z]], Hb[z][1 - cur[z]]
                    for h in range(2):
                        c0 = h * 512
                        lo = max(sft, c0)
                        items = []
                        has_shift = lo < c0 + 512
                        items.append((PS[2 * z + h][:], identb[:], src[:, c0:c0 + 512], True, not has_shift))
                        if has_shift:
                            items.append((PS[2 * z + h][:, lo - c0:512], Lt[z][:], src[:, lo - sft:c0 + 512 - sft], False, True))
                        B.mm(items, [("c", "identb"), ("Lt", z), ("H", z, cur[z])], [psk(2 * z + h)])
                        B.cp(evac_eng(), dst[:, c0:c0 + 512], PS[2 * z + h][:], [psk(2 * z + h)], [("H", z, 1 - cur[z], h)])
                    cur[z] = 1 - cur[z]
            for (gl, g, z) in grp:
                Hf = Hb[z][cur[z]]
                items = []
                for b in range(BL0, 8):
                    o_ = PS[4][:, (b - BL0) * 128:(b - BL0 + 1) * 128] if b < 7 else PS[5][:, 0:128]
                    items.append((o_, Ug[z][:, b * 128:(b + 1) * 128], Sp[:, g, :], True, False))
                    items.append((o_, Hf[:, b * 128 - 1:b * 128 + 127], Dm[:, g, :], False, True))
                B.mm(items, [("Ug", z), ("Sp",), ("Dm",), ("H", z, cur[z])], [psk(4), psk(5)])
                B.cp(evac_eng(), y8o[:, 0:4, :, gl * 16:(gl + 1) * 16], PS[4][:].rearrange("p (b i c) -> p b i c", b=4, i=8), [psk(4)], [("y8o", gl, 0)])
                B.cp(evac_eng(), y8o[:, 4, :, gl * 16:(gl + 1) * 16], PS[5][:, 0:128].rearrange("p (i c) -> p i c", i=8), [psk(5)], [("y8o", gl, 1)])
        P.dma_batch([B.dma("sp", ysv[:, b, :, o * 128:(o + 1) * 128], y8o[:, b], [("y8o",)], [("ys", o, b)], ("ysst",)) for b in range(5)])

    ysr = ys_d.rearrange("(b j i) f -> b i j f", b=5, j=128, i=8)
    n = 0
    for bq in range(5):
        b = BL0 + bq
        for i in range(8):
            s = n % 3
            s2 = n % 2
            n += 1
            B.dma("sp", xi[s][:], xlv[b, i], [], [("xi", s)], ("xi", s))
            B.dma("sp", ysi[s][:], ysr[bq, i], [("ys",)], [("ysi", s)], ("ysi", s))
            y_ = ysi[s]
            t1 = t1e[s2]
            B.act(t1[:], y_[:], AF.Square, [("ysi", s)], [("t1e", s2)])
            B.ts("dve", t1[:], t1[:], 0.044715, 1.0, ALU.mult, ALU.add, [("t1e", s2)], [("t1e", s2)])
            B.tt("pool", t1[:], t1[:], y_[:], ALU.mult, [("t1e", s2), ("ysi", s)], [("t1e", s2)])
            B.act(t1[:], t1[:], AF.Sigmoid, [("t1e", s2)], [("t1e", s2)], scale=1.5957691216057308)
            B.tt("dve", gq[s2][:], y_[:], t1[:], ALU.mult, [("ysi", s), ("t1e", s2)], [("gq", s2)])
            ptv = PS[6 + s2][:].bitcast(BF16)
            B.tr([(ptv[:, k * 128:(k + 1) * 128], gq[s2][:, k * 128:(k + 1) * 128], identb[:]) for k in range(8)], [("gq", s2), ("c", "identb")], [psk(6 + s2)])
            B.cp("act", gTi[s2][:].rearrange("p k j -> p (k j)"), ptv, [psk(6 + s2)], [("gTi", s2)])
            for half in range(2):
                pb = 2 * s2 + half
                items = [(PS[pb][:], gTi[s2][:, k, :], wglu[:, k, half * 512:(half + 1) * 512], k == 0, False) for k in range(8)]
                items.append((PS[pb][:], Cs["onesb"][0:1, :], bglu16[0:1, half * 512:(half + 1) * 512], False, True))
                B.mm(items, [("gTi", s2), ("wglu",), ("c", "onesb"), ("bglu",)], [psk(pb)])
                B.act(sg[half][:], PS[pb][:], AF.Sigmoid, [psk(pb)], [("sg", half)])
                B.tt("dve", sg[half][:], sg[half][:], gq[s2][:, half * 512:(half + 1) * 512], ALU.mult, [("sg", half), ("gq", s2)], [("sg", half)])
                xh = xi[s][:, half * 512:(half + 1) * 512]
                B.tt("pool", xh, xh, sg[half][:], ALU.add, [("sg", half), ("xi", s, half)], [("xi", s, half)])
            if b == BL0:
                dst = x1_d[0:512].rearrange("(j i) f -> i j f", i=8)[i]
                B.dma("sp", dst, xi[s][64:128, :], [("xi", s)], [("x1s", b, i)], ("x1st", s))
            else:
                r0 = b * 1024 - 3584
                dst = x1_d[r0:r0 + 1024].rearrange("(j i) f -> i j f", i=8)[i]
                B.dma("sp", dst, xi[s][:], [("xi", s)], [("x1s", b, i)], ("x1st", s))
    P.barrier()
    stM.close()
    st.close()


_NC_CACHE = {}


def _get_nc(cfg_key, cfg):
    if cfg_key not in _NC_CACHE:
        _NC_CACHE[cfg_key] = build(cfg)
    return _NC_CACHE[cfg_key]


def make_in_maps(inputs, cfg):
    consts = host_consts()
    x = np.asarray(inputs["x"], np.float32)
    pos = np.asarray(inputs["positions"], np.int32)
    maps = []
    for c in range(8):
        b, half = c // 2, c % 2
        m = {}
        for k in WEIGHT_SPECS:
            m[k] = np.ascontiguousarray(np.asarray(inputs[k], np.float32))
        for k, v in consts.items():
            m[k] = v
        m["cflag"] = np.full((128, 1), float(half), np.float32)
        posl = np.zeros((9 * TT,), np.int32)
        if half == 1:
            posl[:] = pos[b, 3584:8192]
        else:
            posl[TT:] = pos[b, 0:4096]
        m["posl"] = posl
        if cfg.get("s5", True):
            xl = np.zeros((8192, D), np.float32)
            if half == 1:
                xl[:] = x[b]
            else:
                xl[4096:] = x[b, 0:4096]
            m["xl"] = xl
        maps.append(m)
    return maps


def kernel(**inputs):
    cfg = {}
    nc = _get_nc("full", cfg)
    maps = make_in_maps(inputs, cfg)
    res = run_bass_kernel_spmd(nc, maps, core_ids=list(range(8)))
    out = np.zeros((4, 8192, D), np.float32)
    for c in range(8):
        b, half = c // 2, c % 2
        out[b, half * 4096:(half + 1) * 4096] = res.results[c]["y"]
    return out
```

```python
import numpy as np
import ml_dtypes
from contextlib import ExitStack
import concourse.bass as bass
import concourse.mybir as mybir
from concourse.bass_utils import run_bass_kernel_spmd

F32 = mybir.dt.float32
BF16 = mybir.dt.bfloat16
I32 = mybir.dt.int32
AF = mybir.ActivationFunctionType
ALU = mybir.AluOpType

D = 1024
DFF = 2816
NCH = 22
TT = 512
EPS = 1e-5
ENGS = ("pe", "act", "dve", "pool", "sp")
STRICT = False


class Op:
    __slots__ = ("eng", "emit", "deps", "idx", "eidx", "signal", "semval", "dma", "dma_cnt", "waits")


class Prog:
    def __init__(self, nc):
        self.nc = nc
        self.ops = []
        self.eng_ops = {e: [] for e in ENGS}
        self.state = {}
        self.by_head = {}
        self.dma_counts = {}
        self.pending_dma = []

    def _conf(self, key):
        for k in self.by_head.get(key[0], ()):
            n = min(len(k), len(key))
            if k[:n] == key[:n]:
                yield k

    def _get(self, key):
        st = self.state.get(key)
        if st is None:
            st = [None, {}, []]
            self.state[key] = st
            self.by_head.setdefault(key[0], []).append(key)
        return st

    def add(self, eng, emit, reads=(), writes=(), dma=None, extra_deps=None):
        op = Op()
        op.eng, op.emit, op.dma, op.signal, op.semval, op.waits = eng, emit, dma, False, 0, []
        op.idx = len(self.ops)
        op.eidx = len(self.eng_ops[eng])
        deps = {}
        reads = [r if isinstance(r, tuple) else (r,) for r in reads]
        writes = [w if isinstance(w, tuple) else (w,) for w in writes]
        for r in reads:
            for k in self._conf(r):
                w = self.state[k][0]
                if w is not None:
                    deps[w] = True
        for w_ in writes:
            for k in self._conf(w_):
                st = self.state[k]
                if st[0] is not None:
                    deps.setdefault(st[0], False)
                for ro in st[1].values():
                    deps.setdefault(ro, False)
                for ro in st[2]:
                    deps.setdefault(ro, False)
        for r in reads:
            st = self._get(r)
            if dma is not None:
                st[2].append(op)
            else:
                st[1][eng] = op
        for w_ in writes:
            for k in list(self._conf(w_)):
                if len(k) > len(w_):
                    del self.state[k]
                    self.by_head[k[0]].remove(k)
            st = self._get(w_)
            st[0], st[1], st[2] = op, {}, []
        if extra_deps:
            for p in extra_deps:
                deps.setdefault(p, False)
        deps.pop(op, None)
        op.deps = deps
        if dma is not None:
            c = self.dma_counts.get(dma, 0) + 1
            self.dma_counts[dma] = c
            op.dma_cnt = c
            self.pending_dma.append(op)
        self.ops.append(op)
        self.eng_ops[eng].append(op)
        return op

    def dma_batch(self, ops):
        c = max(o.dma_cnt for o in ops)
        for o in ops:
            o.dma_cnt = c

    def barrier(self):
        lasts = [self.eng_ops[e][-1] for e in ENGS if self.eng_ops[e]]
        pend = list(self.pending_dma)
        for e in ENGS:
            self.add(e, lambda eng: eng.nop(), extra_deps=lasts + pend)
        self.state = {}
        self.by_head = {}
        self.pending_dma = []

    def finalize_and_emit(self):
        nc = self.nc
        waited = {e: {} for e in ENGS}
        for op in self.ops:
            E = op.eng
            for p, is_raw in op.deps.items():
                if p.dma is not None:
                    k = ("dma", p.dma)
                    if p.dma[0] == "SETUP":
                        p.dma_cnt = self.dma_counts[p.dma]
                    if waited[E].get(k, 0) >= p.dma_cnt:
                        continue
                    waited[E][k] = p.dma_cnt
                    op.waits.append(p)
                elif p.eng == E:
                    if E == "sp":
                        continue
                    if STRICT:
                        if waited[E].get(E, -1) >= p.eidx:
                            continue
                        waited[E][E] = p.eidx
                        op.waits.append(p)
                        p.signal = True
                        continue
                    if E == "pe":
                        continue
                    if is_raw and (op.eidx - p.eidx) <= 1:
                        op.waits.append(p)
                        p.signal = True
                else:
                    if waited[E].get(p.eng, -1) >= p.eidx:
                        continue
                    waited[E][p.eng] = p.eidx
                    op.waits.append(p)
                    p.signal = True
        for e in ENGS:
            c = 0
            for op in self.eng_ops[e]:
                if op.signal:
                    c += 1
                    op.semval = c
        with ExitStack() as es:
            sems = {e: es.enter_context(nc.semaphore("sem_" + e)) for e in ENGS}
            dsems = {}
            for i, k in enumerate(self.dma_counts):
                dsems[k] = es.enter_context(nc.semaphore("dsem%d" % i))
            block = es.enter_context(nc.Block())

            def mk(e):
                def body(eng):
                    for op in self.eng_ops[e]:
                        for p in op.waits:
                            if p.dma is not None:
                                eng.wait_ge(dsems[p.dma], 16 * p.dma_cnt)
                            else:
                                eng.wait_ge(sems[p.eng], p.semval)
                        ins = op.emit(eng)
                        if op.dma is not None:
                            ins.then_inc(dsems[op.dma], 16)
                        elif op.signal:
                            ins.then_inc(sems[e], 1)
                return body

            block.tensor(mk("pe"))
            block.scalar(mk("act"))
            block.vector(mk("dve"))
            block.gpsimd(mk("pool"))
            block.sync(mk("sp"))


QHEADS = [(0, 4), (1, 5), (2, 6), (3, 7), (8, 12), (9, 13), (10, 14), (11, 15)]


def host_consts():
    c = {}
    c["identb"] = np.eye(128, dtype=np.float32).astype(ml_dtypes.bfloat16)
    c["identf"] = np.eye(128, dtype=np.float32)
    pm = np.zeros((128, 128), np.float32)
    for p in range(64):
        pm[p, p + 64] = 1.0
        pm[p + 64, p] = 1.0
    c["pswap"] = pm
    s = np.arange(128) // 16
    c["masks"] = (s[:, None] <= s[None, :]).astype(np.float32)
    q = np.arange(128)[:, None]
    k = np.arange(256)[None, :]
    valid = (k <= q + 128) & (q + 128 - k < 128)
    c["band"] = np.where(valid, 0.0, -1e30).astype(np.float32).astype(ml_dtypes.bfloat16)
    inv = 1.0 / np.power(np.float32(500000.0), np.arange(0, 16, 2, dtype=np.float32) / np.float32(16))
    col = np.zeros((128, 1), np.float32)
    jt = np.zeros((128, 128), np.float32)
    for p in range(128):
        d = p % 64
        if d < 16:
            col[p, 0] = inv[d % 8]
            if d < 8:
                jt[p + 8, p] = -1.0
            else:
                jt[p - 8, p] = 1.0
    c["invf"] = col
    c["jt"] = jt.astype(ml_dtypes.bfloat16)
    c["onesb"] = np.ones((1, 128), np.float32).astype(ml_dtypes.bfloat16)
    c["mhalf"] = np.full((128, 1), -0.5, np.float32)
    return c


CONST_SPECS = {
    "identb": ([128, 128], BF16), "identf": ([128, 128], F32), "pswap": ([128, 128], F32),
    "masks": ([128, 128], F32), "band": ([128, 256], BF16), "invf": ([128, 1], F32),
    "jt": ([128, 128], BF16), "onesb": ([1, 128], BF16), "mhalf": ([128, 1], F32),
}

WEIGHT_SPECS = {
    "norm_mix": [2, 1024], "norm_ffn": [2, 1024], "norm_final": [1024],
    "s5_lambda_re": [1, 64, 64], "s5_lambda_im": [1, 64, 64], "s5_log_dt": [1, 64],
    "s5_b_re": [1, 64, 64, 16], "s5_b_im": [1, 64, 64, 16], "s5_c_re": [1, 64, 16, 64], "s5_c_im": [1, 64, 16, 64],
    "s5_d": [1, 1024], "s5_w_glu": [1, 1024, 1024], "s5_b_glu": [1, 1024],
    "attn_w_qkv": [1, 1024, 1536], "attn_b_qkv": [1, 1536], "attn_sinks": [1, 16],
    "attn_w_o": [1, 1024, 1024], "attn_b_o": [1, 1024],
    "ffn_w_up": [2, 1024, 5632], "ffn_w_conv": [2, 3, 5632], "ffn_b_conv": [2, 5632], "ffn_w_down": [2, 2816, 1024],
}


AX = mybir.AxisListType
TWO_PI = 2.0 * np.pi
MAGIC = 12582912.0
CW1 = 6.28125
CW2 = float(np.float32(TWO_PI - 6.28125))
CW3 = float(TWO_PI - 6.28125 - np.float64(np.float32(TWO_PI - 6.28125)))


class Builder:
    def __init__(self, cfg):
        self.cfg = cfg
        self.nc = bass.Bass("TRN2", target_bir_lowering=False)
        self.P = Prog(self.nc)
        self.es = ExitStack()

    def din(self, name, shape, dt=F32):
        return self.nc.dram_tensor(name, list(shape), dt, kind="ExternalInput").ap()

    def dout(self, name, shape, dt=F32):
        return self.nc.dram_tensor(name, list(shape), dt, kind="ExternalOutput").ap()

    def dscr(self, name, shape, dt):
        return self.nc.dram_tensor(name, list(shape), dt).ap()

    def sb(self, stack, name, shape, dt):
        return stack.enter_context(self.nc.sbuf_tensor(name, list(shape), dt))

    def ps(self, stack, name, shape, dt):
        return stack.enter_context(self.nc.psum_tensor(name, list(shape), dt))

    def dma(self, q, out, in_, reads, writes, key, slow=False):
        if slow:
            f = lambda eng: eng.dma_start(out=out, in_=in_, allow_slow_non_contiguous=True)
        else:
            f = lambda eng: eng.dma_start(out=out, in_=in_)
        return self.P.add(q, f, reads=reads, writes=writes, dma=key)

    def act(self, out, in_, func, reads, writes, scale=None, bias=None, accum=None):
        kw = {}
        if scale is not None:
            kw["scale"] = scale
        if bias is not None:
            kw["bias"] = bias
        if accum is not None:
            kw["accum_out"] = accum
        return self.P.add("act", lambda e: e.activation(out=out, in_=in_, func=func, **kw), reads, writes)

    def ts(self, eng, out, in0, s1, s2, op0, op1, reads, writes):
        if s2 is None:
            f = lambda e: e.tensor_scalar(out=out, in0=in0, scalar1=s1, scalar2=None, op0=op0)
        else:
            f = lambda e: e.tensor_scalar(out=out, in0=in0, scalar1=s1, scalar2=s2, op0=op0, op1=op1)
        return self.P.add(eng, f, reads, writes)

    def stt(self, eng, out, in0, scalar, in1, op0, op1, reads, writes):
        return self.P.add(eng, lambda e: e.scalar_tensor_tensor(out=out, in0=in0, scalar=scalar, in1=in1, op0=op0, op1=op1), reads, writes)

    def tt(self, eng, out, in0, in1, op, reads, writes):
        return self.P.add(eng, lambda e: e.tensor_tensor(out=out, in0=in0, in1=in1, op=op), reads, writes)

    def cp(self, eng, out, in_, reads, writes):
        if eng == "act":
            return self.P.add("act", lambda e: e.copy(out=out, in_=in_), reads, writes)
        return self.P.add(eng, lambda e: e.tensor_copy(out=out, in_=in_), reads, writes)

    def memset(self, eng, ap, val, writes):
        return self.P.add(eng, lambda e: e.memset(ap, val), [], writes)

    def mm(self, items, reads, writes):
        def f(e):
            ins = None
            for (o, l, r, s0, s1) in items:
                ins = e.matmul(o, lhsT=l, rhs=r, start=s0, stop=s1)
            return ins
        return self.P.add("pe", f, reads, writes)

    def tr(self, items, reads, writes):
        def f(e):
            ins = None
            for (o, i_, idn) in items:
                ins = e.transpose(out=o, in_=i_, identity=idn)
            return ins
        return self.P.add("pe", f, reads, writes)


def psk(i):
    return ("ps", i)


def build(cfg):
    B = Builder(cfg)
    nc, P = B.nc, B.P
    NT = cfg.get("NT", 9)
    do_s5 = cfg.get("s5", True)
    layers = cfg.get("layers", ("ffn0", "att", "ffn1", "final"))

    W = {k: B.din(k, s) for k, s in WEIGHT_SPECS.items()}
    C = {k: B.din(k, s, dt) for k, (s, dt) in CONST_SPECS.items()}
    cflag_d = B.din("cflag", [128, 1])
    pos_d = B.din("posl", [9 * TT], I32)
    if do_s5:
        xl_d = B.din("xl", [8192, D])
        x1_d = B.dscr("x1s", [9 * TT, D], F32)
    else:
        xl_d = None
        x1_d = B.din("x1s", [9 * TT, D])
    out_d = B.dout("y", [8 * TT, D])
    dbg_d = B.dout("dbg", [9 * TT, D]) if cfg.get("dbg") else None
    wup_s = B.dscr("wup_s", [2, NCH, 128, 2 * 8 * 128], BF16)
    wdn_s = B.dscr("wdn_s", [2, 2, 11, 128, 2 * 512], BF16)
    wqk_s = B.dscr("wqk_s", [10, 128, 8 * 128], BF16)
    wv_s = B.dscr("wv_s", [128, 8 * 256], BF16)
    wo_s = B.dscr("wo_s", [2, 4, 128, 2 * 512], BF16)
    wglu_s = B.dscr("wglu_s", [8, 128, 1024], BF16)

    glob = B.es
    Cs = {}
    for k, (s, dt) in CONST_SPECS.items():
        Cs[k] = B.sb(glob, "c_" + k, s, dt)
        B.dma("sp", Cs[k][:], C[k], [], [("c", k)], ("SETUP", 0))
    cflag = B.sb(glob, "cflag_s", [128, 1], F32)
    B.dma("sp", cflag[:], cflag_d, [], [("c", "cflag")], ("SETUP", 0))
    PS = [B.ps(glob, "ps%d" % i, [128, 512], F32) for i in range(8)]

    with ExitStack() as st0:
        gcol = B.sb(st0, "gcol", [128, 4, 8], F32)
        for wi, src in enumerate((W["norm_ffn"][0], W["norm_ffn"][1], W["norm_mix"][1])):
            B.dma("sp", gcol[:, wi, :], src.rearrange("(k p) -> p k", p=128), [], [("gcol", wi)], ("SETUP", 0), slow=True)
        wraw = [B.sb(st0, "wraw%d" % i, [128, 5632], F32) for i in range(2)]
        wcv = [B.sb(st0, "wcv%d" % i, [128, 5632], BF16) for i in range(2)]
        cnt = [0]
        ei = [0]

        def nexti():
            i = cnt[0] % 2
            cnt[0] += 1
            return i

        def cast(i, n, gain_ap):
            e_ = ("act", "pool")[ei[0] % 2]
            ei[0] += 1
            if gain_ap is not None:
                if e_ == "act":
                    B.act(wcv[i][:, 0:n], wraw[i][:, 0:n], AF.Copy, [("wraw", i), ("gcol",)], [("wcv", i)], scale=gain_ap)
                else:
                    B.ts(e_, wcv[i][:, 0:n], wraw[i][:, 0:n], gain_ap, None, ALU.mult, None, [("wraw", i), ("gcol",)], [("wcv", i)])
            else:
                B.cp(e_, wcv[i][:, 0:n], wraw[i][:, 0:n], [("wraw", i)], [("wcv", i)])

        for l in range(2):
            if ("ffn%d" % l) not in layers:
                continue
            for k in range(8):
                i = nexti()
                B.dma("sp", wraw[i][:, :], W["ffn_w_up"][l, k * 128:(k + 1) * 128, :], [], [("wraw", i)], ("wraw", i))
                cast(i, 5632, gcol[:, l, k:k + 1])
                dst = wup_s[l].rearrange("m p (a k c) -> p a m k c", a=2, k=8, c=128)[:, :, :, k, :]
                B.dma("sp", dst, wcv[i][:, :].rearrange("p (a m c) -> p a m c", a=2, m=NCH, c=128), [("wcv", i)], [("wscr",)], ("wst", i))
            for mm in range(11):
                i = nexti()
                src = W["ffn_w_down"][l, mm * 256:(mm + 1) * 256, :].rearrange("(j p) n -> p j n", p=128)
                B.dma("sp", wraw[i][:, 0:2048].rearrange("p (j n) -> p j n", j=2), src, [], [("wraw", i)], ("wraw", i))
                cast(i, 2048, None)
                for half in range(2):
                    dst = wdn_s[l, half, mm].rearrange("p (j n) -> p j n", j=2)
                    srcv = wcv[i][:, 0:2048].rearrange("p (j n) -> p j n", j=2)[:, :, half * 512:(half + 1) * 512]
                    B.dma("sp", dst, srcv, [("wcv", i)], [("wscr",)], ("wst", i))
        if "att" in layers:
            for k in range(8):
                i = nexti()
                B.dma("sp", wraw[i][:, 0:1536], W["attn_w_qkv"][0, k * 128:(k + 1) * 128, :], [], [("wraw", i)], ("wraw", i))
                g_ap = gcol[:, 2, k:k + 1]
                for a in range(2):
                    srcv = wraw[i][:, a * 512:(a + 1) * 512].rearrange("p (j b d) -> p b j d", j=2, b=4, d=64)
                    dstv = wcv[i][:, a * 512:(a + 1) * 512].rearrange("p (b j d) -> p b j d", b=4, j=2, d=64)
                    B.ts("pool", dstv, srcv, g_ap, None, ALU.mult, None, [("wraw", i), ("gcol",)], [("wcv", i, "q%d" % a)])
                B.act(wcv[i][:, 1024:1536], wraw[i][:, 1024:1536], AF.Copy, [("wraw", i), ("gcol",)], [("wcv", i, "kv")], scale=g_ap)
                dstqk = wqk_s.rearrange("c p (k n) -> p c k n", k=8)[:, :, k, :]
                B.dma("sp", dstqk, wcv[i][:, 0:1280].rearrange("p (c n) -> p c n", n=128), [("wcv", i)], [("wscr",)], ("wst", i))
                dstv2 = wv_s.rearrange("p (k n) -> p k n", k=8)[:, k, :]
                B.dma("sp", dstv2, wcv[i][:, 1280:1536], [("wcv", i)], [("wscr",)], ("wst", i))
            for c in range(8):
                i = nexti()
                hA, hB = QHEADS[c]
                B.dma("sp", wraw[i][0:64, 0:1024], W["attn_w_o"][0, hA * 64:(hA + 1) * 64, :], [], [("wraw", i, "a")], ("wraw", i))
                B.dma("sp", wraw[i][64:128, 0:1024], W["attn_w_o"][0, hB * 64:(hB + 1) * 64, :], [], [("wraw", i, "b")], ("wrawb", i))
                cast(i, 1024, None)
                for half in range(2):
                    dst = wo_s[half, c // 2].rearrange("p (j n) -> p j n", j=2)[:, c % 2, :]
                    B.dma("sp", dst, wcv[i][:, half * 512:(half + 1) * 512], [("wcv", i)], [("wscr",)], ("wst", i))
        if do_s5:
            for k in range(8):
                i = nexti()
                B.dma("sp", wraw[i][:, 0:1024], W["s5_w_glu"][0, k * 128:(k + 1) * 128, :], [], [("wraw", i)], ("wraw", i))
                cast(i, 1024, None)
                B.dma("sp", wglu_s[k], wcv[i][:, 0:1024], [("wcv", i)], [("wscr",)], ("wst", i))
        P.barrier()

    if do_s5:
        emit_s5(B, W, Cs, PS, xl_d, x1_d, wglu_s, cflag)
        P.barrier()

    emit_phase2(B, W, Cs, PS, cflag, pos_d, x1_d, out_d, dbg_d, wup_s, wdn_s, wqk_s, wv_s, wo_s, NT, layers)
    P.barrier()
    P.finalize_and_emit()
    B.es.close()
    return nc


def emit_phase2(B, W, Cs, PS, cflag, pos_d, x1_d, out_d, dbg_d, wup_s, wdn_s, wqk_s, wv_s, wo_s, NT, layers):
    nc, P = B.nc, B.P
    st = ExitStack()
    sb = lambda name, shape, dt: B.sb(st, name, shape, dt)
    identb = Cs["identb"]

    xres = sb("xres", [128, 4, D], F32)
    hT = sb("hT", [128, 8, TT], BF16)
    gT = sb("gT", [128, NCH, TT], BF16)
    wup = [sb("wup%d" % i, [128, 2, 8, 128], BF16) for i in range(3)]
    wdn = [sb("wdn%d" % i, [128, 2, 512], BF16) for i in range(3)]
    uba = [sb("uba%d" % i, [128, TT + 2], F32) for i in range(2)]
    ubv = [sb("ubv%d" % i, [128, TT + 2], F32) for i in range(2)]
    ca = [sb("ca%d" % i, [128, TT], F32) for i in range(2)]
    cv = [sb("cv%d" % i, [128, TT], F32) for i in range(2)]
    sa = [sb("sa%d" % i, [128, TT], F32) for i in range(2)]
    halo = sb("halo", [128, 2, 2 * NCH, 2], F32)
    convw = sb("convw", [128, 2, 4, 2 * NCH], F32)
    cwnat = sb("cwnat", [2 * NCH, 2, 4, 128], F32)
    junk = sb("junk", [128, D], BF16)
    xn = [sb("xn%d" % i, [128, D], BF16) for i in range(2)]
    ss = sb("ss", [128, 4], F32)
    vv = sb("vv", [128, 4], F32)
    rstd = sb("rstd", [128, 4], F32)
    gfin = sb("gfin", [128, D], F32)
    otile = [sb("otile%d" % i, [128, D], F32) for i in range(2)]

    for l in range(2):
        for k in range(3):
            B.dma("sp", cwnat[:, l, k, :], W["ffn_w_conv"][l, k].rearrange("(c p) -> c p", p=128), [], [("cwnat", l, k)], ("SETUP", 2))
        B.dma("sp", cwnat[:, l, 3, :], W["ffn_b_conv"][l].rearrange("(c p) -> c p", p=128), [], [("cwnat", l, 3)], ("SETUP", 2))
    for l in range(2):
        B.tr([(PS[l][:, k * 64:k * 64 + 2 * NCH], cwnat[:, l, k, :], Cs["identf"][0:2 * NCH, 0:2 * NCH]) for k in range(4)],
             [("cwnat", l), ("c", "identf")], [psk(l)])
        B.cp("dve", convw[:, l, :, :], PS[l][:, 0:256].rearrange("p (k c) -> p k c", k=4)[:, :, 0:2 * NCH], [psk(l)], [("convw", l)])
    B.memset("pool", halo[:], 0.0, [("halo",)])
    B.dma("sp", gfin[:], W["norm_final"].partition_broadcast(128), [], [("gfin",)], ("SETUP", 2))

    def stats_rstd(warm):
        for sub in range(4):
            B.act(junk[:], xres[:, sub, :], AF.Square, [("xres", sub)], [("junk",), ("ss", sub)], accum=ss[:, sub:sub + 1])
        B.ts("dve", vv[:], ss[:], 1.0 / D, EPS, ALU.mult, ALU.add, [("ss",)], [("vv",)])
        B.tt("pool", rstd[:], vv[:], Cs["mhalf"][:, 0:1].broadcast_to([128, 4]), ALU.pow, [("vv",), ("c", "mhalf")], [("rstd",)])
        if warm:
            B.ts("pool", rstd[:], rstd[:], cflag[:, 0:1], None, ALU.mult, None, [("rstd",), ("c", "cflag")], [("rstd",)])

    def norm_to_hT(warm):
        stats_rstd(warm)
        for sub in range(4):
            i = sub % 2
            B.act(xn[i][:], xres[:, sub, :], AF.Copy, [("xres", sub), ("rstd",)], [("xn", i)], scale=rstd[:, sub:sub + 1])
            pb = 6 + i
            ptv = PS[pb][:].bitcast(BF16)
            B.tr([(ptv[:, k * 128:(k + 1) * 128], xn[i][:, k * 128:(k + 1) * 128], identb[:]) for k in range(8)],
                 [("xn", i), ("c", "identb")], [psk(pb)])
            B.cp("dve", hT[:, :, sub * 128:(sub + 1) * 128], ptv.rearrange("p (k t) -> p k t", k=8), [psk(pb)], [("hT", sub)])

    def ffn(l, down=True):
        for m in range(NCH):
            s3 = m % 3
            s2 = m % 2
            B.dma("sp", wup[s3][:].rearrange("p a k c -> p (a k c)"), wup_s[l, m], [("wscr",)], [("wup", s3)], ("wup", s3))
            for av, (ub, pbase, key) in enumerate(((uba, 0, "uba"), (ubv, 2, "ubv"))):
                pb = pbase + s2
                B.mm([(PS[pb][:], wup[s3][:, av, k, :], hT[:, k, :], k == 0, k == 7) for k in range(8)], [("wup", s3), ("hT",)], [psk(pb)])
                ch = av * NCH + m
                u = ub[s2]
                B.cp("pool", u[:, 0:2], halo[:, l, ch, :], [("halo", l, ch)], [(key, s2, "h")])
                B.cp("act", u[:, 2:TT + 2], PS[pb][:], [psk(pb)], [(key, s2, "m")])
                B.cp("pool", halo[:, l, ch, :], u[:, TT:TT + 2], [(key, s2, "m")], [("halo", l, ch)])
                dst = ca[s2] if av == 0 else cv[s2]
                dkey = ("ca", s2) if av == 0 else ("cv", s2)
                cw = [convw[:, l, k, ch:ch + 1] for k in range(4)]
                B.act(dst[:], PS[pb][:], AF.Identity, [psk(pb), ("convw", l)], [dkey], scale=cw[2], bias=cw[3])
                B.stt("dve", dst[:], u[:, 1:TT + 1], cw[1], dst[:], ALU.mult, ALU.add, [(key, s2), ("convw", l), dkey], [dkey])
                B.stt("dve", dst[:], u[:, 0:TT], cw[0], dst[:], ALU.mult, ALU.add, [(key, s2), ("convw", l), dkey], [dkey])
            B.act(sa[s2][:], ca[s2][:], AF.Silu, [("ca", s2)], [("sa", s2)])
            B.tt("dve", gT[:, m, :], sa[s2][:], cv[s2][:], ALU.mult, [("sa", s2), ("cv", s2)], [("gT", m)])
        if down:
            proj_tokmajor(gT, ("gT",), wdn_s[l], 11, None)

    wcount = [0]

    def proj_tokmajor(actT, actkey, wsrc, nmm, bias_row):
        for half in range(2):
            pbase = 4 if half == 0 else 0
            for mm in range(nmm):
                s3 = wcount[0] % 3
                wcount[0] += 1
                B.dma("sp", wdn[s3][:].rearrange("p j n -> p (j n)"), wsrc[half, mm], [("wscr",)], [("wdn", s3)], ("wdn", s3))
                items = []
                for sub in range(4):
                    for j in range(2):
                        first = (mm == 0 and j == 0)
                        last = (mm == nmm - 1 and j == 1 and bias_row is None)
                        items.append((PS[pbase + sub][:], actT[:, 2 * mm + j, sub * 128:(sub + 1) * 128], wdn[s3][:, j, :], first, last))
                B.mm(items, [actkey, ("wdn", s3)], [psk(pbase + sub) for sub in range(4)])
            if bias_row is not None:
                B.mm([(PS[pbase + sub][:], Cs["onesb"][0:1, :], bias_row[0:1, half * 512:(half + 1) * 512], False, True) for sub in range(4)],
                     [("c", "onesb"), ("bo",)], [psk(pbase + sub) for sub in range(4)])
            for sub in range(4):
                xs = xres[:, sub, half * 512:(half + 1) * 512]
                B.tt("dve", xs, xs, PS[pbase + sub][:], ALU.add, [psk(pbase + sub), ("xres", sub, half)], [("xres", sub, half)])

    def final_norm(t):
        stats_rstd(False)
        for sub in range(4):
            i = sub % 2
            B.stt("dve", otile[i][:], xres[:, sub, :], rstd[:, sub:sub + 1], gfin[:], ALU.mult, ALU.mult, [("xres", sub), ("rstd",), ("gfin",)], [("otile", i)])
            r0 = (t - 1) * TT + sub * 128
            B.dma("sp", out_d[r0:r0 + 128, :], otile[i][:], [("otile", i)], [("out", t, sub)], ("otile", i))

    attn = None
    if "att" in layers:
        attn = make_attention(B, W, Cs, PS, st, cflag, pos_d, wqk_s, wv_s, wo_s, xres, hT, norm_to_hT, proj_tokmajor)

    for t in range(NT):
        warm = (t == 0)
        B.dma("sp", xres[:], x1_d[t * TT:(t + 1) * TT, :].rearrange("(s p) f -> p s f", p=128), [("x1s",)], [("xres",)], ("xres",))
        if "ffn0" in layers:
            norm_to_hT(warm)
            ffn(0)
        if attn is not None:
            attn(t, warm)
        if "ffn1" in layers:
            norm_to_hT(warm)
            ffn(1, down=not warm)
        if dbg_d is not None:
            B.dma("sp", dbg_d[t * TT:(t + 1) * TT, :].rearrange("(s p) f -> p s f", p=128), xres[:], [("xres",)], [("dbg", t)], ("dbg",))
        if "final" in layers and not warm:
            final_norm(t)
    B.es.callback(st.close)


def make_attention(B, W, Cs, PS, st, cflag, pos_d, wqk_s, wv_s, wo_s, xres, hT, norm_to_hT, proj_tokmajor):
    sb = lambda name, shape, dt: B.sb(st, name, shape, dt)
    identb = Cs["identb"]
    qT = sb("qT", [128, 8, TT], BF16)
    kT = sb("kT", [128, 2, 128 + TT], BF16)
    Vb = sb("Vb", [128, 5, 256], BF16)
    oT = sb("oT", [128, 8, TT], BF16)
    qraw = [sb("qraw%d" % i, [128, TT], F32) for i in range(2)]
    qb16 = [sb("qb16%d" % i, [128, TT], BF16) for i in range(2)]
    rt1 = [sb("rt1%d" % i, [128, TT], F32) for i in range(2)]
    rt2 = [sb("rt2%d" % i, [128, TT], F32) for i in range(2)]
    cosT = sb("cosT", [128, TT], F32)
    sinT = sb("sinT", [128, TT], F32)
    posi = sb("posi", [128, TT], I32)
    angA = sb("angA", [128, TT], F32)
    angB = sb("angB", [128, TT], F32)
    angC = sb("angC", [128, TT], F32)
    wqk = [sb("wqk%d" % i, [128, 8, 128], BF16) for i in range(3)]
    wv = sb("wv", [128, 8, 256], BF16)
    bqk = sb("bqk", [128, 10], F32)
    bvb = sb("bvb", [128, 256], F32)
    bo32 = sb("bo32", [1, D], F32)
    bo16 = sb("bo16", [1, D], BF16)
    sinkb = sb("sinkb", [128, 16], F32)
    sinkp = sb("sinkp", [128, 8, 2], F32)
    nsinkp = sb("nsinkp", [128, 8, 2], F32)
    maskF = sb("maskF", [128, 256], BF16)
    mtmp = sb("mtmp", [128, 1], F32)
    Pexp = [sb("Pexp%d" % i, [128, 2, 256], BF16) for i in range(2)]
    sm = [sb("sm%d" % i, [128, 8, 2], F32) for i in range(2)]
    dgm = [sb("dgm%d" % i, [128, 2, 128], BF16) for i in range(2)]
    PTs = [sb("PTs%d" % i, [128, 4, 128], BF16) for i in range(2)]

    bq = W["attn_b_qkv"][0]
    for j in range(2):
        for a in range(2):
            B.dma("sp", bqk[64 * j:64 * j + 64, 4 * a:4 * a + 4],
                  bq[0:1024].rearrange("(a j b d) -> a j d b", a=2, j=2, b=4, d=64)[a, j], [], [("bqk", "q", j, a)], ("SETUP", 2), slow=True)
    B.dma("sp", bqk[:, 8:10], bq[1024:1280].rearrange("(c p) -> p c", p=128), [], [("bqk", "k")], ("SETUP", 2), slow=True)
    B.dma("sp", bvb[:], bq[1280:1536].partition_broadcast(128), [], [("bvb",)], ("SETUP", 2))
    B.dma("sp", bo32[:], W["attn_b_o"], [], [("bo32",)], ("SETUP", 2))
    B.cp("act", bo16[:], bo32[:], [("bo32",)], [("bo",)])
    B.dma("sp", sinkb[:], W["attn_sinks"][0].partition_broadcast(128), [], [("sinkb",)], ("SETUP", 2))
    B.cp("dve", sinkp[:].rearrange("p (a b) j -> p a b j", a=2), sinkb[:].rearrange("p (a j b) -> p a b j", a=2, j=2, b=4), [("sinkb",)], [("sinkp",)])
    B.ts("dve", nsinkp[:], sinkp[:], -1.0, None, ALU.mult, None, [("sinkp",)], [("nsinkp",)])
    B.ts("dve", mtmp[:], cflag[:], -1.0, 1e30, ALU.add, ALU.mult, [("c", "cflag")], [("mtmp",)])
    B.ts("dve", maskF[:, 0:128], Cs["band"][:, 0:128], mtmp[:, 0:1], None, ALU.add, None, [("c", "band"), ("mtmp",)], [("maskF", 0)])
    B.cp("dve", maskF[:, 128:256], Cs["band"][:, 128:256], [("c", "band")], [("maskF", 1)])
    B.memset("pool", kT[:], 0.0, [("kT",)])
    B.memset("pool", Vb[:], 0.0, [("Vb",)])
    cnt = [0]

    def attn(t, warm):
        norm_to_hT(warm)
        if B.cfg.get("att_stage", 9) < 0:
            return
        B.dma("sp", posi[:], pos_d[t * TT:(t + 1) * TT].partition_broadcast(128), [], [("posi",)], ("posi",))
        B.ts("dve", angA[:], posi[:], Cs["invf"][:, 0:1], None, ALU.mult, None, [("posi",), ("c", "invf")], [("angA",)])
        B.ts("dve", angB[:], angA[:], float(1.0 / TWO_PI), MAGIC, ALU.mult, ALU.add, [("angA",)], [("angB",)])
        B.ts("dve", angB[:], angB[:], -MAGIC, None, ALU.add, None, [("angB",)], [("angB",)])
        B.stt("dve", angC[:], angB[:], -CW1, angA[:], ALU.mult, ALU.add, [("angB",), ("angA",)], [("angC",)])
        B.stt("dve", angC[:], angB[:], -CW2, angC[:], ALU.mult, ALU.add, [("angB",), ("angC",)], [("angC",)])
        B.stt("dve", angC[:], angB[:], -CW3, angC[:], ALU.mult, ALU.add, [("angB",), ("angC",)], [("angC",)])
        B.ts("dve", angC[:], angC[:], -float(np.pi), float(np.pi), ALU.max, ALU.min, [("angC",)], [("angC",)])
        B.act(sinT[:], angC[:], AF.Sin, [("angC",)], [("sinT",)])
        B.ts("dve", angA[:], angC[:], float(np.pi / 2), None, ALU.add, None, [("angC",)], [("angA",)])
        B.ts("dve", angB[:], angA[:], float(np.pi), -float(TWO_PI), ALU.is_gt, ALU.mult, [("angA",)], [("angB",)])
        B.tt("dve", angA[:], angA[:], angB[:], ALU.add, [("angA",), ("angB",)], [("angA",)])
        B.ts("dve", angA[:], angA[:], -float(np.pi), float(np.pi), ALU.max, ALU.min, [("angA",)], [("angA",)])
        B.act(cosT[:], angA[:], AF.Sin, [("angA",)], [("cosT",)])
        if B.cfg.get("att_stage", 9) < 1:
            return
        for c in range(10):
            s3 = c % 3
            i = c % 2
            B.dma("sp", wqk[s3][:].rearrange("p k n -> p (k n)"), wqk_s[c], [("wscr",)], [("wqk", s3)], ("wqk", s3))
            B.mm([(PS[i][:], wqk[s3][:, k, :], hT[:, k, :], k == 0, k == 7) for k in range(8)], [("wqk", s3), ("hT",)], [psk(i)])
            B.act(qraw[i][:], PS[i][:], AF.Identity, [psk(i), ("bqk",)], [("qraw", i)], bias=bqk[:, c:c + 1])
            B.cp("dve", qb16[i][:], qraw[i][:], [("qraw", i)], [("qb16", i)])
            if B.cfg.get("qk_sub", 9) < 1:
                continue
            B.mm([(PS[2 + i][:], Cs["jt"][:], qb16[i][:], True, True)], [("c", "jt"), ("qb16", i)], [psk(2 + i)])
            B.tt("dve", rt1[i][:], PS[2 + i][:], sinT[:], ALU.mult, [psk(2 + i), ("sinT",)], [("rt1", i)])
            if B.cfg.get("qk_sub", 9) < 2:
                continue
            B.tt("dve", rt2[i][:], qraw[i][:], cosT[:], ALU.mult, [("qraw", i), ("cosT",)], [("rt2", i)])
            if c < 8:
                B.tt("dve", qT[:, c, :], rt1[i][:], rt2[i][:], ALU.add, [("rt1", i), ("rt2", i)], [("qT", c)])
            else:
                B.tt("dve", kT[:, c - 8, 128:128 + TT], rt1[i][:], rt2[i][:], ALU.add, [("rt1", i), ("rt2", i)], [("kT", c - 8, "cur")])
        if B.cfg.get("att_stage", 9) < 2:
            return
        B.dma("sp", wv[:].rearrange("p k n -> p (k n)"), wv_s, [("wscr",)], [("wv",)], ("wv",))
        for sub in range(4):
            pb = 4 + sub % 2
            B.mm([(PS[pb][:, 0:256], hT[:, k, sub * 128:(sub + 1) * 128], wv[:, k, :], k == 0, k == 7) for k in range(8)], [("wv",), ("hT",)], [psk(pb)])
            B.tt("dve", Vb[:, 1 + sub, :], PS[pb][:, 0:256], bvb[:], ALU.add, [psk(pb), ("bvb",)], [("Vb", 1 + sub)])
        if B.cfg.get("att_stage", 9) < 3:
            return
        for qb in range(4):
            msk = maskF if (t == 1 and qb == 0) else Cs["band"]
            mkey = ("maskF",) if (t == 1 and qb == 0) else ("c", "band")
            for c in range(8):
                a = c // 4
                u = cnt[0] % 2
                cnt[0] += 1
                sbk = u
                ptb = 2 + u
                ob = 4 + (c // 4)
                items = []
                for j in range(2):
                    r0 = 64 * j
                    o = PS[sbk][:, j * 256:(j + 1) * 256]
                    items.append((o, qT[r0:r0 + 64, c, qb * 128:(qb + 1) * 128], kT[r0:r0 + 64, a, qb * 128:qb * 128 + 256], True, False))
                    items.append((o, identb[:], msk[:], False, True))
                B.mm(items, [("qT", c), ("kT", a), ("c", "identb"), mkey], [psk(sbk)])
                S = sm[u]
                mx, nm, tsk, es, rs, den, rden = [S[:, r, :] for r in range(7)]
                B.P.add("dve", (lambda e, o_=mx, i_=PS[sbk][:].rearrange("p (j n) -> p j n", j=2): e.tensor_reduce(out=o_, in_=i_, axis=AX.X, op=ALU.max)),
                        [psk(sbk)], [("sm", u, 0)])
                B.stt("dve", nm, mx, -0.125, nsinkp[:, c, :], ALU.mult, ALU.min, [("sm", u, 0), ("nsinkp",)], [("sm", u, 1)])
                for j in range(2):
                    B.act(Pexp[u][:, j, :], PS[sbk][:, j * 256:(j + 1) * 256], AF.Exp, [psk(sbk), ("sm", u, 1)], [("Pexp", u, j), ("sm", u, 4, j)],
                          scale=0.125, bias=S[:, 1, j:j + 1], accum=S[:, 4, j:j + 1])
                B.tt("dve", tsk, nm, sinkp[:, c, :], ALU.add, [("sm", u, 1), ("sinkp",)], [("sm", u, 2)])
                B.act(es, tsk, AF.Exp, [("sm", u, 2)], [("sm", u, 3)])
                B.tt("dve", den, rs, es, ALU.add, [("sm", u, 4), ("sm", u, 3)], [("sm", u, 5)])
                B.P.add("dve", (lambda e, o_=rden, i_=den: e.reciprocal(out=o_, in_=i_)), [("sm", u, 5)], [("sm", u, 6)])
                for j in range(2):
                    B.act(dgm[u][:, j, :], identb[:], AF.Copy, [("c", "identb"), ("sm", u, 6)], [("dgm", u, j)], scale=S[:, 6, j:j + 1])
                items = []
                for j in range(2):
                    for kb in range(2):
                        items.append((PS[ptb][:, (2 * j + kb) * 128:(2 * j + kb + 1) * 128], Pexp[u][:, j, kb * 128:(kb + 1) * 128], dgm[u][:, j, :], True, True))
                B.mm(items, [("Pexp", u), ("dgm", u)], [psk(ptb)])
                B.cp("act", PTs[u][:].rearrange("p a q -> p (a q)"), PS[ptb][:], [psk(ptb)], [("PTs", u)])
                items = []
                for j in range(2):
                    kvh = 2 * a + j
                    for kb in range(2):
                        items.append((PS[ob][64 * j:64 * j + 64, (c % 4) * 128:(c % 4 + 1) * 128], Vb[:, qb + kb, kvh * 64:(kvh + 1) * 64],
                                      PTs[u][:, 2 * j + kb, :], kb == 0, kb == 1))
                B.mm(items, [("Vb", qb), ("Vb", qb + 1), ("PTs", u)], [("ps", ob, c % 4)])
                if c % 4 == 3:
                    B.cp("act", oT[:, 4 * a:4 * a + 4, qb * 128:(qb + 1) * 128], PS[ob][:].rearrange("p (c q) -> p c q", c=4), [psk(ob)], [("oT", a, qb)])
        if B.cfg.get("att_stage", 9) < 4:
            return
        proj_tokmajor(oT, ("oT",), wo_s, 4, bo16)
        B.cp("dve", kT[:, :, 0:128], kT[:, :, TT:TT + 128], [("kT",)], [("kT",)])
        B.cp("dve", Vb[:, 0, :], Vb[:, 4, :], [("Vb", 4)], [("Vb", 0)])

    return attn


S5_MS = list(range(-7, 9)) + [16 << k for k in range(9)]
S5_IDX = {m: i for i, m in enumerate(S5_MS)}
NPW = len(S5_MS)
BL0 = 3


def emit_s5(B, W, Cs, PS, xl_d, x1_d, wglu_s, cflag):
    P = B.P
    st = ExitStack()
    sb = lambda n, s, d: B.sb(st, n, s, d)
    SU = ("SETUP", 1)
    identb, identf = Cs["identb"], Cs["identf"]
    fpi = float(np.pi)

    Sp = sb("Sp", [128, 64, 128], BF16)
    Bz = sb("Bz", [128, 64, 128], BF16)
    Dm = sb("Dm", [128, 64, 128], BF16)
    ARHS = sb("ARHS", [128, 10, 64], F32)
    S2HS = sb("S2HS", [128, 10, 64], F32)
    RSTD = sb("RSTD", [128, 8, 8], F32)
    SSQ = sb("SSQ", [128, 8, 8], F32)
    wglu = sb("wglu", [128, 8, 1024], BF16)
    bglu16 = sb("bglu16", [1, D], BF16)
    dgcol = sb("dgcol", [128, 64], F32)
    ys_d = B.dscr("ys_s", [5 * 1024, D], F32)

    B.dma("sp", wglu[:], wglu_s.rearrange("k p n -> p k n"), [("wscr",)], [("wglu",)], SU)

    stA = ExitStack()
    sa_ = lambda n, s, d: B.sb(stA, n, s, d)
    natL = sa_("natL", [64, 2, 128], F32)
    bglu32 = sa_("bglu32", [1, D], F32)
    B.dma("sp", bglu32[:], W["s5_b_glu"], [], [("bglu32",)], SU)
    B.cp("act", bglu16[:], bglu32[:], [("bglu32",)], [("bglu",)])
    LRr = sa_("LRr", [128, 64], F32)
    LIr = sa_("LIr", [128, 64], F32)
    dtb = sa_("dtb", [128, 64], F32)
    LR = sa_("LR", [128, 64], F32)
    TH = sa_("TH", [128, 64], F32)
    ANG = sa_("ANG", [128, NPW, 64], F32)
    MAG = sa_("MAG", [128, NPW, 64], F32)
    KK = sa_("KK", [128, NPW, 64], F32)
    RR = sa_("RR", [128, NPW, 64], F32)
    PWR = sa_("PWR", [128, NPW, 64], F32)
    PWI = sa_("PWI", [128, NPW, 64], F32)
    zr = sa_("zr", [128, 64], F32)
    zi = sa_("zi", [128, 64], F32)
    zt = [sa_("zt%d" % i, [128, 64], F32) for i in range(4)]
    dcolS = sa_("dcolS", [128, 64], F32)
    gcolS = sa_("gcolS", [128, 64], F32)

    for w, key in enumerate(("s5_lambda_re", "s5_lambda_im")):
        for h in range(2):
            B.dma("sp", natL[:, w, h * 64:(h + 1) * 64], W[key][0], [], [("natL", w, h)], SU)
    B.dma("sp", dtb[:], W["s5_log_dt"][0].partition_broadcast(128), [], [("dtb",)], SU)
    for s in range(8):
        B.dma("sp", dcolS[16 * s:16 * s + 16, :], W["s5_d"][0].rearrange("(g c) -> c g", c=16), [], [("dcolS", s)], SU, slow=True)
        B.dma("sp", gcolS[16 * s:16 * s + 16, :], W["norm_mix"][0].rearrange("(g c) -> c g", c=16), [], [("gcolS", s)], SU, slow=True)
    B.tt("dve", dgcol[:], dcolS[:], gcolS[:], ALU.mult, [("dcolS",), ("gcolS",)], [("dgcol",)])
    B.tr([(PS[0][:, w * 64:(w + 1) * 64], natL[:, w, :], identf[0:64, 0:64]) for w in range(2)], [("natL",), ("c", "identf")], [psk(0)])
    B.cp("dve", LRr[:], PS[0][:, 0:64], [psk(0)], [("LRr",)])
    B.cp("dve", LIr[:], PS[0][:, 64:128], [psk(0)], [("LIr",)])
    B.act(dtb[:], dtb[:], AF.Exp, [("dtb",)], [("dtb",)])
    B.tt("dve", LR[:], LRr[:], dtb[:], ALU.mult, [("LRr",), ("dtb",)], [("LR",)])
    B.tt("dve", TH[:], LIr[:], dtb[:], ALU.mult, [("LIr",), ("dtb",)], [("TH",)])
    for i, m in enumerate(S5_MS):
        B.ts("dve", ANG[:, i, :], TH[:], float(m), None, ALU.mult, None, [("TH",)], [("ANG", i)])
        B.ts("pool", MAG[:, i, :], LR[:], float(m), None, ALU.mult, None, [("LR",)], [("MAG", i)])
    B.act(MAG[:], MAG[:], AF.Exp, [("MAG",)], [("MAG",)])
    B.ts("dve", KK[:], ANG[:], float(1.0 / TWO_PI), MAGIC, ALU.mult, ALU.add, [("ANG",)], [("KK",)])
    B.ts("dve", KK[:], KK[:], -MAGIC, None, ALU.add, None, [("KK",)], [("KK",)])
    B.stt("dve", RR[:], KK[:], -CW1, ANG[:], ALU.mult, ALU.add, [("KK",), ("ANG",)], [("RR",)])
    B.stt("dve", RR[:], KK[:], -CW2, RR[:], ALU.mult, ALU.add, [("KK",), ("RR",)], [("RR",)])
    B.stt("dve", RR[:], KK[:], -CW3, RR[:], ALU.mult, ALU.add, [("KK",), ("RR",)], [("RR",)])
    B.ts("dve", RR[:], RR[:], -fpi, fpi, ALU.max, ALU.min, [("RR",)], [("RR",)])
    B.act(PWI[:], RR[:], AF.Sin, [("RR",)], [("PWI",)])
    B.ts("dve", RR[:], RR[:], fpi / 2, None, ALU.add, None, [("RR",)], [("RR",)])
    B.ts("dve", KK[:], RR[:], fpi, -float(TWO_PI), ALU.is_gt, ALU.mult, [("RR",)], [("KK",)])
    B.tt("dve", RR[:], RR[:], KK[:], ALU.add, [("RR",), ("KK",)], [("RR",)])
    B.ts("dve", RR[:], RR[:], -fpi, fpi, ALU.max, ALU.min, [("RR",)], [("RR",)])
    B.act(PWR[:], RR[:], AF.Sin, [("RR",)], [("PWR",)])
    B.tt("dve", PWR[:], PWR[:], MAG[:], ALU.mult, [("PWR",), ("MAG",)], [("PWR",)])
    B.tt("dve", PWI[:], PWI[:], MAG[:], ALU.mult, [("PWI",), ("MAG",)], [("PWI",)])
    i1 = S5_IDX[1]
    B.tt("dve", zt[0][:], LRr[:], LRr[:], ALU.mult, [("LRr",)], [("zt", 0)])
    B.tt("dve", zt[1][:], LIr[:], LIr[:], ALU.mult, [("LIr",)], [("zt", 1)])
    B.tt("dve", zt[0][:], zt[0][:], zt[1][:], ALU.add, [("zt", 0), ("zt", 1)], [("zt", 0)])
    B.P.add("dve", (lambda e: e.reciprocal(out=zt[0][:], in_=zt[0][:])), [("zt", 0)], [("zt", 0)])
    B.ts("dve", zt[1][:], PWR[:, i1, :], -1.0, None, ALU.add, None, [("PWR",)], [("zt", 1)])
    B.tt("dve", zt[2][:], zt[1][:], LRr[:], ALU.mult, [("zt", 1), ("LRr",)], [("zt", 2)])
    B.tt("dve", zt[3][:], PWI[:, i1, :], LIr[:], ALU.mult, [("PWI",), ("LIr",)], [("zt", 3)])
    B.tt("dve", zt[2][:], zt[2][:], zt[3][:], ALU.add, [("zt", 2), ("zt", 3)], [("zt", 2)])
    B.tt("dve", zr[:], zt[2][:], zt[0][:], ALU.mult, [("zt", 2), ("zt", 0)], [("zr",)])
    B.tt("dve", zt[2][:], PWI[:, i1, :], LRr[:], ALU.mult, [("PWI",), ("LRr",)], [("zt", 2)])
    B.tt("dve", zt[3][:], zt[1][:], LIr[:], ALU.mult, [("zt", 1), ("LIr",)], [("zt", 3)])
    B.tt("dve", zt[2][:], zt[2][:], zt[3][:], ALU.subtract, [("zt", 2), ("zt", 3)], [("zt", 2)])
    B.tt("dve", zi[:], zt[2][:], zt[0][:], ALU.mult, [("zt", 2), ("zt", 0)], [("zi",)])
    for k in range(10):
        ik = S5_IDX[8 << k]
        B.cp("dve", ARHS[:, k, :], PWR[:, ik, :], [("PWR",)], [("ARHS", k)])
        B.cp("dve", S2HS[0:64, k, :], PWI[0:64, ik, :], [("PWI",)], [("S2HS", k, 0)])
        B.ts("dve", S2HS[64:128, k, :], PWI[64:128, ik, :], -1.0, None, ALU.mult, None, [("PWI",)], [("S2HS", k, 1)])

    BR2q = [sa_("BR2_%d" % i, [128, 16, 16], F32) for i in range(4)]
    BI2q = [sa_("BI2_%d" % i, [128, 16, 16], F32) for i in range(4)]
    Gqq = [sa_("Gq_%d" % i, [128, 16, 16], F32) for i in range(4)]
    Cnq = [sa_("Cn_%d" % i, [128, 2, 2, 128], F32) for i in range(4)]
    ZB = sa_("ZB", [128, 16, 16], F32)
    ZBs = sa_("ZBs", [128, 16, 16], F32)
    tq = [sa_("tq%d" % i, [128, 16, 16], F32) for i in range(4)]
    Xt = sa_("Xt", [128, 16, 8, 16], F32)
    BzT = sa_("BzT", [128, 16, 8, 16], F32)
    YD = sa_("YD", [128, 16, 9, 16], F32)
    CA = sa_("CA", [128, 16, 16], F32)
    CB = sa_("CB", [128, 16, 16], F32)
    tmpS = sa_("tmpS", [128, 4, 128], F32)

    def bc(ap2):
        return ap2.unsqueeze(2).broadcast_to([128, 16, 16])

    for q in range(4):
        g0 = 16 * q
        BR2, BI2, Gq, Cn = BR2q[q], BI2q[q], Gqq[q], Cnq[q]
        bre = W["s5_b_re"][0, g0:g0 + 16].rearrange("g p c -> p g c")
        bim = W["s5_b_im"][0, g0:g0 + 16].rearrange("g p c -> p g c")
        B.dma("sp", BR2[0:64], bre, [], [("BR2", q, 0)], SU)
        B.dma("sp", BR2[64:128], bim, [], [("BR2", q, 1)], SU)
        B.dma("sp", BI2[0:64], bim, [], [("BI2", q, 0)], SU)
        B.dma("sp", BI2[64:128], bre, [], [("BI2", q, 1)], SU)
        B.dma("sp", Gq[:].rearrange("p g c -> p (g c)"), W["norm_mix"][0, g0 * 16:(g0 + 16) * 16].partition_broadcast(128), [], [("Gq", q)], SU)
        B.ts("dve", BI2[0:64], BI2[0:64], -1.0, None, ALU.mult, None, [("BI2", q, 0)], [("BI2", q, 0)])
        zrb, zib = bc(zr[:, g0:g0 + 16]), bc(zi[:, g0:g0 + 16])
        B.tt("dve", tq[0][:], BR2[:], zrb, ALU.mult, [("BR2", q), ("zr",)], [("tq", 0)])
        B.tt("dve", tq[1][:], BI2[:], zib, ALU.mult, [("BI2", q), ("zi",)], [("tq", 1)])
        B.tt("dve", tq[0][:], tq[0][:], tq[1][:], ALU.add, [("tq", 0), ("tq", 1)], [("tq", 0)])
        B.tt("dve", ZB[:], tq[0][:], Gq[:], ALU.mult, [("tq", 0), ("Gq", q)], [("ZB",)])
        B.tt("dve", tq[2][:], BI2[:], zrb, ALU.mult, [("BI2", q), ("zr",)], [("tq", 2)])
        B.tt("dve", tq[3][:], BR2[:], zib, ALU.mult, [("BR2", q), ("zi",)], [("tq", 3)])
        B.tt("dve", tq[2][:], tq[2][:], tq[3][:], ALU.subtract, [("tq", 2), ("tq", 3)], [("tq", 2)])
        B.tt("dve", ZBs[:], tq[2][:], Gq[:], ALU.mult, [("tq", 2), ("Gq", q)], [("ZBs",)])
        for s in range(8):
            for (dst, dkey, m) in ((Xt, "Xt", -s), (BzT, "BzT", 7 - s)):
                im = S5_IDX[m]
                B.tt("dve", tq[0][:], ZB[:], bc(PWR[:, im, g0:g0 + 16]), ALU.mult, [("ZB",), ("PWR",)], [("tq", 0)])
                B.tt("pool", tq[1][:], ZBs[:], bc(PWI[:, im, g0:g0 + 16]), ALU.mult, [("ZBs",), ("PWI",)], [("tq", 1)])
                B.tt("dve", dst[:, :, s, :], tq[0][:], tq[1][:], ALU.add, [("tq", 0), ("tq", 1)], [(dkey, s)])
        cre = W["s5_c_re"][0].rearrange("g c p -> (g c) p")
        cim = W["s5_c_im"][0].rearrange("g c p -> (g c) p")
        for t in range(2):
            r0 = (g0 + 8 * t) * 16
            B.dma("sp", Cn[:, t, 0, 0:64], cre[r0:r0 + 128, :], [], [("Cn", q, t, 0, 0)], SU)
            B.dma("sp", Cn[:, t, 0, 64:128], cim[r0:r0 + 128, :], [], [("Cn", q, t, 0, 1)], SU)
            B.dma("sp", Cn[:, t, 1, 0:64], cim[r0:r0 + 128, :], [], [("Cn", q, t, 1, 0)], SU)
            B.dma("sp", Cn[:, t, 1, 64:128], cre[r0:r0 + 128, :], [], [("Cn", q, t, 1, 1)], SU)
        B.tr([(PS[1][:, (2 * v + t) * 128:(2 * v + t + 1) * 128], Cn[:, t, v, :], identf[:]) for v in range(2) for t in range(2)],
             [("Cn", q), ("c", "identf")], [psk(1)])
        caf = CA[:].rearrange("p g c -> p (g c)")
        cbf = CB[:].rearrange("p g c -> p (g c)")
        B.cp("dve", caf[0:64, :], PS[1][0:64, 0:256], [psk(1)], [("CA", 0)])
        B.ts("dve", caf[64:128, :], PS[1][64:128, 0:256], -1.0, None, ALU.mult, None, [psk(1)], [("CA", 1)])
        B.ts("dve", cbf, PS[1][:, 256:512], -1.0, None, ALU.mult, None, [psk(1)], [("CB",)])
        for k in range(9):
            ik = S5_IDX[k]
            B.tt("dve", tq[2][:], CA[:], bc(PWR[:, ik, g0:g0 + 16]), ALU.mult, [("CA",), ("PWR",)], [("tq", 2)])
            B.tt("pool", tq[3][:], CB[:], bc(PWI[:, ik, g0:g0 + 16]), ALU.mult, [("CB",), ("PWI",)], [("tq", 3)])
            B.tt("dve", YD[:, :, k, :], tq[2][:], tq[3][:], ALU.add, [("tq", 2), ("tq", 3)], [("YD", k)])
        for g4 in range(4):
            gl0 = 4 * g4
            B.mm([(PS[2][:, j * 128:(j + 1) * 128], Xt[:, gl0 + j].rearrange("p s c -> p (s c)"),
                   YD[:, gl0 + j, 0:8, :].rearrange("p k c -> p (k c)"), True, True) for j in range(4)], [("Xt",), ("YD",)], [psk(2)])
            B.tt("dve", tmpS[:], PS[2][:].rearrange("p (j n) -> p j n", j=4), Cs["masks"][:].unsqueeze(1).broadcast_to([128, 4, 128]), ALU.mult,
                 [psk(2), ("c", "masks")], [("tmpS",)])
            for j in range(4):
                g = g0 + gl0 + j
                B.stt("dve", Sp[:, g, :], identf[:], dgcol[:, g:g + 1], tmpS[:, j, :], ALU.mult, ALU.add, [("c", "identf"), ("dgcol",), ("tmpS",)], [("Sp", g)])
            B.tr([(PS[3][:, j * 128:(j + 1) * 128], BzT[:, gl0 + j].rearrange("p s c -> p (s c)"), identf[:]) for j in range(4)],
                 [("BzT",), ("c", "identf")], [psk(3)])
            B.cp("act", Bz[:, g0 + gl0:g0 + gl0 + 4, :].rearrange("p g n -> p (g n)"), PS[3][:], [psk(3)], [("Bz", g0 + gl0)])
        B.cp("act", Dm[:, g0:g0 + 16, :].rearrange("p g (k c) -> p g k c", k=8), YD[:, :, 1:9, :], [("YD",)], [("Dm", q)])
    P.barrier()
    stA.close()

    stM = ExitStack()
    sm_ = lambda n, s, d: B.sb(stM, n, s, d)
    xo = sm_("xo", [128, 8, 8, 128], F32)
    uo = sm_("uo", [128, 8, 8, 128], BF16)
    Ug = [sm_("Ug%d" % i, [128, 1024], BF16) for i in range(2)]
    Hb = [[sm_("H%d%d" % (i, j), [128, 1024], BF16) for j in range(2)] for i in range(2)]
    Lt = [sm_("Lt%d" % i, [128, 10, 128], BF16) for i in range(2)]
    La = sm_("La", [128, 10, 128], F32)
    Lb = sm_("Lb", [128, 10, 128], F32)
    y8o = sm_("y8o", [128, 5, 8, 128], F32)
    xi = [sm_("xi%d" % i, [128, D], F32) for i in range(2)]
    ysi = [sm_("ysi%d" % i, [128, D], F32) for i in range(2)]
    t1e = [sm_("t1e%d" % i, [128, D], F32) for i in range(2)]
    gq = [sm_("gq%d" % i, [128, D], BF16) for i in range(2)]
    gTi = [sm_("gTi%d" % i, [128, 8, 128], BF16) for i in range(2)]
    sg = [sm_("sg%d" % i, [128, 512], F32) for i in range(2)]
    xlv = xl_d.rearrange("(b j i) f -> b i j f", b=8, j=128, i=8)
    NB = B.cfg.get("s5_blocks", 8)

    n = 0
    for b in range(8):
        for i in range(8):
            s = n % 2
            n += 1
            B.dma("sp", xi[s][:], xlv[b, i], [], [("xi", s)], ("xi", s))
            B.act(gq[0][:], xi[s][:], AF.Square, [("xi", s)], [("gq", 0), ("SSQ", b, i)], accum=SSQ[:, b, i:i + 1])
    B.ts("dve", SSQ[:], SSQ[:], 1.0 / D, EPS, ALU.mult, ALU.add, [("SSQ",)], [("SSQ",)])
    B.tt("pool", RSTD[:].rearrange("p b i -> p (b i)"), SSQ[:].rearrange("p b i -> p (b i)"), Cs["mhalf"][:, 0:1].broadcast_to([128, 64]), ALU.pow,
         [("SSQ",), ("c", "mhalf")], [("RSTD",)])

    xov = xl_d.rearrange("(b j i) f -> j b i f", b=8, j=128, i=8)
    ysv = ys_d.rearrange("(b j i) f -> j b i f", b=5, j=128, i=8)
    ev = [0]

    def evac_eng():
        ev[0] += 1
        return ("act", "dve")[ev[0] % 2]

    for o in range(8):
        P.dma_batch([B.dma("sp", xo[:, b], xov[:, b, :, o * 128:(o + 1) * 128], [], [("xo", b)], ("xo",)) for b in range(8)])
        for b in range(8):
            B.tt(("dve", "pool")[b % 2], uo[:, b].rearrange("p gl (s c) -> p s gl c", s=8, c=16),
                 xo[:, b].rearrange("p s (gl c) -> p s gl c", gl=8, c=16),
                 RSTD[:, b, :].unsqueeze(2).unsqueeze(3).broadcast_to([128, 8, 8, 16]), ALU.mult,
                 [("xo", b), ("RSTD",)], [("uo", b)])
        for gp in range(4):
            grp = [(2 * gp + z, 8 * o + 2 * gp + z, z) for z in range(2)]
            cur = {}
            for (gl, g, z) in grp:
                ptv = PS[6 + z][:].bitcast(BF16)
                B.tr([(ptv[:, b * 128:(b + 1) * 128], uo[:, b, gl, :], identb[:]) for b in range(8)], [("uo",), ("c", "identb")], [psk(6 + z)])
                B.cp(evac_eng(), Ug[z][:], ptv, [psk(6 + z)], [("Ug", z)])
                for h in range(2):
                    B.mm([(PS[2 * z + h][:], Bz[:, g, :], Ug[z][:, h * 512:(h + 1) * 512], True, True)], [("Bz",), ("Ug", z)], [psk(2 * z + h)])
                    B.cp(evac_eng(), Hb[z][0][:, h * 512:(h + 1) * 512], PS[2 * z + h][:], [psk(2 * z + h)], [("H", z, 0, h)])
                cur[z] = 0
                B.tt("pool", La[:], Cs["pswap"][:].unsqueeze(1).broadcast_to([128, 10, 128]), S2HS[:, :, g:g + 1].broadcast_to([128, 10, 128]), ALU.mult,
                     [("c", "pswap"), ("S2HS",)], [("La",)])
                B.tt("pool", Lb[:], identf[:].unsqueeze(1).broadcast_to([128, 10, 128]), ARHS[:, :, g:g + 1].broadcast_to([128, 10, 128]), ALU.mult,
                     [("c", "identf"), ("ARHS",)], [("Lb",)])
                B.tt("pool", Lt[z][:], La[:], Lb[:], ALU.add, [("La",), ("Lb",)], [("Lt", z)])
            for k in range(10):
                sft = 1 << k
                for (gl, g, z) in grp:
                    src, dst = Hb[z][cur[z]], Hb[z][1 - cur[z]]
                    for h in range(2):
                        c0 = h * 512
                        lo = max(sft, c0)
                        items = []
                        has_shift = lo < c0 + 512
                        items.append((PS[2 * z + h][:], identb[:], src[:, c0:c0 + 512], True, not has_shift))
                        if has_shift:
                            items.append((PS[2 * z + h][:, lo - c0:512], Lt[z][:, k, :], src[:, lo - sft:c0 + 512 - sft], False, True))
                        B.mm(items, [("c", "identb"), ("Lt", z), ("H", z, cur[z])], [psk(2 * z + h)])
                        B.cp(evac_eng(), dst[:, c0:c0 + 512], PS[2 * z + h][:], [psk(2 * z + h)], [("H", z, 1 - cur[z], h)])
                    cur[z] = 1 - cur[z]
            for (gl, g, z) in grp:
                Hf = Hb[z][cur[z]]
                items = []
                for b in range(BL0, 8):
                    o_ = PS[4][:, (b - BL0) * 128:(b - BL0 + 1) * 128] if b < 7 else PS[5][:, 0:128]
                    items.append((o_, Ug[z][:, b * 128:(b + 1) * 128], Sp[:, g, :], True, False))
                    items.append((o_, Hf[:, b * 128 - 1:b * 128 + 127], Dm[:, g, :], False, True))
                B.mm(items, [("Ug", z), ("Sp",), ("Dm",), ("H", z, cur[z])], [psk(4), psk(5)])
                B.cp(evac_eng(), y8o[:, 0:4, :, gl * 16:(gl + 1) * 16], PS[4][:].rearrange("p (b i c) -> p b i c", b=4, i=8), [psk(4)], [("y8o", gl, 0)])
                B.cp(evac_eng(), y8o[:, 4, :, gl * 16:(gl + 1) * 16], PS[5][:, 0:128].rearrange("p (i c) -> p i c", i=8), [psk(5)], [("y8o", gl, 1)])
        P.dma_batch([B.dma("sp", ysv[:, b, :, o * 128:(o + 1) * 128], y8o[:, b], [("y8o",)], [("ys", o, b)], ("ysst",)) for b in range(5)])

    ysr = ys_d.rearrange("(b j i) f -> b i j f", b=5, j=128, i=8)
    n = 0
    for bq in range(5):
        b = BL0 + bq
        for i in range(8):
            s = n % 2
            s2 = n % 2
            n += 1
            B.dma("sp", xi[s][:], xlv[b, i], [], [("xi", s)], ("xi", s))
            B.dma("sp", ysi[s][:], ysr[bq, i], [("ys",)], [("ysi", s)], ("ysi", s))
            y_ = ysi[s]
            t1 = t1e[s2]
            B.act(t1[:], y_[:], AF.Square, [("ysi", s)], [("t1e", s2)])
            B.ts("dve", t1[:], t1[:], 0.044715, 1.0, ALU.mult, ALU.add, [("t1e", s2)], [("t1e", s2)])
            B.tt("pool", t1[:], t1[:], y_[:], ALU.mult, [("t1e", s2), ("ysi", s)], [("t1e", s2)])
            B.act(t1[:], t1[:], AF.Sigmoid, [("t1e", s2)], [("t1e", s2)], scale=1.5957691216057308)
            B.tt("dve", gq[s2][:], y_[:], t1[:], ALU.mult, [("ysi", s), ("t1e", s2)], [("gq", s2)])
            ptv = PS[6 + s2][:].bitcast(BF16)
            B.tr([(ptv[:, k * 128:(k + 1) * 128], gq[s2][:, k * 128:(k + 1) * 128], identb[:]) for k in range(8)], [("gq", s2), ("c", "identb")], [psk(6 + s2)])
            B.cp("act", gTi[s2][:].rearrange("p k j -> p (k j)"), ptv, [psk(6 + s2)], [("gTi", s2)])
            for half in range(2):
                pb = 2 * s2 + half
                items = [(PS[pb][:], gTi[s2][:, k, :], wglu[:, k, half * 512:(half + 1) * 512], k == 0, False) for k in range(8)]
                items.append((PS[pb][:], Cs["onesb"][0:1, :], bglu16[0:1, half * 512:(half + 1) * 512], False, True))
                B.mm(items, [("gTi", s2), ("wglu",), ("c", "onesb"), ("bglu",)], [psk(pb)])
                B.act(sg[half][:], PS[pb][:], AF.Sigmoid, [psk(pb)], [("sg", half)])
                B.tt("dve", sg[half][:], sg[half][:], gq[s2][:, half * 512:(half + 1) * 512], ALU.mult, [("sg", half), ("gq", s2)], [("sg", half)])
                xh = xi[s][:, half * 512:(half + 1) * 512]
                B.tt("pool", xh, xh, sg[half][:], ALU.add, [("sg", half), ("xi", s, half)], [("xi", s, half)])
            if b == BL0:
                dst = x1_d[0:512].rearrange("(j i) f -> i j f", i=8)[i]
                B.dma("sp", dst, xi[s][64:128, :], [("xi", s)], [("x1s", b, i)], ("x1st", s))
            else:
                r0 = b * 1024 - 3584
                dst = x1_d[r0:r0 + 1024].rearrange("(j i) f -> i j f", i=8)[i]
                B.dma("sp", dst, xi[s][:], [("xi", s)], [("x1s", b, i)], ("x1st", s))
    P.barrier()
    stM.close()
    st.close()


_NC_CACHE = {}


def _get_nc(cfg_key, cfg):
    if cfg_key not in _NC_CACHE:
        _NC_CACHE[cfg_key] = build(cfg)
    return _NC_CACHE[cfg_key]


def make_in_maps(inputs, cfg):
    consts = host_consts()
    x = np.asarray(inputs["x"], np.float32)
    pos = np.asarray(inputs["positions"], np.int32)
    maps = []
    for c in range(8):
        b, half = c // 2, c % 2
        m = {}
        for k in WEIGHT_SPECS:
            m[k] = np.ascontiguousarray(np.asarray(inputs[k], np.float32))
        for k, v in consts.items():
            m[k] = v
        m["cflag"] = np.full((128, 1), float(half), np.float32)
        posl = np.zeros((9 * TT,), np.int32)
        if half == 1:
            posl[:] = pos[b, 3584:8192]
        else:
            posl[TT:] = pos[b, 0:4096]
        m["posl"] = posl
        if cfg.get("s5", True):
            xl = np.zeros((8192, D), np.float32)
            if half == 1:
                xl[:] = x[b]
            else:
                xl[4096:] = x[b, 0:4096]
            m["xl"] = xl
        maps.append(m)
    return maps


def kernel(**inputs):
    cfg = {}
    nc = _get_nc("full", cfg)
    maps = make_in_maps(inputs, cfg)
    res = run_bass_kernel_spmd(nc, maps, core_ids=list(range(8)))
    out = np.zeros((4, 8192, D), np.float32)
    for c in range(8):
        b, half = c // 2, c % 2
        out[b, half * 4096:(half + 1) * 4096] = res.results[c]["y"]
    return out
```

```python
import numpy as np
import ml_dtypes
from contextlib import ExitStack
import concourse.bass as bass
import concourse.mybir as mybir
from concourse.bass_utils import run_bass_kernel_spmd

F32 = mybir.dt.float32
BF16 = mybir.dt.bfloat16
I32 = mybir.dt.int32
AF = mybir.ActivationFunctionType
ALU = mybir.AluOpType

D = 1024
DFF = 2816
NCH = 22
TT = 512
EPS = 1e-5
ENGS = ("pe", "act", "dve", "pool", "sp")
STRICT = False


class Op:
    __slots__ = ("eng", "emit", "deps", "idx", "eidx", "signal", "semval", "dma", "dma_cnt", "waits")


class Prog:
    def __init__(self, nc):
        self.nc = nc
        self.ops = []
        self.eng_ops = {e: [] for e in ENGS}
        self.state = {}
        self.by_head = {}
        self.dma_counts = {}
        self.pending_dma = []

    def _conf(self, key):
        for k in self.by_head.get(key[0], ()):
            n = min(len(k), len(key))
            if k[:n] == key[:n]:
                yield k

    def _get(self, key):
        st = self.state.get(key)
        if st is None:
            st = [None, {}, []]
            self.state[key] = st
            self.by_head.setdefault(key[0], []).append(key)
        return st

    def add(self, eng, emit, reads=(), writes=(), dma=None, extra_deps=None):
        op = Op()
        op.eng, op.emit, op.dma, op.signal, op.semval, op.waits = eng, emit, dma, False, 0, []
        op.idx = len(self.ops)
        op.eidx = len(self.eng_ops[eng])
        deps = {}
        reads = [r if isinstance(r, tuple) else (r,) for r in reads]
        writes = [w if isinstance(w, tuple) else (w,) for w in writes]
        for r in reads:
            for k in self._conf(r):
                w = self.state[k][0]
                if w is not None:
                    deps[w] = True
        for w_ in writes:
            for k in self._conf(w_):
                st = self.state[k]
                if st[0] is not None:
                    deps.setdefault(st[0], False)
                for ro in st[1].values():
                    deps.setdefault(ro, False)
                for ro in st[2]:
                    deps.setdefault(ro, False)
        for r in reads:
            st = self._get(r)
            if dma is not None:
                st[2].append(op)
            else:
                st[1][eng] = op
        for w_ in writes:
            for k in list(self._conf(w_)):
                if len(k) > len(w_):
                    del self.state[k]
                    self.by_head[k[0]].remove(k)
            st = self._get(w_)
            st[0], st[1], st[2] = op, {}, []
        if extra_deps:
            for p in extra_deps:
                deps.setdefault(p, False)
        deps.pop(op, None)
        op.deps = deps
        if dma is not None:
            c = self.dma_counts.get(dma, 0) + 1
            self.dma_counts[dma] = c
            op.dma_cnt = c
            self.pending_dma.append(op)
        self.ops.append(op)
        self.eng_ops[eng].append(op)
        return op

    def dma_batch(self, ops):
        c = max(o.dma_cnt for o in ops)
        for o in ops:
            o.dma_cnt = c

    def barrier(self):
        lasts = [self.eng_ops[e][-1] for e in ENGS if self.eng_ops[e]]
        pend = list(self.pending_dma)
        for e in ENGS:
            self.add(e, lambda eng: eng.nop(), extra_deps=lasts + pend)
        self.state = {}
        self.by_head = {}
        self.pending_dma = []

    def finalize_and_emit(self):
        nc = self.nc
        waited = {e: {} for e in ENGS}
        for op in self.ops:
            E = op.eng
            for p, is_raw in op.deps.items():
                if p.dma is not None:
                    k = ("dma", p.dma)
                    if p.dma[0] == "SETUP":
                        p.dma_cnt = self.dma_counts[p.dma]
                    if waited[E].get(k, 0) >= p.dma_cnt:
                        continue
                    waited[E][k] = p.dma_cnt
                    op.waits.append(p)
                elif p.eng == E:
                    if E == "sp":
                        continue
                    if STRICT:
                        if waited[E].get(E, -1) >= p.eidx:
                            continue
                        waited[E][E] = p.eidx
                        op.waits.append(p)
                        p.signal = True
                        continue
                    if E == "pe":
                        continue
                    if is_raw and (op.eidx - p.eidx) <= 1:
                        op.waits.append(p)
                        p.signal = True
                else:
                    if waited[E].get(p.eng, -1) >= p.eidx:
                        continue
                    waited[E][p.eng] = p.eidx
                    op.waits.append(p)
                    p.signal = True
        for e in ENGS:
            c = 0
            for op in self.eng_ops[e]:
                if op.signal:
                    c += 1
                    op.semval = c
        with ExitStack() as es:
            sems = {e: es.enter_context(nc.semaphore("sem_" + e)) for e in ENGS}
            dsems = {}
            for i, k in enumerate(self.dma_counts):
                dsems[k] = es.enter_context(nc.semaphore("dsem%d" % i))
            block = es.enter_context(nc.Block())

            def mk(e):
                def body(eng):
                    for op in self.eng_ops[e]:
                        for p in op.waits:
                            if p.dma is not None:
                                eng.wait_ge(dsems[p.dma], 16 * p.dma_cnt)
                            else:
                                eng.wait_ge(sems[p.eng], p.semval)
                        ins = op.emit(eng)
                        if op.dma is not None:
                            ins.then_inc(dsems[op.dma], 16)
                        elif op.signal:
                            ins.then_inc(sems[e], 1)
                return body

            block.tensor(mk("pe"))
            block.scalar(mk("act"))
            block.vector(mk("dve"))
            block.gpsimd(mk("pool"))
            block.sync(mk("sp"))


QHEADS = [(0, 4), (1, 5), (2, 6), (3, 7), (8, 12), (9, 13), (10, 14), (11, 15)]


def host_consts():
    c = {}
    c["identb"] = np.eye(128, dtype=np.float32).astype(ml_dtypes.bfloat16)
    c["identf"] = np.eye(128, dtype=np.float32)
    pm = np.zeros((128, 128), np.float32)
    for p in range(64):
        pm[p, p + 64] = 1.0
        pm[p + 64, p] = 1.0
    c["pswap"] = pm
    s = np.arange(128) // 16
    c["masks"] = (s[:, None] <= s[None, :]).astype(np.float32)
    q = np.arange(128)[:, None]
    k = np.arange(256)[None, :]
    valid = (k <= q + 128) & (q + 128 - k < 128)
    c["band"] = np.where(valid, 0.0, -1e30).astype(np.float32).astype(ml_dtypes.bfloat16)
    inv = 1.0 / np.power(np.float32(500000.0), np.arange(0, 16, 2, dtype=np.float32) / np.float32(16))
    col = np.zeros((128, 1), np.float32)
    jt = np.zeros((128, 128), np.float32)
    for p in range(128):
        d = p % 64
        if d < 16:
            col[p, 0] = inv[d % 8]
            if d < 8:
                jt[p + 8, p] = -1.0
            else:
                jt[p - 8, p] = 1.0
    c["invf"] = col
    c["jt"] = jt.astype(ml_dtypes.bfloat16)
    c["onesb"] = np.ones((1, 128), np.float32).astype(ml_dtypes.bfloat16)
    c["mhalf"] = np.full((128, 1), -0.5, np.float32)
    return c


CONST_SPECS = {
    "identb": ([128, 128], BF16), "identf": ([128, 128], F32), "pswap": ([128, 128], F32),
    "masks": ([128, 128], F32), "band": ([128, 256], BF16), "invf": ([128, 1], F32),
    "jt": ([128, 128], BF16), "onesb": ([1, 128], BF16), "mhalf": ([128, 1], F32),
}

WEIGHT_SPECS = {
    "norm_mix": [2, 1024], "norm_ffn": [2, 1024], "norm_final": [1024],
    "s5_lambda_re": [1, 64, 64], "s5_lambda_im": [1, 64, 64], "s5_log_dt": [1, 64],
    "s5_b_re": [1, 64, 64, 16], "s5_b_im": [1, 64, 64, 16], "s5_c_re": [1, 64, 16, 64], "s5_c_im": [1, 64, 16, 64],
    "s5_d": [1, 1024], "s5_w_glu": [1, 1024, 1024], "s5_b_glu": [1, 1024],
    "attn_w_qkv": [1, 1024, 1536], "attn_b_qkv": [1, 1536], "attn_sinks": [1, 16],
    "attn_w_o": [1, 1024, 1024], "attn_b_o": [1, 1024],
    "ffn_w_up": [2, 1024, 5632], "ffn_w_conv": [2, 3, 5632], "ffn_b_conv": [2, 5632], "ffn_w_down": [2, 2816, 1024],
}


AX = mybir.AxisListType
TWO_PI = 2.0 * np.pi
MAGIC = 12582912.0
CW1 = 6.28125
CW2 = float(np.float32(TWO_PI - 6.28125))
CW3 = float(TWO_PI - 6.28125 - np.float64(np.float32(TWO_PI - 6.28125)))


class Builder:
    def __init__(self, cfg):
        self.cfg = cfg
        self.nc = bass.Bass("TRN2", target_bir_lowering=False)
        self.P = Prog(self.nc)
        self.es = ExitStack()

    def din(self, name, shape, dt=F32):
        return self.nc.dram_tensor(name, list(shape), dt, kind="ExternalInput").ap()

    def dout(self, name, shape, dt=F32):
        return self.nc.dram_tensor(name, list(shape), dt, kind="ExternalOutput").ap()

    def dscr(self, name, shape, dt):
        return self.nc.dram_tensor(name, list(shape), dt).ap()

    def sb(self, stack, name, shape, dt):
        return stack.enter_context(self.nc.sbuf_tensor(name, list(shape), dt))

    def ps(self, stack, name, shape, dt):
        return stack.enter_context(self.nc.psum_tensor(name, list(shape), dt))

    def dma(self, q, out, in_, reads, writes, key, slow=False):
        if slow:
            f = lambda eng: eng.dma_start(out=out, in_=in_, allow_slow_non_contiguous=True)
        else:
            f = lambda eng: eng.dma_start(out=out, in_=in_)
        return self.P.add(q, f, reads=reads, writes=writes, dma=key)

    def act(self, out, in_, func, reads, writes, scale=None, bias=None, accum=None):
        kw = {}
        if scale is not None:
            kw["scale"] = scale
        if bias is not None:
            kw["bias"] = bias
        if accum is not None:
            kw["accum_out"] = accum
        return self.P.add("act", lambda e: e.activation(out=out, in_=in_, func=func, **kw), reads, writes)

    def ts(self, eng, out, in0, s1, s2, op0, op1, reads, writes):
        if s2 is None:
            f = lambda e: e.tensor_scalar(out=out, in0=in0, scalar1=s1, scalar2=None, op0=op0)
        else:
            f = lambda e: e.tensor_scalar(out=out, in0=in0, scalar1=s1, scalar2=s2, op0=op0, op1=op1)
        return self.P.add(eng, f, reads, writes)

    def stt(self, eng, out, in0, scalar, in1, op0, op1, reads, writes):
        return self.P.add(eng, lambda e: e.scalar_tensor_tensor(out=out, in0=in0, scalar=scalar, in1=in1, op0=op0, op1=op1), reads, writes)

    def tt(self, eng, out, in0, in1, op, reads, writes):
        return self.P.add(eng, lambda e: e.tensor_tensor(out=out, in0=in0, in1=in1, op=op), reads, writes)

    def cp(self, eng, out, in_, reads, writes):
        if eng == "act":
            return self.P.add("act", lambda e: e.copy(out=out, in_=in_), reads, writes)
        return self.P.add(eng, lambda e: e.tensor_copy(out=out, in_=in_), reads, writes)

    def memset(self, eng, ap, val, writes):
        return self.P.add(eng, lambda e: e.memset(ap, val), [], writes)

    def mm(self, items, reads, writes):
        def f(e):
            ins = None
            for (o, l, r, s0, s1) in items:
                ins = e.matmul(o, lhsT=l, rhs=r, start=s0, stop=s1)
            return ins
        return self.P.add("pe", f, reads, writes)

    def tr(self, items, reads, writes):
        def f(e):
            ins = None
            for (o, i_, idn) in items:
                ins = e.transpose(out=o, in_=i_, identity=idn)
            return ins
        return self.P.add("pe", f, reads, writes)


def psk(i):
    return ("ps", i)


def build(cfg):
    B = Builder(cfg)
    nc, P = B.nc, B.P
    NT = cfg.get("NT", 9)
    do_s5 = cfg.get("s5", True)
    layers = cfg.get("layers", ("ffn0", "att", "ffn1", "final"))

    W = {k: B.din(k, s) for k, s in WEIGHT_SPECS.items()}
    C = {k: B.din(k, s, dt) for k, (s, dt) in CONST_SPECS.items()}
    cflag_d = B.din("cflag", [128, 1])
    pos_d = B.din("posl", [9 * TT], I32)
    if do_s5:
        xl_d = B.din("xl", [8192, D])
        x1_d = B.dscr("x1s", [9 * TT, D], F32)
    else:
        xl_d = None
        x1_d = B.din("x1s", [9 * TT, D])
    out_d = B.dout("y", [8 * TT, D])
    dbg_d = B.dout("dbg", [9 * TT, D]) if cfg.get("dbg") else None
    wup_s = B.dscr("wup_s", [2, NCH, 128, 2 * 8 * 128], BF16)
    wdn_s = B.dscr("wdn_s", [2, 2, 11, 128, 2 * 512], BF16)
    wqk_s = B.dscr("wqk_s", [10, 128, 8 * 128], BF16)
    wv_s = B.dscr("wv_s", [128, 8 * 256], BF16)
    wo_s = B.dscr("wo_s", [2, 4, 128, 2 * 512], BF16)
    wglu_s = B.dscr("wglu_s", [8, 128, 1024], BF16)

    glob = B.es
    Cs = {}
    for k, (s, dt) in CONST_SPECS.items():
        Cs[k] = B.sb(glob, "c_" + k, s, dt)
        B.dma("sp", Cs[k][:], C[k], [], [("c", k)], ("SETUP", 0))
    cflag = B.sb(glob, "cflag_s", [128, 1], F32)
    B.dma("sp", cflag[:], cflag_d, [], [("c", "cflag")], ("SETUP", 0))
    PS = [B.ps(glob, "ps%d" % i, [128, 512], F32) for i in range(8)]

    with ExitStack() as st0:
        gcol = B.sb(st0, "gcol", [128, 4, 8], F32)
        for wi, src in enumerate((W["norm_ffn"][0], W["norm_ffn"][1], W["norm_mix"][1])):
            B.dma("sp", gcol[:, wi, :], src.rearrange("(k p) -> p k", p=128), [], [("gcol", wi)], ("SETUP", 0), slow=True)
        wraw = [B.sb(st0, "wraw%d" % i, [128, 5632], F32) for i in range(2)]
        wcv = [B.sb(st0, "wcv%d" % i, [128, 5632], BF16) for i in range(2)]
        cnt = [0]
        ei = [0]

        def nexti():
            i = cnt[0] % 2
            cnt[0] += 1
            return i

        def cast(i, n, gain_ap):
            e_ = ("act", "dve")[ei[0] % 2]
            ei[0] += 1
            if gain_ap is not None:
                if e_ == "act":
                    B.act(wcv[i][:, 0:n], wraw[i][:, 0:n], AF.Copy, [("wraw", i), ("gcol",)], [("wcv", i)], scale=gain_ap)
                else:
                    B.ts(e_, wcv[i][:, 0:n], wraw[i][:, 0:n], gain_ap, None, ALU.mult, None, [("wraw", i), ("gcol",)], [("wcv", i)])
            else:
                B.cp(e_, wcv[i][:, 0:n], wraw[i][:, 0:n], [("wraw", i)], [("wcv", i)])

        for l in range(2):
            if ("ffn%d" % l) not in layers:
                continue
            for k in range(8):
                i = nexti()
                B.dma("sp", wraw[i][:, :], W["ffn_w_up"][l, k * 128:(k + 1) * 128, :], [], [("wraw", i)], ("wraw", i))
                cast(i, 5632, gcol[:, l, k:k + 1])
                dst = wup_s[l].rearrange("m p (a k c) -> p a m k c", a=2, k=8, c=128)[:, :, :, k, :]
                B.dma("sp", dst, wcv[i][:, :].rearrange("p (a m c) -> p a m c", a=2, m=NCH, c=128), [("wcv", i)], [("wscr",)], ("wst", i))
            for mm in range(11):
                i = nexti()
                src = W["ffn_w_down"][l, mm * 256:(mm + 1) * 256, :].rearrange("(j p) n -> p j n", p=128)
                B.dma("sp", wraw[i][:, 0:2048].rearrange("p (j n) -> p j n", j=2), src, [], [("wraw", i)], ("wraw", i))
                cast(i, 2048, None)
                for half in range(2):
                    dst = wdn_s[l, half, mm].rearrange("p (j n) -> p j n", j=2)
                    srcv = wcv[i][:, 0:2048].rearrange("p (j n) -> p j n", j=2)[:, :, half * 512:(half + 1) * 512]
                    B.dma("sp", dst, srcv, [("wcv", i)], [("wscr",)], ("wst", i))
        if "att" in layers:
            for k in range(8):
                i = nexti()
                B.dma("sp", wraw[i][:, 0:1536], W["attn_w_qkv"][0, k * 128:(k + 1) * 128, :], [], [("wraw", i)], ("wraw", i))
                g_ap = gcol[:, 2, k:k + 1]
                for a in range(2):
                    srcv = wraw[i][:, a * 512:(a + 1) * 512].rearrange("p (j b d) -> p b j d", j=2, b=4, d=64)
                    dstv = wcv[i][:, a * 512:(a + 1) * 512].rearrange("p (b j d) -> p b j d", b=4, j=2, d=64)
                    B.ts("dve", dstv, srcv, g_ap, None, ALU.mult, None, [("wraw", i), ("gcol",)], [("wcv", i, "q%d" % a)])
                B.act(wcv[i][:, 1024:1536], wraw[i][:, 1024:1536], AF.Copy, [("wraw", i), ("gcol",)], [("wcv", i, "kv")], scale=g_ap)
                dstqk = wqk_s.rearrange("c p (k n) -> p c k n", k=8)[:, :, k, :]
                B.dma("sp", dstqk, wcv[i][:, 0:1280].rearrange("p (c n) -> p c n", n=128), [("wcv", i)], [("wscr",)], ("wst", i))
                dstv2 = wv_s.rearrange("p (k n) -> p k n", k=8)[:, k, :]
                B.dma("sp", dstv2, wcv[i][:, 1280:1536], [("wcv", i)], [("wscr",)], ("wst", i))
            for c in range(8):
                i = nexti()
                hA, hB = QHEADS[c]
                B.dma("sp", wraw[i][0:64, 0:1024], W["attn_w_o"][0, hA * 64:(hA + 1) * 64, :], [], [("wraw", i, "a")], ("wraw", i))
                B.dma("sp", wraw[i][64:128, 0:1024], W["attn_w_o"][0, hB * 64:(hB + 1) * 64, :], [], [("wraw", i, "b")], ("wrawb", i))
                cast(i, 1024, None)
                for half in range(2):
                    dst = wo_s[half, c // 2].rearrange("p (j n) -> p j n", j=2)[:, c % 2, :]
                    B.dma("sp", dst, wcv[i][:, half * 512:(half + 1) * 512], [("wcv", i)], [("wscr",)], ("wst", i))
        if do_s5:
            for k in range(8):
                i = nexti()
                B.dma("sp", wraw[i][:, 0:1024], W["s5_w_glu"][0, k * 128:(k + 1) * 128, :], [], [("wraw", i)], ("wraw", i))
                cast(i, 1024, None)
                B.dma("sp", wglu_s[k], wcv[i][:, 0:1024], [("wcv", i)], [("wscr",)], ("wst", i))
        P.barrier()

    if do_s5:
        emit_s5(B, W, Cs, PS, xl_d, x1_d, wglu_s, cflag)
        P.barrier()

    emit_phase2(B, W, Cs, PS, cflag, pos_d, x1_d, out_d, dbg_d, wup_s, wdn_s, wqk_s, wv_s, wo_s, NT, layers)
    P.barrier()
    P.finalize_and_emit()
    B.es.close()
    return nc


def emit_phase2(B, W, Cs, PS, cflag, pos_d, x1_d, out_d, dbg_d, wup_s, wdn_s, wqk_s, wv_s, wo_s, NT, layers):
    nc, P = B.nc, B.P
    st = ExitStack()
    sb = lambda name, shape, dt: B.sb(st, name, shape, dt)
    identb = Cs["identb"]

    xres = sb("xres", [128, 4, D], F32)
    hT = sb("hT", [128, 8, TT], BF16)
    gT = sb("gT", [128, NCH, TT], BF16)
    wup = [sb("wup%d" % i, [128, 2, 8, 128], BF16) for i in range(3)]
    wdn = [sb("wdn%d" % i, [128, 2, 512], BF16) for i in range(3)]
    uba = [sb("uba%d" % i, [128, TT + 2], F32) for i in range(2)]
    ubv = [sb("ubv%d" % i, [128, TT + 2], F32) for i in range(2)]
    ca = [sb("ca%d" % i, [128, TT], F32) for i in range(2)]
    cv = [sb("cv%d" % i, [128, TT], F32) for i in range(2)]
    sa = [sb("sa%d" % i, [128, TT], F32) for i in range(2)]
    halo = sb("halo", [128, 2, 2 * NCH, 2], F32)
    convw = sb("convw", [128, 2, 4, 2 * NCH], F32)
    cwnat = sb("cwnat", [2 * NCH, 2, 4, 128], F32)
    junk = sb("junk", [128, D], BF16)
    xn = [sb("xn%d" % i, [128, D], BF16) for i in range(2)]
    ss = sb("ss", [128, 4], F32)
    vv = sb("vv", [128, 4], F32)
    rstd = sb("rstd", [128, 4], F32)
    gfin = sb("gfin", [128, D], F32)
    otile = [sb("otile%d" % i, [128, D], F32) for i in range(2)]

    for l in range(2):
        for k in range(3):
            B.dma("sp", cwnat[:, l, k, :], W["ffn_w_conv"][l, k].rearrange("(c p) -> c p", p=128), [], [("cwnat", l, k)], ("SETUP", 2))
        B.dma("sp", cwnat[:, l, 3, :], W["ffn_b_conv"][l].rearrange("(c p) -> c p", p=128), [], [("cwnat", l, 3)], ("SETUP", 2))
    for l in range(2):
        B.tr([(PS[l][:, k * 64:k * 64 + 2 * NCH], cwnat[:, l, k, :], Cs["identf"][0:2 * NCH, 0:2 * NCH]) for k in range(4)],
             [("cwnat", l), ("c", "identf")], [psk(l)])
        B.cp("dve", convw[:, l, :, :], PS[l][:, 0:256].rearrange("p (k c) -> p k c", k=4)[:, :, 0:2 * NCH], [psk(l)], [("convw", l)])
    B.memset("pool", halo[:], 0.0, [("halo",)])
    B.dma("sp", gfin[:], W["norm_final"].partition_broadcast(128), [], [("gfin",)], ("SETUP", 2))

    def stats_rstd(warm):
        for sub in range(4):
            B.act(junk[:], xres[:, sub, :], AF.Square, [("xres", sub)], [("junk",), ("ss", sub)], accum=ss[:, sub:sub + 1])
        B.ts("dve", vv[:], ss[:], 1.0 / D, EPS, ALU.mult, ALU.add, [("ss",)], [("vv",)])
        B.tt("pool", rstd[:], vv[:], Cs["mhalf"][:, 0:1].broadcast_to([128, 4]), ALU.pow, [("vv",), ("c", "mhalf")], [("rstd",)])
        if warm:
            B.ts("pool", rstd[:], rstd[:], cflag[:, 0:1], None, ALU.mult, None, [("rstd",), ("c", "cflag")], [("rstd",)])

    def norm_to_hT(warm):
        stats_rstd(warm)
        for sub in range(4):
            i = sub % 2
            B.act(xn[i][:], xres[:, sub, :], AF.Copy, [("xres", sub), ("rstd",)], [("xn", i)], scale=rstd[:, sub:sub + 1])
            pb = 6 + i
            ptv = PS[pb][:].bitcast(BF16)
            B.tr([(ptv[:, k * 128:(k + 1) * 128], xn[i][:, k * 128:(k + 1) * 128], identb[:]) for k in range(8)],
                 [("xn", i), ("c", "identb")], [psk(pb)])
            B.cp("dve", hT[:, :, sub * 128:(sub + 1) * 128], ptv.rearrange("p (k t) -> p k t", k=8), [psk(pb)], [("hT", sub)])

    def ffn(l, down=True):
        for m in range(NCH):
            s3 = m % 3
            s2 = m % 2
            B.dma("sp", wup[s3][:].rearrange("p a k c -> p (a k c)"), wup_s[l, m], [("wscr",)], [("wup", s3)], ("wup", s3))
            for av, (ub, pbase, key) in enumerate(((uba, 0, "uba"), (ubv, 2, "ubv"))):
                pb = pbase + s2
                B.mm([(PS[pb][:], wup[s3][:, av, k, :], hT[:, k, :], k == 0, k == 7) for k in range(8)], [("wup", s3), ("hT",)], [psk(pb)])
                ch = av * NCH + m
                u = ub[s2]
                B.cp("pool", u[:, 0:2], halo[:, l, ch, :], [("halo", l, ch)], [(key, s2, "h")])
                B.cp("act", u[:, 2:TT + 2], PS[pb][:], [psk(pb)], [(key, s2, "m")])
                B.cp("pool", halo[:, l, ch, :], u[:, TT:TT + 2], [(key, s2, "m")], [("halo", l, ch)])
                dst = ca[s2] if av == 0 else cv[s2]
                dkey = ("ca", s2) if av == 0 else ("cv", s2)
                cw = [convw[:, l, k, ch:ch + 1] for k in range(4)]
                B.act(dst[:], PS[pb][:], AF.Identity, [psk(pb), ("convw", l)], [dkey], scale=cw[2], bias=cw[3])
                B.stt("dve", dst[:], u[:, 1:TT + 1], cw[1], dst[:], ALU.mult, ALU.add, [(key, s2), ("convw", l), dkey], [dkey])
                B.stt("dve", dst[:], u[:, 0:TT], cw[0], dst[:], ALU.mult, ALU.add, [(key, s2), ("convw", l), dkey], [dkey])
            B.act(sa[s2][:], ca[s2][:], AF.Silu, [("ca", s2)], [("sa", s2)])
            B.tt("dve", gT[:, m, :], sa[s2][:], cv[s2][:], ALU.mult, [("sa", s2), ("cv", s2)], [("gT", m)])
        if down:
            proj_tokmajor(gT, ("gT",), wdn_s[l], 11, None)

    wcount = [0]

    def proj_tokmajor(actT, actkey, wsrc, nmm, bias_row):
        for half in range(2):
            pbase = 4 if half == 0 else 0
            for mm in range(nmm):
                s3 = wcount[0] % 3
                wcount[0] += 1
                B.dma("sp", wdn[s3][:].rearrange("p j n -> p (j n)"), wsrc[half, mm], [("wscr",)], [("wdn", s3)], ("wdn", s3))
                items = []
                for sub in range(4):
                    for j in range(2):
                        first = (mm == 0 and j == 0)
                        last = (mm == nmm - 1 and j == 1 and bias_row is None)
                        items.append((PS[pbase + sub][:], actT[:, 2 * mm + j, sub * 128:(sub + 1) * 128], wdn[s3][:, j, :], first, last))
                B.mm(items, [actkey, ("wdn", s3)], [psk(pbase + sub) for sub in range(4)])
            if bias_row is not None:
                B.mm([(PS[pbase + sub][:], Cs["onesb"][0:1, :], bias_row[0:1, half * 512:(half + 1) * 512], False, True) for sub in range(4)],
                     [("c", "onesb"), ("bo",)], [psk(pbase + sub) for sub in range(4)])
            for sub in range(4):
                xs = xres[:, sub, half * 512:(half + 1) * 512]
                B.tt("dve", xs, xs, PS[pbase + sub][:], ALU.add, [psk(pbase + sub), ("xres", sub, half)], [("xres", sub, half)])

    def final_norm(t):
        stats_rstd(False)
        for sub in range(4):
            i = sub % 2
            B.stt("dve", otile[i][:], xres[:, sub, :], rstd[:, sub:sub + 1], gfin[:], ALU.mult, ALU.mult, [("xres", sub), ("rstd",), ("gfin",)], [("otile", i)])
            r0 = (t - 1) * TT + sub * 128
            B.dma("sp", out_d[r0:r0 + 128, :], otile[i][:], [("otile", i)], [("out", t, sub)], ("otile", i))

    attn = None
    if "att" in layers:
        attn = make_attention(B, W, Cs, PS, st, cflag, pos_d, wqk_s, wv_s, wo_s, xres, hT, norm_to_hT, proj_tokmajor)

    for t in range(NT):
        warm = (t == 0)
        B.dma("sp", xres[:], x1_d[t * TT:(t + 1) * TT, :].rearrange("(s p) f -> p s f", p=128), [("x1s",)], [("xres",)], ("xres",))
        if "ffn0" in layers:
            norm_to_hT(warm)
            ffn(0)
        if attn is not None:
            attn(t, warm)
        if "ffn1" in layers:
            norm_to_hT(warm)
            ffn(1, down=not warm)
        if dbg_d is not None:
            B.dma("sp", dbg_d[t * TT:(t + 1) * TT, :].rearrange("(s p) f -> p s f", p=128), xres[:], [("xres",)], [("dbg", t)], ("dbg",))
        if "final" in layers and not warm:
            final_norm(t)
    B.es.callback(st.close)


def make_attention(B, W, Cs, PS, st, cflag, pos_d, wqk_s, wv_s, wo_s, xres, hT, norm_to_hT, proj_tokmajor):
    sb = lambda name, shape, dt: B.sb(st, name, shape, dt)
    identb = Cs["identb"]
    qT = sb("qT", [128, 8, TT], BF16)
    kT = sb("kT", [128, 2, 128 + TT], BF16)
    Vb = sb("Vb", [128, 5, 256], BF16)
    oT = sb("oT", [128, 8, TT], BF16)
    qraw = [sb("qraw%d" % i, [128, TT], F32) for i in range(2)]
    qb16 = [sb("qb16%d" % i, [128, TT], BF16) for i in range(2)]
    rt1 = [sb("rt1%d" % i, [128, TT], F32) for i in range(2)]
    rt2 = [sb("rt2%d" % i, [128, TT], F32) for i in range(2)]
    cosT = sb("cosT", [128, TT], F32)
    sinT = sb("sinT", [128, TT], F32)
    posi = sb("posi", [128, TT], I32)
    angA = sb("angA", [128, TT], F32)
    angB = sb("angB", [128, TT], F32)
    angC = sb("angC", [128, TT], F32)
    wqk = [sb("wqk%d" % i, [128, 8, 128], BF16) for i in range(3)]
    wv = sb("wv", [128, 8, 256], BF16)
    bqk = sb("bqk", [128, 10], F32)
    bvb = sb("bvb", [128, 256], F32)
    bo32 = sb("bo32", [1, D], F32)
    bo16 = sb("bo16", [1, D], BF16)
    sinkb = sb("sinkb", [128, 16], F32)
    sinkp = sb("sinkp", [128, 8, 2], F32)
    nsinkp = sb("nsinkp", [128, 8, 2], F32)
    maskF = sb("maskF", [128, 256], BF16)
    mtmp = sb("mtmp", [128, 1], F32)
    Pexp = [sb("Pexp%d" % i, [128, 2, 256], BF16) for i in range(2)]
    sm = [sb("sm%d" % i, [128, 8, 2], F32) for i in range(2)]
    dgm = [sb("dgm%d" % i, [128, 2, 128], BF16) for i in range(2)]
    PTs = [sb("PTs%d" % i, [128, 4, 128], BF16) for i in range(2)]

    bq = W["attn_b_qkv"][0]
    for j in range(2):
        for a in range(2):
            B.dma("sp", bqk[64 * j:64 * j + 64, 4 * a:4 * a + 4],
                  bq[0:1024].rearrange("(a j b d) -> a j d b", a=2, j=2, b=4, d=64)[a, j], [], [("bqk", "q", j, a)], ("SETUP", 2), slow=True)
    B.dma("sp", bqk[:, 8:10], bq[1024:1280].rearrange("(c p) -> p c", p=128), [], [("bqk", "k")], ("SETUP", 2), slow=True)
    B.dma("sp", bvb[:], bq[1280:1536].partition_broadcast(128), [], [("bvb",)], ("SETUP", 2))
    B.dma("sp", bo32[:], W["attn_b_o"], [], [("bo32",)], ("SETUP", 2))
    B.cp("act", bo16[:], bo32[:], [("bo32",)], [("bo",)])
    B.dma("sp", sinkb[:], W["attn_sinks"][0].partition_broadcast(128), [], [("sinkb",)], ("SETUP", 2))
    B.cp("dve", sinkp[:].rearrange("p (a b) j -> p a b j", a=2), sinkb[:].rearrange("p (a j b) -> p a b j", a=2, j=2, b=4), [("sinkb",)], [("sinkp",)])
    B.ts("dve", nsinkp[:], sinkp[:], -1.0, None, ALU.mult, None, [("sinkp",)], [("nsinkp",)])
    B.ts("dve", mtmp[:], cflag[:], -1.0, 1e30, ALU.add, ALU.mult, [("c", "cflag")], [("mtmp",)])
    B.ts("dve", maskF[:, 0:128], Cs["band"][:, 0:128], mtmp[:, 0:1], None, ALU.add, None, [("c", "band"), ("mtmp",)], [("maskF", 0)])
    B.cp("dve", maskF[:, 128:256], Cs["band"][:, 128:256], [("c", "band")], [("maskF", 1)])
    B.memset("pool", kT[:], 0.0, [("kT",)])
    B.memset("pool", Vb[:], 0.0, [("Vb",)])
    cnt = [0]

    def attn(t, warm):
        norm_to_hT(warm)
        if B.cfg.get("att_stage", 9) < 0:
            return
        B.dma("sp", posi[:], pos_d[t * TT:(t + 1) * TT].partition_broadcast(128), [], [("posi",)], ("posi",))
        B.ts("dve", angA[:], posi[:], Cs["invf"][:, 0:1], None, ALU.mult, None, [("posi",), ("c", "invf")], [("angA",)])
        B.ts("dve", angB[:], angA[:], float(1.0 / TWO_PI), MAGIC, ALU.mult, ALU.add, [("angA",)], [("angB",)])
        B.ts("dve", angB[:], angB[:], -MAGIC, None, ALU.add, None, [("angB",)], [("angB",)])
        B.stt("dve", angC[:], angB[:], -CW1, angA[:], ALU.mult, ALU.add, [("angB",), ("angA",)], [("angC",)])
        B.stt("dve", angC[:], angB[:], -CW2, angC[:], ALU.mult, ALU.add, [("angB",), ("angC",)], [("angC",)])
        B.stt("dve", angC[:], angB[:], -CW3, angC[:], ALU.mult, ALU.add, [("angB",), ("angC",)], [("angC",)])
        B.ts("dve", angC[:], angC[:], -float(np.pi), float(np.pi), ALU.max, ALU.min, [("angC",)], [("angC",)])
        B.act(sinT[:], angC[:], AF.Sin, [("angC",)], [("sinT",)])
        B.ts("dve", angA[:], angC[:], float(np.pi / 2), None, ALU.add, None, [("angC",)], [("angA",)])
        B.ts("dve", angB[:], angA[:], float(np.pi), -float(TWO_PI), ALU.is_gt, ALU.mult, [("angA",)], [("angB",)])
        B.tt("dve", angA[:], angA[:], angB[:], ALU.add, [("angA",), ("angB",)], [("angA",)])
        B.ts("dve", angA[:], angA[:], -float(np.pi), float(np.pi), ALU.max, ALU.min, [("angA",)], [("angA",)])
        B.act(cosT[:], angA[:], AF.Sin, [("angA",)], [("cosT",)])
        if B.cfg.get("att_stage", 9) < 1:
            return
        for c in range(10):
            s3 = c % 3
            i = c % 2
            B.dma("sp", wqk[s3][:].rearrange("p k n -> p (k n)"), wqk_s[c], [("wscr",)], [("wqk", s3)], ("wqk", s3))
            B.mm([(PS[i][:], wqk[s3][:, k, :], hT[:, k, :], k == 0, k == 7) for k in range(8)], [("wqk", s3), ("hT",)], [psk(i)])
            B.act(qraw[i][:], PS[i][:], AF.Identity, [psk(i), ("bqk",)], [("qraw", i)], bias=bqk[:, c:c + 1])
            B.cp("dve", qb16[i][:], qraw[i][:], [("qraw", i)], [("qb16", i)])
            if B.cfg.get("qk_sub", 9) < 1:
                continue
            B.mm([(PS[2 + i][:], Cs["jt"][:], qb16[i][:], True, True)], [("c", "jt"), ("qb16", i)], [psk(2 + i)])
            B.tt("dve", rt1[i][:], PS[2 + i][:], sinT[:], ALU.mult, [psk(2 + i), ("sinT",)], [("rt1", i)])
            if B.cfg.get("qk_sub", 9) < 2:
                continue
            B.tt("dve", rt2[i][:], qraw[i][:], cosT[:], ALU.mult, [("qraw", i), ("cosT",)], [("rt2", i)])
            if c < 8:
                B.tt("dve", qT[:, c, :], rt1[i][:], rt2[i][:], ALU.add, [("rt1", i), ("rt2", i)], [("qT", c)])
            else:
                B.tt("dve", kT[:, c - 8, 128:128 + TT], rt1[i][:], rt2[i][:], ALU.add, [("rt1", i), ("rt2", i)], [("kT", c - 8, "cur")])
        if B.cfg.get("att_stage", 9) < 2:
            return
        B.dma("sp", wv[:].rearrange("p k n -> p (k n)"), wv_s, [("wscr",)], [("wv",)], ("wv",))
        for sub in range(4):
            pb = 4 + sub % 2
            B.mm([(PS[pb][:, 0:256], hT[:, k, sub * 128:(sub + 1) * 128], wv[:, k, :], k == 0, k == 7) for k in range(8)], [("wv",), ("hT",)], [psk(pb)])
            B.tt("dve", Vb[:, 1 + sub, :], PS[pb][:, 0:256], bvb[:], ALU.add, [psk(pb), ("bvb",)], [("Vb", 1 + sub)])
        if B.cfg.get("att_stage", 9) < 3:
            return
        for qb in range(4):
            msk = maskF if (t == 1 and qb == 0) else Cs["band"]
            mkey = ("maskF",) if (t == 1 and qb == 0) else ("c", "band")
            for c in range(8):
                a = c // 4
                u = cnt[0] % 2
                cnt[0] += 1
                sbk = u
                ptb = 2 + u
                ob = 4 + (c // 4)
                items = []
                for j in range(2):
                    r0 = 64 * j
                    o = PS[sbk][:, j * 256:(j + 1) * 256]
                    items.append((o, qT[r0:r0 + 64, c, qb * 128:(qb + 1) * 128], kT[r0:r0 + 64, a, qb * 128:qb * 128 + 256], True, False))
                    items.append((o, identb[:], msk[:], False, True))
                B.mm(items, [("qT", c), ("kT", a), ("c", "identb"), mkey], [psk(sbk)])
                S = sm[u]
                mx, nm, tsk, es, rs, den, rden = [S[:, r, :] for r in range(7)]
                B.P.add("dve", (lambda e, o_=mx, i_=PS[sbk][:].rearrange("p (j n) -> p j n", j=2): e.tensor_reduce(out=o_, in_=i_, axis=AX.X, op=ALU.max)),
                        [psk(sbk)], [("sm", u, 0)])
                B.stt("dve", nm, mx, -0.125, nsinkp[:, c, :], ALU.mult, ALU.min, [("sm", u, 0), ("nsinkp",)], [("sm", u, 1)])
                for j in range(2):
                    B.act(Pexp[u][:, j, :], PS[sbk][:, j * 256:(j + 1) * 256], AF.Exp, [psk(sbk), ("sm", u, 1)], [("Pexp", u, j), ("sm", u, 4, j)],
                          scale=0.125, bias=S[:, 1, j:j + 1], accum=S[:, 4, j:j + 1])
                B.tt("dve", tsk, nm, sinkp[:, c, :], ALU.add, [("sm", u, 1), ("sinkp",)], [("sm", u, 2)])
                B.act(es, tsk, AF.Exp, [("sm", u, 2)], [("sm", u, 3)])
                B.tt("dve", den, rs, es, ALU.add, [("sm", u, 4), ("sm", u, 3)], [("sm", u, 5)])
                B.P.add("dve", (lambda e, o_=rden, i_=den: e.reciprocal(out=o_, in_=i_)), [("sm", u, 5)], [("sm", u, 6)])
                for j in range(2):
                    B.act(dgm[u][:, j, :], identb[:], AF.Copy, [("c", "identb"), ("sm", u, 6)], [("dgm", u, j)], scale=S[:, 6, j:j + 1])
                items = []
                for j in range(2):
                    for kb in range(2):
                        items.append((PS[ptb][:, (2 * j + kb) * 128:(2 * j + kb + 1) * 128], Pexp[u][:, j, kb * 128:(kb + 1) * 128], dgm[u][:, j, :], True, True))
                B.mm(items, [("Pexp", u), ("dgm", u)], [psk(ptb)])
                B.cp("act", PTs[u][:].rearrange("p a q -> p (a q)"), PS[ptb][:], [psk(ptb)], [("PTs", u)])
                items = []
                for j in range(2):
                    kvh = 2 * a + j
                    for kb in range(2):
                        items.append((PS[ob][64 * j:64 * j + 64, (c % 4) * 128:(c % 4 + 1) * 128], Vb[:, qb + kb, kvh * 64:(kvh + 1) * 64],
                                      PTs[u][:, 2 * j + kb, :], kb == 0, kb == 1))
                B.mm(items, [("Vb", qb), ("Vb", qb + 1), ("PTs", u)], [("ps", ob, c % 4)])
                if c % 4 == 3:
                    B.cp("act", oT[:, 4 * a:4 * a + 4, qb * 128:(qb + 1) * 128], PS[ob][:].rearrange("p (c q) -> p c q", c=4), [psk(ob)], [("oT", a, qb)])
        if B.cfg.get("att_stage", 9) < 4:
            return
        proj_tokmajor(oT, ("oT",), wo_s, 4, bo16)
        B.cp("dve", kT[:, :, 0:128], kT[:, :, TT:TT + 128], [("kT",)], [("kT",)])
        B.cp("dve", Vb[:, 0, :], Vb[:, 4, :], [("Vb", 4)], [("Vb", 0)])

    return attn


S5_MS = list(range(-7, 9)) + [16 << k for k in range(9)]
S5_IDX = {m: i for i, m in enumerate(S5_MS)}
NPW = len(S5_MS)
BL0 = 3


def emit_s5(B, W, Cs, PS, xl_d, x1_d, wglu_s, cflag):
    P = B.P
    st = ExitStack()
    sb = lambda n, s, d: B.sb(st, n, s, d)
    SU = ("SETUP", 1)
    identb, identf = Cs["identb"], Cs["identf"]
    fpi = float(np.pi)

    Sp = sb("Sp", [128, 64, 128], BF16)
    Bz = sb("Bz", [128, 64, 128], BF16)
    Dm = sb("Dm", [128, 64, 128], BF16)
    ARHS = sb("ARHS", [128, 10, 64], F32)
    S2HS = sb("S2HS", [128, 10, 64], F32)
    RSTD = sb("RSTD", [128, 8, 8], F32)
    SSQ = sb("SSQ", [128, 8, 8], F32)
    wglu = sb("wglu", [128, 8, 1024], BF16)
    bglu16 = sb("bglu16", [1, D], BF16)
    dgcol = sb("dgcol", [128, 64], F32)
    ys_d = B.dscr("ys_s", [5 * 1024, D], F32)

    B.dma("sp", wglu[:], wglu_s.rearrange("k p n -> p k n"), [("wscr",)], [("wglu",)], SU)

    stA = ExitStack()
    sa_ = lambda n, s, d: B.sb(stA, n, s, d)
    natL = sa_("natL", [64, 2, 128], F32)
    bglu32 = sa_("bglu32", [1, D], F32)
    B.dma("sp", bglu32[:], W["s5_b_glu"], [], [("bglu32",)], SU)
    B.cp("act", bglu16[:], bglu32[:], [("bglu32",)], [("bglu",)])
    LRr = sa_("LRr", [128, 64], F32)
    LIr = sa_("LIr", [128, 64], F32)
    dtb = sa_("dtb", [128, 64], F32)
    LR = sa_("LR", [128, 64], F32)
    TH = sa_("TH", [128, 64], F32)
    ANG = sa_("ANG", [128, NPW, 64], F32)
    MAG = sa_("MAG", [128, NPW, 64], F32)
    KK = sa_("KK", [128, NPW, 64], F32)
    RR = sa_("RR", [128, NPW, 64], F32)
    PWR = sa_("PWR", [128, NPW, 64], F32)
    PWI = sa_("PWI", [128, NPW, 64], F32)
    zr = sa_("zr", [128, 64], F32)
    zi = sa_("zi", [128, 64], F32)
    zt = [sa_("zt%d" % i, [128, 64], F32) for i in range(4)]
    dcolS = sa_("dcolS", [128, 64], F32)
    gcolS = sa_("gcolS", [128, 64], F32)

    for w, key in enumerate(("s5_lambda_re", "s5_lambda_im")):
        for h in range(2):
            B.dma("sp", natL[:, w, h * 64:(h + 1) * 64], W[key][0], [], [("natL", w, h)], SU)
    B.dma("sp", dtb[:], W["s5_log_dt"][0].partition_broadcast(128), [], [("dtb",)], SU)
    for s in range(8):
        B.dma("sp", dcolS[16 * s:16 * s + 16, :], W["s5_d"][0].rearrange("(g c) -> c g", c=16), [], [("dcolS", s)], SU, slow=True)
        B.dma("sp", gcolS[16 * s:16 * s + 16, :], W["norm_mix"][0].rearrange("(g c) -> c g", c=16), [], [("gcolS", s)], SU, slow=True)
    B.tt("dve", dgcol[:], dcolS[:], gcolS[:], ALU.mult, [("dcolS",), ("gcolS",)], [("dgcol",)])
    B.tr([(PS[0][:, w * 64:(w + 1) * 64], natL[:, w, :], identf[0:64, 0:64]) for w in range(2)], [("natL",), ("c", "identf")], [psk(0)])
    B.cp("dve", LRr[:], PS[0][:, 0:64], [psk(0)], [("LRr",)])
    B.cp("dve", LIr[:], PS[0][:, 64:128], [psk(0)], [("LIr",)])
    B.act(dtb[:], dtb[:], AF.Exp, [("dtb",)], [("dtb",)])
    B.tt("dve", LR[:], LRr[:], dtb[:], ALU.mult, [("LRr",), ("dtb",)], [("LR",)])
    B.tt("dve", TH[:], LIr[:], dtb[:], ALU.mult, [("LIr",), ("dtb",)], [("TH",)])
    for i, m in enumerate(S5_MS):
        B.ts("dve", ANG[:, i, :], TH[:], float(m), None, ALU.mult, None, [("TH",)], [("ANG", i)])
        B.ts("pool", MAG[:, i, :], LR[:], float(m), None, ALU.mult, None, [("LR",)], [("MAG", i)])
    B.act(MAG[:], MAG[:], AF.Exp, [("MAG",)], [("MAG",)])
    B.ts("dve", KK[:], ANG[:], float(1.0 / TWO_PI), MAGIC, ALU.mult, ALU.add, [("ANG",)], [("KK",)])
    B.ts("dve", KK[:], KK[:], -MAGIC, None, ALU.add, None, [("KK",)], [("KK",)])
    B.stt("dve", RR[:], KK[:], -CW1, ANG[:], ALU.mult, ALU.add, [("KK",), ("ANG",)], [("RR",)])
    B.stt("dve", RR[:], KK[:], -CW2, RR[:], ALU.mult, ALU.add, [("KK",), ("RR",)], [("RR",)])
    B.stt("dve", RR[:], KK[:], -CW3, RR[:], ALU.mult, ALU.add, [("KK",), ("RR",)], [("RR",)])
    B.ts("dve", RR[:], RR[:], -fpi, fpi, ALU.max, ALU.min, [("RR",)], [("RR",)])
    B.act(PWI[:], RR[:], AF.Sin, [("RR",)], [("PWI",)])
    B.ts("dve", RR[:], RR[:], fpi / 2, None, ALU.add, None, [("RR",)], [("RR",)])
    B.ts("dve", KK[:], RR[:], fpi, -float(TWO_PI), ALU.is_gt, ALU.mult, [("RR",)], [("KK",)])
    B.tt("dve", RR[:], RR[:], KK[:], ALU.add, [("RR",), ("KK",)], [("RR",)])
    B.ts("dve", RR[:], RR[:], -fpi, fpi, ALU.max, ALU.min, [("RR",)], [("RR",)])
    B.act(PWR[:], RR[:], AF.Sin, [("RR",)], [("PWR",)])
    B.tt("dve", PWR[:], PWR[:], MAG[:], ALU.mult, [("PWR",), ("MAG",)], [("PWR",)])
    B.tt("dve", PWI[:], PWI[:], MAG[:], ALU.mult, [("PWI",), ("MAG",)], [("PWI",)])
    i1 = S5_IDX[1]
    B.tt("dve", zt[0][:], LRr[:], LRr[:], ALU.mult, [("LRr",)], [("zt", 0)])
    B.tt("dve", zt[1][:], LIr[:], LIr[:], ALU.mult, [("LIr",)], [("zt", 1)])
    B.tt("dve", zt[0][:], zt[0][:], zt[1][:], ALU.add, [("zt", 0), ("zt", 1)], [("zt", 0)])
    B.P.add("dve", (lambda e: e.reciprocal(out=zt[0][:], in_=zt[0][:])), [("zt", 0)], [("zt", 0)])
    B.ts("dve", zt[1][:], PWR[:, i1, :], -1.0, None, ALU.add, None, [("PWR",)], [("zt", 1)])
    B.tt("dve", zt[2][:], zt[1][:], LRr[:], ALU.mult, [("zt", 1), ("LRr",)], [("zt", 2)])
    B.tt("dve", zt[3][:], PWI[:, i1, :], LIr[:], ALU.mult, [("PWI",), ("LIr",)], [("zt", 3)])
    B.tt("dve", zt[2][:], zt[2][:], zt[3][:], ALU.add, [("zt", 2), ("zt", 3)], [("zt", 2)])
    B.tt("dve", zr[:], zt[2][:], zt[0][:], ALU.mult, [("zt", 2), ("zt", 0)], [("zr",)])
    B.tt("dve", zt[2][:], PWI[:, i1, :], LRr[:], ALU.mult, [("PWI",), ("LRr",)], [("zt", 2)])
    B.tt("dve", zt[3][:], zt[1][:], LIr[:], ALU.mult, [("zt", 1), ("LIr",)], [("zt", 3)])
    B.tt("dve", zt[2][:], zt[2][:], zt[3][:], ALU.subtract, [("zt", 2), ("zt", 3)], [("zt", 2)])
    B.tt("dve", zi[:], zt[2][:], zt[0][:], ALU.mult, [("zt", 2), ("zt", 0)], [("zi",)])
    for k in range(10):
        ik = S5_IDX[8 << k]
        B.cp("dve", ARHS[:, k, :], PWR[:, ik, :], [("PWR",)], [("ARHS", k)])
        B.cp("dve", S2HS[0:64, k, :], PWI[0:64, ik, :], [("PWI",)], [("S2HS", k, 0)])
        B.ts("dve", S2HS[64:128, k, :], PWI[64:128, ik, :], -1.0, None, ALU.mult, None, [("PWI",)], [("S2HS", k, 1)])

    BR2q = [sa_("BR2_%d" % i, [128, 16, 16], F32) for i in range(4)]
    BI2q = [sa_("BI2_%d" % i, [128, 16, 16], F32) for i in range(4)]
    Gqq = [sa_("Gq_%d" % i, [128, 16, 16], F32) for i in range(4)]
    Cnq = [sa_("Cn_%d" % i, [128, 2, 2, 128], F32) for i in range(4)]
    ZB = sa_("ZB", [128, 16, 16], F32)
    ZBs = sa_("ZBs", [128, 16, 16], F32)
    tq = [sa_("tq%d" % i, [128, 16, 16], F32) for i in range(4)]
    Xt = sa_("Xt", [128, 16, 8, 16], F32)
    BzT = sa_("BzT", [128, 16, 8, 16], F32)
    YD = sa_("YD", [128, 16, 9, 16], F32)
    CA = sa_("CA", [128, 16, 16], F32)
    CB = sa_("CB", [128, 16, 16], F32)
    tmpS = sa_("tmpS", [128, 4, 128], F32)

    def bc(ap2):
        return ap2.unsqueeze(2).broadcast_to([128, 16, 16])

    for q in range(4):
        g0 = 16 * q
        BR2, BI2, Gq, Cn = BR2q[q], BI2q[q], Gqq[q], Cnq[q]
        bre = W["s5_b_re"][0, g0:g0 + 16].rearrange("g p c -> p g c")
        bim = W["s5_b_im"][0, g0:g0 + 16].rearrange("g p c -> p g c")
        B.dma("sp", BR2[0:64], bre, [], [("BR2", q, 0)], SU)
        B.dma("sp", BR2[64:128], bim, [], [("BR2", q, 1)], SU)
        B.dma("sp", BI2[0:64], bim, [], [("BI2", q, 0)], SU)
        B.dma("sp", BI2[64:128], bre, [], [("BI2", q, 1)], SU)
        B.dma("sp", Gq[:].rearrange("p g c -> p (g c)"), W["norm_mix"][0, g0 * 16:(g0 + 16) * 16].partition_broadcast(128), [], [("Gq", q)], SU)
        B.ts("dve", BI2[0:64], BI2[0:64], -1.0, None, ALU.mult, None, [("BI2", q, 0)], [("BI2", q, 0)])
        zrb, zib = bc(zr[:, g0:g0 + 16]), bc(zi[:, g0:g0 + 16])
        B.tt("dve", tq[0][:], BR2[:], zrb, ALU.mult, [("BR2", q), ("zr",)], [("tq", 0)])
        B.tt("dve", tq[1][:], BI2[:], zib, ALU.mult, [("BI2", q), ("zi",)], [("tq", 1)])
        B.tt("dve", tq[0][:], tq[0][:], tq[1][:], ALU.add, [("tq", 0), ("tq", 1)], [("tq", 0)])
        B.tt("dve", ZB[:], tq[0][:], Gq[:], ALU.mult, [("tq", 0), ("Gq", q)], [("ZB",)])
        B.tt("dve", tq[2][:], BI2[:], zrb, ALU.mult, [("BI2", q), ("zr",)], [("tq", 2)])
        B.tt("dve", tq[3][:], BR2[:], zib, ALU.mult, [("BR2", q), ("zi",)], [("tq", 3)])
        B.tt("dve", tq[2][:], tq[2][:], tq[3][:], ALU.subtract, [("tq", 2), ("tq", 3)], [("tq", 2)])
        B.tt("dve", ZBs[:], tq[2][:], Gq[:], ALU.mult, [("tq", 2), ("Gq", q)], [("ZBs",)])
        for s in range(8):
            for (dst, dkey, m) in ((Xt, "Xt", -s), (BzT, "BzT", 7 - s)):
                im = S5_IDX[m]
                B.tt("dve", tq[0][:], ZB[:], bc(PWR[:, im, g0:g0 + 16]), ALU.mult, [("ZB",), ("PWR",)], [("tq", 0)])
                B.tt("dve", tq[1][:], ZBs[:], bc(PWI[:, im, g0:g0 + 16]), ALU.mult, [("ZBs",), ("PWI",)], [("tq", 1)])
                B.tt("dve", dst[:, :, s, :], tq[0][:], tq[1][:], ALU.add, [("tq", 0), ("tq", 1)], [(dkey, s)])
        cre = W["s5_c_re"][0].rearrange("g c p -> (g c) p")
        cim = W["s5_c_im"][0].rearrange("g c p -> (g c) p")
        for t in range(2):
            r0 = (g0 + 8 * t) * 16
            B.dma("sp", Cn[:, t, 0, 0:64], cre[r0:r0 + 128, :], [], [("Cn", q, t, 0, 0)], SU)
            B.dma("sp", Cn[:, t, 0, 64:128], cim[r0:r0 + 128, :], [], [("Cn", q, t, 0, 1)], SU)
            B.dma("sp", Cn[:, t, 1, 0:64], cim[r0:r0 + 128, :], [], [("Cn", q, t, 1, 0)], SU)
            B.dma("sp", Cn[:, t, 1, 64:128], cre[r0:r0 + 128, :], [], [("Cn", q, t, 1, 1)], SU)
        B.tr([(PS[1][:, (2 * v + t) * 128:(2 * v + t + 1) * 128], Cn[:, t, v, :], identf[:]) for v in range(2) for t in range(2)],
             [("Cn", q), ("c", "identf")], [psk(1)])
        caf = CA[:].rearrange("p g c -> p (g c)")
        cbf = CB[:].rearrange("p g c -> p (g c)")
        B.cp("dve", caf[0:64, :], PS[1][0:64, 0:256], [psk(1)], [("CA", 0)])
        B.ts("dve", caf[64:128, :], PS[1][64:128, 0:256], -1.0, None, ALU.mult, None, [psk(1)], [("CA", 1)])
        B.ts("dve", cbf, PS[1][:, 256:512], -1.0, None, ALU.mult, None, [psk(1)], [("CB",)])
        for k in range(9):
            ik = S5_IDX[k]
            B.tt("dve", tq[2][:], CA[:], bc(PWR[:, ik, g0:g0 + 16]), ALU.mult, [("CA",), ("PWR",)], [("tq", 2)])
            B.tt("dve", tq[3][:], CB[:], bc(PWI[:, ik, g0:g0 + 16]), ALU.mult, [("CB",), ("PWI",)], [("tq", 3)])
            B.tt("dve", YD[:, :, k, :], tq[2][:], tq[3][:], ALU.add, [("tq", 2), ("tq", 3)], [("YD", k)])
        for g4 in range(4):
            gl0 = 4 * g4
            B.mm([(PS[2][:, j * 128:(j + 1) * 128], Xt[:, gl0 + j].rearrange("p s c -> p (s c)"),
                   YD[:, gl0 + j, 0:8, :].rearrange("p k c -> p (k c)"), True, True) for j in range(4)], [("Xt",), ("YD",)], [psk(2)])
            B.tt("dve", tmpS[:], PS[2][:].rearrange("p (j n) -> p j n", j=4), Cs["masks"][:].unsqueeze(1).broadcast_to([128, 4, 128]), ALU.mult,
                 [psk(2), ("c", "masks")], [("tmpS",)])
            for j in range(4):
                g = g0 + gl0 + j
                B.stt("dve", Sp[:, g, :], identf[:], dgcol[:, g:g + 1], tmpS[:, j, :], ALU.mult, ALU.add, [("c", "identf"), ("dgcol",), ("tmpS",)], [("Sp", g)])
            B.tr([(PS[3][:, j * 128:(j + 1) * 128], BzT[:, gl0 + j].rearrange("p s c -> p (s c)"), identf[:]) for j in range(4)],
                 [("BzT",), ("c", "identf")], [psk(3)])
            B.cp("act", Bz[:, g0 + gl0:g0 + gl0 + 4, :].rearrange("p g n -> p (g n)"), PS[3][:], [psk(3)], [("Bz", g0 + gl0)])
        B.cp("act", Dm[:, g0:g0 + 16, :].rearrange("p g (k c) -> p g k c", k=8), YD[:, :, 1:9, :], [("YD",)], [("Dm", q)])
    P.barrier()
    stA.close()

    stM = ExitStack()
    sm_ = lambda n, s, d: B.sb(stM, n, s, d)
    xo = sm_("xo", [128, 8, 8, 128], F32)
    uo = sm_("uo", [128, 8, 8, 128], BF16)
    Ug = [sm_("Ug%d" % i, [128, 1024], BF16) for i in range(2)]
    Hb = [[sm_("H%d%d" % (i, j), [128, 1024], BF16) for j in range(2)] for i in range(2)]
    Lt = [sm_("Lt%d" % i, [128, 10, 128], BF16) for i in range(2)]
    La = sm_("La", [128, 10, 128], F32)
    y8o = sm_("y8o", [128, 5, 8, 128], F32)
    xi = [sm_("xi%d" % i, [128, D], F32) for i in range(2)]
    ysi = [sm_("ysi%d" % i, [128, D], F32) for i in range(2)]
    t1e = [sm_("t1e%d" % i, [128, D], F32) for i in range(2)]
    gq = [sm_("gq%d" % i, [128, D], BF16) for i in range(2)]
    gTi = [sm_("gTi%d" % i, [128, 8, 128], BF16) for i in range(2)]
    sg = [sm_("sg%d" % i, [128, 512], F32) for i in range(2)]
    xlv = xl_d.rearrange("(b j i) f -> b i j f", b=8, j=128, i=8)
    NB = B.cfg.get("s5_blocks", 8)

    n = 0
    for b in range(8):
        for i in range(8):
            s = n % 2
            n += 1
            B.dma("sp", xi[s][:], xlv[b, i], [], [("xi", s)], ("xi", s))
            B.act(gq[0][:], xi[s][:], AF.Square, [("xi", s)], [("gq", 0), ("SSQ", b, i)], accum=SSQ[:, b, i:i + 1])
    B.ts("dve", SSQ[:], SSQ[:], 1.0 / D, EPS, ALU.mult, ALU.add, [("SSQ",)], [("SSQ",)])
    B.tt("pool", RSTD[:].rearrange("p b i -> p (b i)"), SSQ[:].rearrange("p b i -> p (b i)"), Cs["mhalf"][:, 0:1].broadcast_to([128, 64]), ALU.pow,
         [("SSQ",), ("c", "mhalf")], [("RSTD",)])

    xov = xl_d.rearrange("(b j i) f -> j b i f", b=8, j=128, i=8)
    ysv = ys_d.rearrange("(b j i) f -> j b i f", b=5, j=128, i=8)
    ev = [0]

    def evac_eng():
        ev[0] += 1
        return ("act", "dve")[ev[0] % 2]

    for o in range(8):
        P.dma_batch([B.dma("sp", xo[:, b], xov[:, b, :, o * 128:(o + 1) * 128], [], [("xo", b)], ("xo",)) for b in range(8)])
        for b in range(8):
            B.tt("dve", uo[:, b].rearrange("p gl (s c) -> p s gl c", s=8, c=16),
                 xo[:, b].rearrange("p s (gl c) -> p s gl c", gl=8, c=16),
                 RSTD[:, b, :].unsqueeze(2).unsqueeze(3).broadcast_to([128, 8, 8, 16]), ALU.mult,
                 [("xo", b), ("RSTD",)], [("uo", b)])
        for gp in range(4):
            grp = [(2 * gp + z, 8 * o + 2 * gp + z, z) for z in range(2)]
            cur = {}
            for (gl, g, z) in grp:
                ptv = PS[6 + z][:].bitcast(BF16)
                B.tr([(ptv[:, b * 128:(b + 1) * 128], uo[:, b, gl, :], identb[:]) for b in range(8)], [("uo",), ("c", "identb")], [psk(6 + z)])
                B.cp(evac_eng(), Ug[z][:], ptv, [psk(6 + z)], [("Ug", z)])
                for h in range(2):
                    B.mm([(PS[2 * z + h][:], Bz[:, g, :], Ug[z][:, h * 512:(h + 1) * 512], True, True)], [("Bz",), ("Ug", z)], [psk(2 * z + h)])
                    B.cp(evac_eng(), Hb[z][0][:, h * 512:(h + 1) * 512], PS[2 * z + h][:], [psk(2 * z + h)], [("H", z, 0, h)])
                cur[z] = 0
                B.tt("dve", La[:], Cs["pswap"][:].unsqueeze(1).broadcast_to([128, 10, 128]), S2HS[:, :, g:g + 1].broadcast_to([128, 10, 128]), ALU.mult,
                     [("c", "pswap"), ("S2HS",)], [("La",)])
                for k in range(10):
                    B.stt("dve", Lt[z][:, k, :], identf[:], ARHS[:, k, g:g + 1], La[:, k, :], ALU.mult, ALU.add,
                          [("c", "identf"), ("ARHS",), ("La",)], [("Lt", z, k)])
            for k in range(10):
                sft = 1 << k
                for (gl, g, z) in grp:
                    src, dst = Hb[z][cur[z]], Hb[z][1 - cur[z]]
                    for h in range(2):
                        c0 = h * 512
                        lo = max(sft, c0)
                        items = []
                        has_shift = lo < c0 + 512
                        items.append((PS[2 * z + h][:], identb[:], src[:, c0:c0 + 512], True, not has_shift))
                        if has_shift:
                            items.append((PS[2 * z + h][:, lo - c0:512], Lt[z][:, k, :], src[:, lo - sft:c0 + 512 - sft], False, True))
                        B.mm(items, [("c", "identb"), ("Lt", z), ("H", z, cur[z])], [psk(2 * z + h)])
                        B.cp(evac_eng(), dst[:, c0:c0 + 512], PS[2 * z + h][:], [psk(2 * z + h)], [("H", z, 1 - cur[z], h)])
                    cur[z] = 1 - cur[z]
            for (gl, g, z) in grp:
                Hf = Hb[z][cur[z]]
                items = []
                for b in range(BL0, 8):
                    o_ = PS[4][:, (b - BL0) * 128:(b - BL0 + 1) * 128] if b < 7 else PS[5][:, 0:128]
                    items.append((o_, Ug[z][:, b * 128:(b + 1) * 128], Sp[:, g, :], True, False))
                    items.append((o_, Hf[:, b * 128 - 1:b * 128 + 127], Dm[:, g, :], False, True))
                B.mm(items, [("Ug", z), ("Sp",), ("Dm",), ("H", z, cur[z])], [psk(4), psk(5)])
                B.cp(evac_eng(), y8o[:, 0:4, :, gl * 16:(gl + 1) * 16], PS[4][:].rearrange("p (b i c) -> p b i c", b=4, i=8), [psk(4)], [("y8o", gl, 0)])
                B.cp(evac_eng(), y8o[:, 4, :, gl * 16:(gl + 1) * 16], PS[5][:, 0:128].rearrange("p (i c) -> p i c", i=8), [psk(5)], [("y8o", gl, 1)])
        P.dma_batch([B.dma("sp", ysv[:, b, :, o * 128:(o + 1) * 128], y8o[:, b], [("y8o",)], [("ys", o, b)], ("ysst",)) for b in range(5)])

    ysr = ys_d.rearrange("(b j i) f -> b i j f", b=5, j=128, i=8)
    n = 0
    for bq in range(5):
        b = BL0 + bq
        for i in range(8):
            s = n % 2
            s2 = n % 2
            n += 1
            B.dma("sp", xi[s][:], xlv[b, i], [], [("xi", s)], ("xi", s))
            B.dma("sp", ysi[s][:], ysr[bq, i], [("ys",)], [("ysi", s)], ("ysi", s))
            y_ = ysi[s]
            t1 = t1e[s2]
            B.act(t1[:], y_[:], AF.Square, [("ysi", s)], [("t1e", s2)])
            B.ts("dve", t1[:], t1[:], 0.044715, 1.0, ALU.mult, ALU.add, [("t1e", s2)], [("t1e", s2)])
            B.tt("dve", t1[:], t1[:], y_[:], ALU.mult, [("t1e", s2), ("ysi", s)], [("t1e", s2)])
            B.act(t1[:], t1[:], AF.Sigmoid, [("t1e", s2)], [("t1e", s2)], scale=1.5957691216057308)
            B.tt("dve", gq[s2][:], y_[:], t1[:], ALU.mult, [("ysi", s), ("t1e", s2)], [("gq", s2)])
            ptv = PS[6 + s2][:].bitcast(BF16)
            B.tr([(ptv[:, k * 128:(k + 1) * 128], gq[s2][:, k * 128:(k + 1) * 128], identb[:]) for k in range(8)], [("gq", s2), ("c", "identb")], [psk(6 + s2)])
            B.cp("act", gTi[s2][:].rearrange("p k j -> p (k j)"), ptv, [psk(6 + s2)], [("gTi", s2)])
            for half in range(2):
                pb = 2 * s2 + half
                items = [(PS[pb][:], gTi[s2][:, k, :], wglu[:, k, half * 512:(half + 1) * 512], k == 0, False) for k in range(8)]
                items.append((PS[pb][:], Cs["onesb"][0:1, :], bglu16[0:1, half * 512:(half + 1) * 512], False, True))
                B.mm(items, [("gTi", s2), ("wglu",), ("c", "onesb"), ("bglu",)], [psk(pb)])
                B.act(sg[half][:], PS[pb][:], AF.Sigmoid, [psk(pb)], [("sg", half)])
                B.tt("dve", sg[half][:], sg[half][:], gq[s2][:, half * 512:(half + 1) * 512], ALU.mult, [("sg", half), ("gq", s2)], [("sg", half)])
                xh = xi[s][:, half * 512:(half + 1) * 512]
                B.tt("dve", xh, xh, sg[half][:], ALU.add, [("sg", half), ("xi", s, half)], [("xi", s, half)])
            if b == BL0:
                dst = x1_d[0:512].rearrange("(j i) f -> i j f", i=8)[i]
                B.dma("sp", dst, xi[s][64:128, :], [("xi", s)], [("x1s", b, i)], ("x1st", s))
            else:
                r0 = b * 1024 - 3584
                dst = x1_d[r0:r0 + 1024].rearrange("(j i) f -> i j f", i=8)[i]
                B.dma("sp", dst, xi[s][:], [("xi", s)], [("x1s", b, i)], ("x1st", s))
    P.barrier()
    stM.close()
    st.close()


_NC_CACHE = {}


def _get_nc(cfg_key, cfg):
    if cfg_key not in _NC_CACHE:
        _NC_CACHE[cfg_key] = build(cfg)
    return _NC_CACHE[cfg_key]


def make_in_maps(inputs, cfg):
    consts = host_consts()
    x = np.asarray(inputs["x"], np.float32)
    pos = np.asarray(inputs["positions"], np.int32)
    maps = []
    for c in range(8):
        b, half = c // 2, c % 2
        m = {}
        for k in WEIGHT_SPECS:
            m[k] = np.ascontiguousarray(np.asarray(inputs[k], np.float32))
        for k, v in consts.items():
            m[k] = v
        m["cflag"] = np.full((128, 1), float(half), np.float32)
        posl = np.zeros((9 * TT,), np.int32)
        if half == 1:
            posl[:] = pos[b, 3584:8192]
        else:
            posl[TT:] = pos[b, 0:4096]
        m["posl"] = posl
        if cfg.get("s5", True):
            xl = np.zeros((8192, D), np.float32)
            if half == 1:
                xl[:] = x[b]
            else:
                xl[4096:] = x[b, 0:4096]
            m["xl"] = xl
        maps.append(m)
    return maps


def kernel(**inputs):
    cfg = {}
    nc = _get_nc("full", cfg)
    maps = make_in_maps(inputs, cfg)
    res = run_bass_kernel_spmd(nc, maps, core_ids=list(range(8)))
    out = np.zeros((4, 8192, D), np.float32)
    for c in range(8):
        b, half = c // 2, c % 2
        out[b, half * 4096:(half + 1) * 4096] = res.results[c]["y"]
    return out
```
